# Optimizing a Trainium2 kernel written in Bass

```python
import jax, jax.numpy as jnp
from jax import lax
import numpy as np

D_MODEL = 2048
BATCH = 4
SEQ = 2048
DEPTH = 1
DEC_BATCH = 128
DEC_SEQ = 4
PAST_LEN = 16384
PAGE_SIZE = 128

N_META = 16
D_A = 1024
CONV_A = 31
D_B = 2048
HEAD_DIM = 64
N_HEADS_B = D_B // HEAD_DIM
N_GROUPS = 8
D_STATE = 128
CONV_B = 4
CHUNK = 128
D_FF = 5632
CONV_F = 3
EPS = 1e-6
D_XBC = D_B + 2 * N_GROUPS * D_STATE
D_IN = 2 * D_A + D_B + D_XBC + N_HEADS_B + 2 * D_MODEL
SPLITS = (D_A, 2 * D_A, 2 * D_A + D_B, 2 * D_A + D_B + D_XBC, 2 * D_A + D_B + D_XBC + N_HEADS_B)

kernel_name = 'hybrid_conformer_ssd_convffn_step'

F32 = jnp.float32


def rmsnorm(x, g):
    xf = x.astype(F32)
    r = lax.rsqrt(jnp.mean(xf * xf, axis=-1, keepdims=True) + EPS)
    return (xf * r * g.astype(F32)).astype(x.dtype)


def layernorm(x, g, b):
    xf = x.astype(F32)
    mu = jnp.mean(xf, axis=-1, keepdims=True)
    xc = xf - mu
    r = lax.rsqrt(jnp.mean(xc * xc, axis=-1, keepdims=True) + EPS)
    return (xc * r * g.astype(F32) + b.astype(F32)).astype(x.dtype)


def causal_dwconv(u, past, w, b):
    W = w.shape[0]
    cat = jnp.concatenate([past.astype(u.dtype), u], axis=1)
    out = lax.conv_general_dilated(cat, w.astype(u.dtype)[:, None, :], window_strides=(1,), padding='VALID',
                                   dimension_numbers=('NWC', 'WIO', 'NWC'), feature_group_count=u.shape[-1])
    return out + b.astype(u.dtype), cat[:, cat.shape[1] - (W - 1):]


def ssd(xh, dt, a, Bm, Cm, h0, chunk):
    b, L, H, P = xh.shape
    G, N = Bm.shape[2], Bm.shape[3]
    R = H // G
    nc = L // chunk
    x = xh.astype(F32).reshape(b, nc, chunk, G, R, P)
    dtc = dt.astype(F32).reshape(b, nc, chunk, G, R)
    Bc = Bm.astype(F32).reshape(b, nc, chunk, G, N)
    Cc = Cm.astype(F32).reshape(b, nc, chunk, G, N)
    cum = jnp.cumsum(dtc * a.astype(F32).reshape(G, R), axis=2)
    xdt = x * dtc[..., None]
    seg = cum[:, :, :, None] - cum[:, :, None, :]
    causal = jnp.tril(jnp.ones((chunk, chunk), bool))[None, None, :, :, None, None]
    decay = jnp.exp(jnp.where(causal, seg, -jnp.inf))
    cb = jnp.einsum('bcqgn,bckgn->bcqkg', Cc, Bc)
    y_intra = jnp.einsum('bcqkgr,bckgrp->bcqgrp', cb[..., None] * decay, xdt)
    to_end = jnp.exp(cum[:, :, -1:] - cum)
    states = jnp.einsum('bckgn,bckgrp->bcgrpn', Bc, xdt * to_end[..., None])
    chunk_decay = jnp.exp(cum[:, :, -1])

    def step(h, inp):
        s, d = inp
        return d[..., None, None] * h + s, h

    hT, h_prev = lax.scan(step, h0.astype(F32).reshape(b, G, R, P, N),
                          (jnp.moveaxis(states, 1, 0), jnp.moveaxis(chunk_decay, 1, 0)))
    h_prev = jnp.moveaxis(h_prev, 0, 1)
    y_inter = jnp.einsum('bcqgn,bcgrpn->bcqgrp', Cc, h_prev) * jnp.exp(cum)[..., None]
    return (y_intra + y_inter).reshape(b, L, H, P), hT.reshape(b, H, P, N)


def hybrid_layer(x, conf_buf, xbc_buf, ssm_h, ffn_buf, lead_pad, chunk,
                 g_pre1, g_post1, w_in, b_gate, w_dw_a, b_dw_a, g_ln_a, b_ln_a, w_a_out, b_a_out,
                 w_dw_b, b_dw_b, dt_bias, a_log, d_skip, g_norm_b, w_b_out, w_o,
                 g_pre2, g_post2, w_up, w_dw_f, b_dw_f, w_down):
    b, L, _ = x.shape
    h = rmsnorm(x, g_pre1)
    proj = h @ w_in
    a_val, a_gate, z, xbc, dt_raw, gate_raw = jnp.split(proj, SPLITS, axis=-1)
    u = a_val * jax.nn.sigmoid(a_gate)
    u, conf_new = causal_dwconv(u, conf_buf, w_dw_a, b_dw_a)
    u = jax.nn.silu(layernorm(u, g_ln_a, b_ln_a))
    y_a = u @ w_a_out + b_a_out
    xbc, xbc_new = causal_dwconv(xbc, xbc_buf, w_dw_b, b_dw_b)
    xbc = jax.nn.silu(xbc)
    xs, Bm, Cm = jnp.split(xbc, (D_B, D_B + N_GROUPS * D_STATE), axis=-1)
    xh = xs.reshape(b, L, N_HEADS_B, HEAD_DIM)
    Bm = Bm.reshape(b, L, N_GROUPS, D_STATE)
    Cm = Cm.reshape(b, L, N_GROUPS, D_STATE)
    dt = jax.nn.softplus(dt_raw.astype(F32) + dt_bias.astype(F32))
    a = -jnp.exp(a_log.astype(F32))
    pad = ((0, 0), (lead_pad, 0), (0, 0), (0, 0))
    y, ssm_new = ssd(jnp.pad(xh, pad), jnp.pad(dt, pad[:3]), a, jnp.pad(Bm, pad), jnp.pad(Cm, pad), ssm_h, chunk)
    y = y[:, lead_pad:] + d_skip.astype(F32)[:, None] * xh.astype(F32)
    y = y.reshape(b, L, D_B).astype(x.dtype)
    y_b = rmsnorm(y * jax.nn.silu(z), g_norm_b) @ w_b_out
    g_a, g_b = jnp.split(jax.nn.sigmoid(gate_raw + b_gate), 2, axis=-1)
    m = (g_a * y_a + g_b * y_b) @ w_o
    x = x + rmsnorm(m, g_post1)
    h = rmsnorm(x, g_pre2)
    u = h @ w_up
    u, ffn_new = causal_dwconv(u, ffn_buf, w_dw_f, b_dw_f)
    gt, val = jnp.split(u, 2, axis=-1)
    f = (jax.nn.gelu(gt) * val) @ w_down
    x = x + rmsnorm(f, g_post2)
    return x, conf_new, xbc_new, ssm_new, ffn_new


def setup_inputs(seed: int = 0) -> dict:
    key = jax.random.key(seed)
    ks = iter(jax.random.split(key, 40))
    nrm = lambda shape, s: jax.random.normal(next(ks), shape, F32) * s
    gain = lambda shape: 1.0 + nrm(shape, 0.02)
    dt0 = jnp.exp(jax.random.uniform(next(ks), (DEPTH, N_HEADS_B), F32, np.log(1e-3), np.log(1e-1)))
    return {
        'x_prompt': nrm((BATCH, SEQ, D_MODEL), 1.0),
        'x_sample': nrm((DEC_BATCH, DEC_SEQ, D_MODEL), 1.0),
        'state_conv_a': nrm((DEPTH, DEC_BATCH, CONV_A - 1, D_A), 1.0),
        'state_conv_b': nrm((DEPTH, DEC_BATCH, CONV_B - 1, D_XBC), 1.0),
        'state_ssm': nrm((DEPTH, DEC_BATCH, N_HEADS_B, HEAD_DIM, D_STATE), 0.1),
        'state_conv_ffn': nrm((DEPTH, DEC_BATCH, CONV_F - 1, 2 * D_FF), 1.0),
        'meta_tokens': nrm((N_META, D_MODEL), 1.0),
        'g_pre1': gain((DEPTH, D_MODEL)),
        'g_post1': gain((DEPTH, D_MODEL)),
        'w_in': nrm((DEPTH, D_MODEL, D_IN), D_MODEL ** -0.5),
        'b_gate': nrm((DEPTH, 2 * D_MODEL), 0.1),
        'w_dw_a': nrm((DEPTH, CONV_A, D_A), CONV_A ** -0.5),
        'b_dw_a': nrm((DEPTH, D_A), 0.02),
        'g_ln_a': gain((DEPTH, D_A)),
        'b_ln_a': nrm((DEPTH, D_A), 0.02),
        'w_a_out': nrm((DEPTH, D_A, D_MODEL), D_A ** -0.5),
        'b_a_out': nrm((DEPTH, D_MODEL), 0.02),
        'w_dw_b': nrm((DEPTH, CONV_B, D_XBC), CONV_B ** -0.5),
        'b_dw_b': nrm((DEPTH, D_XBC), 0.02),
        'dt_bias': dt0 + jnp.log(-jnp.expm1(-dt0)),
        'a_log': jnp.log(jax.random.uniform(next(ks), (DEPTH, N_HEADS_B), F32, 1.0, 16.0)),
        'd_skip': gain((DEPTH, N_HEADS_B)),
        'g_norm_b': gain((DEPTH, D_B)),
        'w_b_out': nrm((DEPTH, D_B, D_MODEL), D_B ** -0.5),
        'w_o': nrm((DEPTH, D_MODEL, D_MODEL), D_MODEL ** -0.5),
        'g_pre2': gain((DEPTH, D_MODEL)),
        'g_post2': gain((DEPTH, D_MODEL)),
        'w_up': nrm((DEPTH, D_MODEL, 2 * D_FF), D_MODEL ** -0.5),
        'w_dw_f': nrm((DEPTH, CONV_F, 2 * D_FF), CONV_F ** -0.5),
        'b_dw_f': nrm((DEPTH, 2 * D_FF), 0.02),
        'w_down': nrm((DEPTH, D_FF, D_MODEL), D_FF ** -0.5),
    }


def reference(x_prompt, x_sample, state_conv_a, state_conv_b, state_ssm, state_conv_ffn, meta_tokens,
              g_pre1, g_post1, w_in, b_gate, w_dw_a, b_dw_a, g_ln_a, b_ln_a, w_a_out, b_a_out,
              w_dw_b, b_dw_b, dt_bias, a_log, d_skip, g_norm_b, w_b_out, w_o,
              g_pre2, g_post2, w_up, w_dw_f, b_dw_f, w_down):
    bp = x_prompt.shape[0]
    xp = jnp.concatenate([jnp.broadcast_to(meta_tokens.astype(x_prompt.dtype)[None], (bp, N_META, D_MODEL)), x_prompt], axis=1)
    xs = x_sample
    pa, pb, ph, pf = [], [], [], []
    sa, sb, sh, sf = [], [], [], []
    for l in range(DEPTH):
        w = (g_pre1[l], g_post1[l], w_in[l], b_gate[l], w_dw_a[l], b_dw_a[l], g_ln_a[l], b_ln_a[l], w_a_out[l], b_a_out[l],
             w_dw_b[l], b_dw_b[l], dt_bias[l], a_log[l], d_skip[l], g_norm_b[l], w_b_out[l], w_o[l],
             g_pre2[l], g_post2[l], w_up[l], w_dw_f[l], b_dw_f[l], w_down[l])
        xp, c_a, c_b, c_h, c_f = hybrid_layer(
            xp, jnp.zeros((bp, CONV_A - 1, D_A), xp.dtype), jnp.zeros((bp, CONV_B - 1, D_XBC), xp.dtype),
            jnp.zeros((bp, N_HEADS_B, HEAD_DIM, D_STATE), F32), jnp.zeros((bp, CONV_F - 1, 2 * D_FF), xp.dtype),
            CHUNK - N_META, CHUNK, *w)
        pa.append(c_a); pb.append(c_b); ph.append(c_h); pf.append(c_f)
        xs, d_a, d_b, d_h, d_f = hybrid_layer(
            xs, state_conv_a[l], state_conv_b[l], state_ssm[l], state_conv_ffn[l], 0, xs.shape[1], *w)
        sa.append(d_a); sb.append(d_b); sh.append(d_h); sf.append(d_f)
    y_prompt = xp[:, N_META:]
    return (y_prompt, xs, jnp.stack(pa), jnp.stack(pb), jnp.stack(ph), jnp.stack(pf),
            jnp.stack(sa), jnp.stack(sb), jnp.stack(sh), jnp.stack(sf))
```

```python
import numpy as np
import concourse.bass as bass
import concourse.mybir as mybir
from concourse.bass_utils import run_bass_kernel_spmd

F32 = mybir.dt.float32
BF16 = mybir.dt.bfloat16
ALU = mybir.AluOpType
AF = mybir.ActivationFunctionType

D = 2048; KC = 16
DA = 1024; NJA = 8; CA = 31
DB = 2048; H = 32; HP = 64; NG = 8; NS = 128
DXBC = 4096; CB = 4
DFF = 5632; NF = 44; CF = 3
NMETA = 16
EPS = 1e-6
NSEQ = 16
NSTOK = 64
TSP = 256
TSMAX = 256
NPRE = 1024
NMAIN = 1024
C_PRE = 0
C_LEAD = NPRE
C_MAIN = NPRE + 128
C_SMP = C_MAIN + NMAIN
NCOL = C_SMP + NSTOK
WSLOT = 4096
NWSLOT = 4


class Buf:
    __slots__ = ("name", "w", "r", "dsem", "dval")

    def __init__(self, name):
        self.name = name
        self.w = None
        self.r = {}
        self.dsem = None
        self.dval = 0


class Sync:
    def __init__(self, nc, same_engine_waits=True):
        self.nc = nc
        self.eng = {"pe": nc.tensor, "act": nc.scalar, "dve": nc.vector, "pool": nc.gpsimd, "sp": nc.sync}
        self.sem = {k: nc.alloc_semaphore("s_" + k) for k in ("pe", "act", "dve", "pool")}
        self.tick = {k: 0 for k in self.sem}
        self.seen = {k: {} for k in self.eng}
        self.same = same_engine_waits
        self.nins = 0
        self.big = False

    def _waits(self, e, reads, writes, extra=()):
        need = {}
        deps = list(extra)
        for b in reads:
            if b.w is not None:
                deps.append(b.w)
        for b in writes:
            if b.w is not None:
                deps.append(b.w)
            deps.extend(d for d in b.r.values() if not (d[0] == "eng" and d[1] == e))
        for d in deps:
            if d[0] == "eng":
                key, sem, val = d[1], self.sem[d[1]], d[2]
                if key == e and (e == "pe" or not self.same or d[3]):
                    continue
            else:
                key, sem, val = d[3], d[1], d[2]
            if self.seen[e].get(key, 0) >= val:
                continue
            if key not in need or need[key][1] < val:
                need[key] = (sem, val)
        for key, (sem, val) in need.items():
            self.eng[e].wait_ge(sem, val)
            self.seen[e][key] = val

    def op(self, e, reads, writes, fn):
        self._waits(e, reads, writes)
        ins = fn(self.eng[e])
        self.nins += 1
        self.tick[e] += 1
        T = self.tick[e]
        ins.then_inc(self.sem[e], 1)
        for b in writes:
            b.w = ("eng", e, T, self.big)
            b.r = {}
        for b in reads:
            if b not in writes:
                b.r[e] = ("eng", e, T, False)
        return ins

    def dma(self, q, out, in_, reads, writes, sembuf=None, extra=()):
        self._waits(q, reads, writes, extra)
        sb = sembuf or (writes[0] if writes else reads[0])
        if sb.dsem is None:
            sb.dsem = self.nc.alloc_semaphore("d_" + sb.name)
        ins = self.eng[q].dma_start(out=out, in_=in_)
        sb.dval += 16
        ins.then_inc(sb.dsem, 16)
        dep = ("dma", sb.dsem, sb.dval, "d_" + sb.name)
        for b in writes:
            b.w = dep
            b.r = {}
        for b in reads:
            if b not in writes:
                b.r["d_" + sb.name] = dep
        return ins

    def finish(self, e, bufs):
        self._waits(e, [], bufs)


def _cst_layout():
    off = {}
    cur = 0

    def add(name, n):
        nonlocal cur
        off[name] = (cur, n)
        cur += n

    for nm in ("g_pre1", "g_post1", "g_pre2", "g_post2", "g_norm_b", "b_a_out", "dsk"):
        add(nm, 16)
    add("b_gate", 32)
    add("w_dw_a", NJA * CA)
    for nm in ("b_dw_a", "g_ln_a", "b_ln_a"):
        add(nm, NJA)
    add("w_dw_b", 32 * CB)
    add("b_dw_b", 32)
    add("w_dw_f", 88 * CF)
    add("b_dw_f", 88)
    add("dt_bias", 32)
    add("a_log", 32)
    add("valid", 18)
    add("ones", 128)
    add("ident", 128)
    add("tri_p", 128)
    add("tri_s", 64)
    add("blk_s", 64)
    add("mneg_p", 512)
    add("mneg_s", 256)
    add("sel", NSEQ)
    return off, cur


CST, NCST = _cst_layout()


def build_program():
    nc = bass.Bass("TRN2", target_bir_lowering=False)
    sy = Sync(nc)

    def din(name, shape, dt=F32):
        return nc.dram_tensor(name, list(shape), dt, kind="ExternalInput").ap()

    def dout(name, shape):
        return nc.dram_tensor(name, list(shape), F32, kind="ExternalOutput").ap()

    xT_d = din("xT", [128, KC, NCOL])
    cst_d = din("cst", [128, NCST])
    wA_d = din("wA", [NJA, 128, KC * 256])
    wX_d = din("wX", [16, 128, KC * 256])
    wDT_d = din("wDT", [128, KC * 32])
    wZ_d = din("wZ", [8, 128, KC * 256])
    w6a_d = din("w6a", [16, 128, 24 * 128])
    w6b_d = din("w6b", [16, 128, 32 * 128])
    wO_d = din("wO", [8, 128, KC * 256])
    wU_d = din("wU", [NF, 128, KC * 256])
    wD_d = din("wD", [32, 128, 22 * 128])
    sca_d = din("sca", [128, NJA, NSEQ, CA - 1])
    scb_d = din("scb", [128, 32, NSEQ, CB - 1])
    scf_d = din("scf", [128, 88, NSEQ, CF - 1])
    ssm_d = din("ssm", [NSEQ, 128, DB])

    yT_o = dout("yT", [128, KC, NMAIN + NSTOK])
    oca_o = dout("o_ca", [128, NJA, CA - 1])
    ocb_o = dout("o_cb", [128, 32, CB - 1])
    ocf_o = dout("o_cf", [128, 88, CF - 1])
    ossm_o = dout("o_ssm", [128, DB])
    osca_o = dout("o_sca", [128, NJA, NSEQ, CA - 1])
    oscb_o = dout("o_scb", [128, 32, NSEQ, CB - 1])
    oscf_o = dout("o_scf", [128, 88, NSEQ, CF - 1])
    osssm_o = dout("o_sssm", [NSEQ, 128, DB])

    def sb(name, shape, dt=F32):
        return nc.alloc_sbuf_tensor(name, list(shape), dt)

    TS = TSMAX
    cst = sb("cst_s", [128, NCST]); B_cst = Buf("cst")
    xT = sb("xTs", [128, KC, TS]); B_xT = Buf("xT")
    hT = sb("hT", [128, KC, TS], BF16); B_hT = Buf("hT")
    uA = sb("uA", [128, NJA, TS], BF16); B_uA = Buf("uA")
    R1 = sb("R1", [128, 12288], BF16); B_R1 = [Buf("R1_xs"), Buf("R1_B"), Buf("R1_C")]
    R2 = sb("R2", [128, 5632]); B_R2 = Buf("R2")
    zs = sb("zs", [128, KC, TS], BF16); B_zs = Buf("zs")
    gyT = sb("gyT", [128, KC, TS], BF16); B_gy = Buf("gyT")
    minT = zs; B_min = B_zs
    wsl = [sb(f"w{i}", [128, WSLOT], BF16) for i in range(NWSLOT)]
    B_w = [Buf(f"w{i}") for i in range(NWSLOT)]
    wdt = sb("wdt", [128, KC, 32], BF16); B_wdt = Buf("wdt")
    haloA = sb("haloA", [128, NJA, CA - 1]); B_hA = Buf("haloA")
    haloB = sb("haloB", [128, 32, CB - 1]); B_hB = Buf("haloB")
    haloF = sb("haloF", [128, 88, CF - 1]); B_hF = Buf("haloF")
    xsT = R1[:, 0:8192].bitcast(F32).rearrange("p (j t) -> p j t", j=KC)
    BT = R1[:, 8192:10240].rearrange("p (j t) -> p j t", j=NG)
    CT = R1[:, 10240:12288].rearrange("p (j t) -> p j t", j=NG)
    fT = R1[:, 0:NF * TS].rearrange("p (j t) -> p j t", j=NF)
    uexts = R1[:, 0:2 * NJA * NSEQ * 34].bitcast(F32).rearrange("p (j s t) -> p j s t", j=NJA, s=NSEQ)
    uext = R2[:, 0:NJA * (CA - 1 + TS)].rearrange("p (j t) -> p j t", j=NJA)
    cvA = R2[:, NJA * (CA - 1 + TS):NJA * (CA - 1 + TS) + NJA * TS].rearrange("p (j t) -> p j t", j=NJA)
    mT = R2[:, 0:KC * TS].rearrange("p (j t) -> p j t", j=KC)
    scb_s = R2[:, 0:1536].rearrange("p (j s t) -> p j s t", j=32, s=NSEQ); B_scb = B_R2
    ocb_s = R2[:, 1536:3072].rearrange("p (j s t) -> p j s t", j=32, s=NSEQ); B_ocb = B_R2
    h0b = R2[:, 0:1024].bitcast(BF16); B_h0b = Buf("h0b")
    CEall = R2[:, 1024:2048].bitcast(BF16); B_CEall = Buf("CEall")
    bmsk = R2[:, 2048:2560].bitcast(BF16); B_bmsk = Buf("bmsk")
    h0s = R2[:, 3072:5120]; B_h0s = Buf("h0s")
    scf_s = R2[:, 0:2816].rearrange("p (j s t) -> p j s t", j=88, s=NSEQ)
    ocf_s = R2[:, 2816:5632].rearrange("p (j s t) -> p j s t", j=88, s=NSEQ)

    def wt(name, n, dt=F32):
        return sb(name, [128, n], dt), Buf(name)

    rs1, B_rs1 = wt("rs1", TS)
    rs2, B_rs2 = wt("rs2", TS)
    rsy, B_rsy = wt("rsy", TS)
    t_mean, B_mean = wt("t_mean", TS)
    t_nmr, B_nmr = wt("t_nmr", TS)
    tmpA = [wt(f"tmpA{i}", TS) for i in range(2)]
    tmpB = [wt(f"tmpB{i}", TS) for i in range(2)]
    tmpC = [wt(f"tmpC{i}", TS) for i in range(2)]
    tmpD = [wt(f"tmpD{i}", TS) for i in range(2)]
    ext1 = [wt(f"ext1_{i}", TS + 4) for i in range(2)]
    ext2 = [wt(f"ext2_{i}", TS + 4) for i in range(2)]
    sqb = [wt(f"sqb{i}", TS, BF16) for i in range(2)]
    sqT = minT
    B_sqT = B_min
    dtx, B_dtx = wt("dtx", 64)
    dtl, B_dtl = wt("dtl", 64)
    dte, B_dte = wt("dte", 64)
    dta, B_dta = wt("dta", 64)
    cum, B_cum = wt("cum", 64)
    ncum, B_ncum = wt("ncum", 64)
    tend, B_tend = wt("tend", 64)
    dtte, B_dtte = wt("dtte", 64)
    decB, B_decB = wt("decB", 64)
    xdt, B_xdt = wt("xdt", DB, BF16)
    h0s2, B_h0s2 = wt("h0s2", DB)
    xdtp, B_xdtp = wt("xdtp", DB, BF16)
    btok, B_btok = wt("btok", NG * 128, BF16)
    hst, B_hst = wt("hst", DB)
    hbf, B_hbf = wt("hbf", DB, BF16)
    Rh = [wt(f"Rh{i}", 512, BF16) for i in range(2)]
    Rl = [wt(f"Rl{i}", 512, BF16) for i in range(2)]
    dth, B_dth = wt("dth", 64, BF16)
    dtl2, B_dtl2 = wt("dtl2", 64, BF16)
    mnegp_b, B_mnp = wt("mnegp_b", 512, BF16)
    mnegs_b, B_mns = wt("mnegs_b", 256, BF16)
    trip_b, B_trp = wt("trip_b", 128, BF16)
    tris_b, B_trs = wt("tris_b", 64, BF16)
    Eg = [wt(f"Eg{i}", 512) for i in range(2)]
    Xg = [wt(f"Xg{i}", 512) for i in range(2)]
    decS, B_decS = Xg[1]
    MTg = [wt(f"MTg{i}", 512, BF16) for i in range(2)]
    CEg = [wt(f"CEg{i}", 512, BF16) for i in range(2)]
    v1g = [wt("v1g0", 256)] * 2
    vg = [wt(f"vg{i}", 256) for i in range(2)]
    sqg = [wt(f"sqg{i}", 256, BF16) for i in range(2)]
    onesb, B_onesb = wt("onesb", 128, BF16)
    identb, B_identb = wt("identb", 128, BF16)

    PS = [nc.alloc_psum_tensor(f"ps{i}", [128, 512], F32) for i in range(8)]
    B_PS = [Buf(f"ps{i}") for i in range(8)]
    B_PS7a = B_PS7b = B_PS[7]

    def C(name, a=0, b=None):
        o, n = CST[name]
        if b is None:
            b = n
        return cst[:, o + a:o + b]

    sy.dma("sp", cst[:], cst_d, [], [B_cst])
    sy.dma("pool", wdt[:].rearrange("p k c -> p (k c)"), wDT_d, [], [B_wdt])
    sy.op("dve", [B_cst], [B_onesb], lambda e: e.tensor_copy(out=onesb[:], in_=C("ones")))
    sy.op("dve", [B_cst], [B_identb], lambda e: e.tensor_copy(out=identb[:], in_=C("ident")))
    sy.op("dve", [B_cst], [B_mnp], lambda e: e.tensor_copy(out=mnegp_b[:], in_=C("mneg_p")))
    sy.op("dve", [B_cst], [B_mns], lambda e: e.tensor_copy(out=mnegs_b[:], in_=C("mneg_s")))
    sy.op("dve", [B_cst], [B_trp], lambda e: e.tensor_copy(out=trip_b[:], in_=C("tri_p")))
    sy.op("dve", [B_cst], [B_trs], lambda e: e.tensor_copy(out=tris_b[:], in_=C("tri_s")))
    aB, B_aB = wt("aB", 32)
    sy.op("act", [B_cst], [B_aB], lambda e: e.activation(out=aB[:], in_=C("a_log"), func=AF.Exp))
    sy.op("dve", [B_aB], [B_aB], lambda e: e.tensor_scalar_mul(out=aB[:], in0=aB[:], scalar1=-1.0))
    for t_, b_ in ((haloA, B_hA), (haloB, B_hB), (haloF, B_hF), (hst, B_hst)):
        sy.op("pool", [], [b_], lambda e, t_=t_: e.memset(t_[:], 0.0))
    sy.op("pool", [], [B_hbf], lambda e: e.memset(hbf[:], 0.0))

    ones_f = C("ones")
    ident_f = C("ident")

    wctr = [0]

    wscr = {}
    wdone = {}
    B_ws = [Buf(f"wst{i}") for i in range(NWSLOT)]

    def wload(name, full, idx, n):
        i = wctr[0] % NWSLOT
        wctr[0] += 1
        if name not in wscr:
            wscr[name] = nc.dram_tensor("scr_" + name, [full.shape[0], 128, n], BF16, kind="Internal").ap()
        key = (name, idx)
        if key not in wdone and name in ("wU", "wD") and cache_on[0]:
            sy.dma("pool", wsl[i][:, 0:n], full[idx], [], [B_w[i]])
        elif key not in wdone:
            sy.dma("pool", wsl[i][:, 0:n], full[idx], [], [B_w[i]])
            sy.dma("sp", wscr[name][idx], wsl[i][:, 0:n], [B_w[i]], [], sembuf=B_ws[i])
            wdone[key] = ("dma", B_ws[i].dsem, B_ws[i].dval, "d_" + B_ws[i].name)
        else:
            sy.dma("pool", wsl[i][:, 0:n], wscr[name][idx], [], [B_w[i]], extra=[wdone[key]])
        return wsl[i], B_w[i]

    cache_list = ([x for j in range(KC) for x in (("w6a", w6a_d, j, 24 * 128), ("w6b", w6b_d, j, 32 * 128))]
                  + [("wO", wO_d, i, KC * 256) for i in range(8)])
    cache_it = iter(cache_list)
    cache_on = [False]
    cache_acc = [0.0]

    def cache_tick(amount):
        if not cache_on[0]:
            return
        cache_acc[0] += amount
        while cache_acc[0] >= 1.0:
            cache_acc[0] -= 1.0
            item = next(cache_it, None)
            if item is not None and (item[0], item[2]) not in wdone:
                wload(*item)

    rr = {}

    def rot(lst, key):
        i = rr.get(key, 0)
        rr[key] = i + 1
        return lst[i % len(lst)]

    def rstd_from(ps_ap, Bps, n, out_t, B_out, scale):
        sy.op("act", [Bps], [B_out], lambda e: e.activation(out=out_t[:, 0:n], in_=ps_ap, func=AF.Sqrt,
                                                            bias=EPS, scale=scale))
        sy.op("dve", [B_out], [B_out], lambda e: e.reciprocal(out=out_t[:, 0:n], in_=out_t[:, 0:n]))

    def s1_load_norm(c0, n):
        sy.dma("sp", xT[:, :, 0:n], xT_d[:, :, c0:c0 + n], [], [B_xT])
        sy.op("act", [B_xT], [B_sqT], lambda e: e.activation(out=sqT[:, :, 0:n], in_=xT[:, :, 0:n], func=AF.Square))

        def mm(e):
            for kc in range(KC):
                ins = e.matmul(PS[7][:, 0:n], lhsT=onesb[:], rhs=sqT[:, kc, 0:n], start=(kc == 0), stop=(kc == KC - 1))
            return ins
        sy.op("pe", [B_sqT, B_onesb], [B_PS[7]], mm)
        rstd_from(PS[7][:, 0:n], B_PS[7], n, rs1, B_rs1, 1.0 / D)
        for kc in range(KC):
            sy.op("dve", [B_xT, B_rs1, B_cst], [B_hT], lambda e, kc=kc: e.scalar_tensor_tensor(
                out=hT[:, kc, 0:n], in0=xT[:, kc, 0:n], scalar=C("g_pre1", kc, kc + 1), in1=rs1[:, 0:n],
                op0=ALU.mult, op1=ALU.mult))

    def proj_group(ps, Bps, w, Bw, k0, nk, ncolw, c0w, rhs, Brhs, n):
        def mm(e):
            for k in range(nk):
                o = (k0 + k) * ncolw + c0w
                ins = e.matmul(ps[:, 0:n], lhsT=w[:, o:o + 128], rhs=rhs[:, k, 0:n], start=(k == 0), stop=(k == nk - 1))
            return ins
        sy.op("pe", [Bw] + Brhs, [Bps], mm)

    B_ux = [Buf(f"uext{j}") for j in range(NJA)]
    B_cv = [Buf(f"cvA{j}") for j in range(NJA)]
    fdummy = sb("fdummy", [128, 8]); B_fd = Buf("fdummy")

    def fence(bufs):
        sy.op("dve", [], list(bufs) + [B_fd], lambda e: e.memset(fdummy[:, 0:1], 0.0))

    def s2_begin(np_, ns):
        fence([B_R2] + B_ux + B_cv)
        if ns:
            for jj in range(NJA):
                for hh in range(2):
                    sy.dma("sp", uexts[:, jj, 8 * hh:8 * hh + 8, 0:CA - 1], sca_d[:, jj, 8 * hh:8 * hh + 8, :], [], [B_R1[0], B_R1[1]])

    def s2_chunk(j, np_, ns, defer_taps=False):
        n = np_ + ns
        W31 = CST["w_dw_a"][0]
        w, Bw = wload("wA", wA_d, j, KC * 256)
        pa, Bpa = PS[2 * (j % 2)], B_PS[2 * (j % 2)]
        pb, Bpb = PS[2 * (j % 2) + 1], B_PS[2 * (j % 2) + 1]
        proj_group(pa, Bpa, w, Bw, 0, KC, 256, 0, hT, [B_hT], n)
        proj_group(pb, Bpb, w, Bw, 0, KC, 256, 128, hT, [B_hT], n)
        sg, Bsg = rot(tmpA, "tmpA")
        sy.op("act", [Bpb], [Bsg], lambda e: e.activation(out=sg[:, 0:n], in_=pb[:, 0:n], func=AF.Sigmoid))
        sy.op("dve", [B_hA], [B_ux[j]], lambda e: e.tensor_copy(out=uext[:, j, 0:CA - 1], in_=haloA[:, j, :]))
        sy.op("dve", [Bpa, Bsg], [B_ux[j]], lambda e: e.tensor_tensor(
            out=uext[:, j, CA - 1:CA - 1 + np_], in0=pa[:, 0:np_], in1=sg[:, 0:np_], op=ALU.mult))
        sy.op("dve", [B_ux[j]], [B_hA], lambda e: e.tensor_copy(out=haloA[:, j, :], in_=uext[:, j, np_:np_ + CA - 1]))
        if ns:
            sy.op("dve", [Bpa, Bsg], [B_R1[0], B_R1[1]], lambda e: e.tensor_tensor(
                out=uexts[:, j, :, CA - 1:CA + 3], in0=pa[:, np_:n].rearrange("p (s t) -> p s t", s=NSEQ),
                in1=sg[:, np_:n].rearrange("p (s t) -> p s t", s=NSEQ), op=ALU.mult))
        run_deferred()
        if not defer_taps:
            s2_taps(j, np_, 0, CA)
        if ns:
            s2_sample_taps(j, np_, ns)
        cache_tick(1.0)

    def s2_taps(j, np_, k0, k1):
        W31 = CST["w_dw_a"][0]
        sy.big = True
        for k in range(k0, k1):
            wk = cst[:, W31 + j * CA + k:W31 + j * CA + k + 1]
            if k == 0:
                sy.op("dve", [B_ux[j], B_cst], [B_cv[j]], lambda e: e.tensor_scalar(
                    out=cvA[:, j, 0:np_], in0=uext[:, j, k:k + np_], scalar1=wk, scalar2=C("b_dw_a", j, j + 1),
                    op0=ALU.mult, op1=ALU.add))
            else:
                sy.op("dve", [B_ux[j], B_cst], [B_cv[j]], lambda e: e.scalar_tensor_tensor(
                    out=cvA[:, j, 0:np_], in0=uext[:, j, k:k + np_], scalar=wk, in1=cvA[:, j, 0:np_],
                    op0=ALU.mult, op1=ALU.add))
        sy.big = False

    def s2_sample_taps(j, np_, ns):
        n = np_ + ns
        W31 = CST["w_dw_a"][0]
        if ns:
            cs_ = cvA[:, j, np_:n].rearrange("p (s t) -> p s t", s=NSEQ)
            for k in range(CA):
                wk = cst[:, W31 + j * CA + k:W31 + j * CA + k + 1]
                if k == 0:
                    sy.op("dve", [B_R1[0], B_R1[1], B_cst], [B_cv[j]], lambda e: e.tensor_scalar(
                        out=cs_, in0=uexts[:, j, :, 0:4], scalar1=wk, scalar2=C("b_dw_a", j, j + 1),
                        op0=ALU.mult, op1=ALU.add))
                else:
                    sy.op("dve", [B_R1[0], B_R1[1], B_cst], [B_cv[j]], lambda e: e.scalar_tensor_tensor(
                        out=cs_, in0=uexts[:, j, :, k:k + 4], scalar=wk, in1=cs_, op0=ALU.mult, op1=ALU.add))

    def s2_tail_steps(np_, ns, bsum, bsq):
        n = np_ + ns
        steps = []

        def st_stats(j):
            def f():
                cp, Bcp = rot(sqg, "cpb")
                sq, Bsq = rot(sqb, "sqb")
                sy.op("act", [B_cv[j]], [Bcp], lambda e: e.copy(out=cp[:, 0:n], in_=cvA[:, j, 0:n]))
                sy.op("act", [B_cv[j]], [Bsq], lambda e: e.activation(out=sq[:, 0:n], in_=cvA[:, j, 0:n], func=AF.Square))
                sy.op("pe", [Bcp, B_onesb], [B_PS[bsum]], lambda e: e.matmul(
                    PS[bsum][:, 0:n], lhsT=onesb[:], rhs=cp[:, 0:n], start=(j == 0), stop=(j == NJA - 1)))
                sy.op("pe", [Bsq, B_onesb], [B_PS[bsq]], lambda e: e.matmul(
                    PS[bsq][:, 0:n], lhsT=onesb[:], rhs=sq[:, 0:n], start=(j == 0), stop=(j == NJA - 1)))
            return f

        def st_chain():
            sy.op("act", [B_PS[bsum]], [B_mean], lambda e: e.mul(out=t_mean[:, 0:n], in_=PS[bsum][:, 0:n], mul=1.0 / DA))
            m2, Bm2 = tmpD[1]
            sy.op("dve", [B_mean], [Bm2], lambda e: e.tensor_tensor(out=m2[:, 0:n], in0=t_mean[:, 0:n], in1=t_mean[:, 0:n], op=ALU.mult))
            sy.op("dve", [B_PS[bsq], Bm2], [Bm2], lambda e: e.scalar_tensor_tensor(
                out=m2[:, 0:n], in0=PS[bsq][:, 0:n], scalar=1.0 / DA, in1=m2[:, 0:n], op0=ALU.mult, op1=ALU.subtract))
            sy.op("act", [Bm2], [B_rs2], lambda e: e.activation(out=rs2[:, 0:n], in_=m2[:, 0:n], func=AF.Sqrt, bias=EPS, scale=1.0))
            sy.op("dve", [B_rs2], [B_rs2], lambda e: e.reciprocal(out=rs2[:, 0:n], in_=rs2[:, 0:n]))
            sy.op("dve", [B_mean, B_rs2], [B_nmr], lambda e: e.scalar_tensor_tensor(
                out=t_nmr[:, 0:n], in0=t_mean[:, 0:n], scalar=-1.0, in1=rs2[:, 0:n], op0=ALU.mult, op1=ALU.mult))

        def st_norm(j):
            def f():
                t1, Bt1 = rot(tmpC, "tmpC")
                sy.big = True
                sy.op("dve", [B_cv[j], B_rs2], [Bt1], lambda e: e.tensor_tensor(out=t1[:, 0:n], in0=cvA[:, j, 0:n], in1=rs2[:, 0:n], op=ALU.mult))
                sy.op("dve", [Bt1, B_nmr], [Bt1], lambda e: e.tensor_tensor(out=t1[:, 0:n], in0=t1[:, 0:n], in1=t_nmr[:, 0:n], op=ALU.add))
                sy.big = False
                sy.op("act", [Bt1, B_cst], [B_uA], lambda e: e.activation(
                    out=uA[:, j, 0:n], in_=t1[:, 0:n], func=AF.Silu, bias=C("b_ln_a", j, j + 1), scale=C("g_ln_a", j, j + 1)))
            return f
        if ns:
            def st_osca():
                for jj in range(NJA):
                    for hh in range(2):
                        sy.dma("sp", osca_o[:, jj, 8 * hh:8 * hh + 8, :], uexts[:, jj, 8 * hh:8 * hh + 8, 4:CA + 3],
                               [B_R1[0], B_R1[1]], [], sembuf=B_R1[0])
            steps.append(st_osca)
        steps += [st_stats(j) for j in range(NJA)]
        steps.append(st_chain)
        steps += [st_norm(j) for j in range(NJA)]
        steps.append(lambda: fence(B_ux + B_cv + [B_R2]))
        return steps

    def s2_tail(np_, ns):
        for f in s2_tail_steps(np_, ns, 6, 7):
            f()

    def halo_a_from_prefix(n_last):
        c0 = n_last - 32
        for j in range(NJA):
            w, Bw = wload("wA", wA_d, j, KC * 256)
            pa, Bpa = PS[2 * (j % 2)], B_PS[2 * (j % 2)]
            pb, Bpb = PS[2 * (j % 2) + 1], B_PS[2 * (j % 2) + 1]

            def mm(e, c0w, ps):
                for k in range(KC):
                    o = k * 256 + c0w
                    ins = e.matmul(ps[:, 0:32], lhsT=w[:, o:o + 128], rhs=hT[:, k, c0:c0 + 32], start=(k == 0), stop=(k == KC - 1))
                return ins
            sy.op("pe", [Bw, B_hT], [Bpa], lambda e: mm(e, 0, pa))
            sy.op("pe", [Bw, B_hT], [Bpb], lambda e: mm(e, 128, pb))
            sg, Bsg = rot(tmpA, "tmpA")
            sy.op("act", [Bpb], [Bsg], lambda e: e.activation(out=sg[:, 0:32], in_=pb[:, 0:32], func=AF.Sigmoid))
            sy.op("dve", [Bpa, Bsg], [B_hA], lambda e: e.tensor_tensor(
                out=haloA[:, j, :], in0=pa[:, 2:32], in1=sg[:, 2:32], op=ALU.mult))

    wcur = {}

    def s3_begin(ns):
        if ns:
            sy.dma("sp", scb_s, scb_d, [], [B_scb])

    def s3_chunk(cc, np_, ns, pbase):
        n = np_ + ns
        W4 = CST["w_dw_b"][0]
        if cc % 2 == 0:
            wcur["x"] = wload("wX", wX_d, cc // 2, KC * 256)
        w, Bw = wcur["x"]
        ps, Bps = PS[pbase + cc % 2], B_PS[pbase + cc % 2]
        proj_group(ps, Bps, w, Bw, 0, KC, 256, 128 * (cc % 2), hT, [B_hT], n)
        run_deferred()
        ex, Bex = rot(ext1, "ext1")
        ac, Bac = rot(tmpB, "tmpBx")
        sy.op("act", [Bps], [Bex], lambda e: e.copy(out=ex[:, 3:3 + np_], in_=ps[:, 0:np_]))
        sy.op("act", [Bps, B_cst], [Bac], lambda e: e.activation(
            out=ac[:, 0:np_], in_=ps[:, 0:np_], func=AF.Identity, bias=C("b_dw_b", cc, cc + 1),
            scale=cst[:, W4 + cc * CB + 3:W4 + cc * CB + 4]))
        sy.op("dve", [B_hB], [Bex], lambda e: e.tensor_copy(out=ex[:, 0:3], in_=haloB[:, cc, :]))
        sy.op("dve", [Bex], [B_hB], lambda e: e.tensor_copy(out=haloB[:, cc, :], in_=ex[:, np_:np_ + 3]))
        sy.big = True
        for k in range(CB - 1):
            wk = cst[:, W4 + cc * CB + k:W4 + cc * CB + k + 1]
            sy.op("dve", [Bex, B_cst], [Bac], lambda e: e.scalar_tensor_tensor(
                out=ac[:, 0:np_], in0=ex[:, k:k + np_], scalar=wk, in1=ac[:, 0:np_], op0=ALU.mult, op1=ALU.add))
        sy.big = False
        if ns:
            exs, Bexs = rot(ext2, "ext2")
            exv = exs[:, 0:NSEQ * 7].rearrange("p (s t) -> p s t", s=NSEQ)
            acv = ac[:, np_:n].rearrange("p (s t) -> p s t", s=NSEQ)
            sy.op("dve", [B_scb], [Bexs], lambda e: e.tensor_copy(out=exv[:, :, 0:3], in_=scb_s[:, cc, :, :]))
            sy.op("act", [Bps], [Bexs], lambda e: e.copy(out=exv[:, :, 3:7], in_=ps[:, np_:n].rearrange("p (s t) -> p s t", s=NSEQ)))
            sy.op("dve", [Bexs], [B_ocb], lambda e: e.tensor_copy(out=ocb_s[:, cc, :, :], in_=exv[:, :, 4:7]))
            for k in range(CB):
                wk = cst[:, W4 + cc * CB + k:W4 + cc * CB + k + 1]
                if k == 0:
                    sy.op("dve", [Bexs, B_cst], [Bac], lambda e: e.tensor_scalar(
                        out=acv, in0=exv[:, :, 0:4], scalar1=wk, scalar2=C("b_dw_b", cc, cc + 1), op0=ALU.mult, op1=ALU.add))
                else:
                    sy.op("dve", [Bexs, B_cst], [Bac], lambda e: e.scalar_tensor_tensor(
                        out=acv, in0=exv[:, :, k:k + 4], scalar=wk, in1=acv, op0=ALU.mult, op1=ALU.add))
        if cc < 16:
            dst, Bd = xsT[:, cc, 0:n], B_R1[0]
        elif cc < 24:
            dst, Bd = BT[:, cc - 16, 0:n], B_R1[1]
        else:
            dst, Bd = CT[:, cc - 24, 0:n], B_R1[2]
        sy.op("act", [Bac], [Bd], lambda e: e.activation(out=dst, in_=ac[:, 0:n], func=AF.Silu))
        cache_tick(0.5)

    def s3_end(ns):
        if ns:
            sy.dma("sp", oscb_o, ocb_s, [B_ocb], [])

    def s3_xbc(np_, ns, ncc):
        s3_begin(ns)
        for cc in range(ncc):
            s3_chunk(cc, np_, ns, 0)
        s3_end(ns)

    def s4_chunk(cc, n, pbase):
        if cc % 2 == 0:
            wcur["z"] = wload("wZ", wZ_d, cc // 2, KC * 256)
        w, Bw = wcur["z"]
        ps, Bps = PS[pbase + cc % 2], B_PS[pbase + cc % 2]
        proj_group(ps, Bps, w, Bw, 0, KC, 256, 128 * (cc % 2), hT, [B_hT], n)
        sy.op("act", [Bps], [B_zs], lambda e: e.activation(out=zs[:, cc, 0:n], in_=ps[:, 0:n], func=AF.Silu))
        cache_tick(0.5)

    def s234(np_, ns):
        n = np_ + ns
        if ns:
            s2_begin(np_, ns)
            for j in range(NJA):
                s2_chunk(j, np_, ns)
            s2_tail(np_, ns)
            s3_xbc(np_, ns, 32)
            for cc in range(KC):
                s4_chunk(cc, n, 0)
            return []
        s2_begin(np_, ns)
        s3_begin(ns)
        for j in range(NJA):
            s2_chunk(j, np_, ns, defer_taps=True)
            s2_taps(j, np_, 0, 8)
            s3_chunk(4 * j, np_, ns, 4)
            s2_taps(j, np_, 8, 16)
            s3_chunk(4 * j + 1, np_, ns, 4)
            s4_chunk(2 * j, n, 6)
            s2_taps(j, np_, 16, 24)
            s3_chunk(4 * j + 2, np_, ns, 4)
            s2_taps(j, np_, 24, CA)
            s3_chunk(4 * j + 3, np_, ns, 4)
            s4_chunk(2 * j + 1, n, 6)
        s3_end(ns)
        steps = s2_tail_steps(np_, ns, 6, 7)
        while steps:
            steps.pop(0)()
        return steps

    deferred = []
    TOFF = [0]

    def run_deferred():
        while deferred:
            deferred.pop(0)()

    def ssd_dt_all(tiles):
        nt = 128
        T = len(tiles)
        W = 32 * T

        def v3(ap):
            return ap.rearrange("p (t h) -> p t h", t=T)

        def mm(e):
            for t, (c0, _) in enumerate(tiles):
                for kc in range(KC):
                    ins = e.matmul(PS[5][0:nt, 32 * t:32 * t + 32], lhsT=hT[:, kc, c0:c0 + nt], rhs=wdt[:, kc, :], start=(kc == 0), stop=(kc == KC - 1))
            return ins
        sy.op("pe", [B_hT, B_wdt], [B_PS[5]], mm)
        sy.op("dve", [B_PS[5], B_cst], [B_dtx], lambda e: e.tensor_tensor(
            out=v3(dtx[:, 0:W]), in0=v3(PS[5][:, 0:W]), in1=C("dt_bias").unsqueeze(1).to_broadcast([128, T, 32]), op=ALU.add))
        sy.op("dve", [B_dtx], [B_dtl], lambda e: e.scalar_tensor_tensor(out=dtl[:, 0:W], in0=dtx[:, 0:W], scalar=-1.0, in1=dtx[:, 0:W], op0=ALU.mult, op1=ALU.max))
        sy.op("act", [B_dtl], [B_dtl], lambda e: e.activation(out=dtl[:, 0:W], in_=dtl[:, 0:W], func=AF.Exp, scale=-1.0))
        sy.op("act", [B_dtl], [B_dtl], lambda e: e.activation(out=dtl[:, 0:W], in_=dtl[:, 0:W], func=AF.Ln, bias=1.0, scale=1.0))
        sy.op("dve", [B_dtx, B_dtl], [B_dte], lambda e: e.scalar_tensor_tensor(
            out=dte[:, 0:W], in0=dtx[:, 0:W], scalar=0.0, in1=dtl[:, 0:W], op0=ALU.max, op1=ALU.add))
        for t, (_, vidx) in enumerate(tiles):
            vo = CST["valid"][0] + vidx
            sy.op("dve", [B_dte, B_cst], [B_dte], lambda e, t=t, vo=vo: e.tensor_scalar_mul(
                out=dte[:, 32 * t:32 * t + 32], in0=dte[:, 32 * t:32 * t + 32], scalar1=cst[:, vo:vo + 1]))
        sy.op("dve", [B_dte, B_aB], [B_dta], lambda e: e.tensor_tensor(
            out=v3(dta[:, 0:W]), in0=v3(dte[:, 0:W]), in1=aB[:, :].unsqueeze(1).to_broadcast([128, T, 32]), op=ALU.mult))
        sy.op("dve", [B_dta], [B_dth], lambda e: e.tensor_copy(out=dth[:, 0:W], in_=dta[:, 0:W]))
        sy.op("dve", [B_dta, B_dth], [B_dtl2], lambda e: e.tensor_tensor(out=dtl2[:, 0:W], in0=dta[:, 0:W], in1=dth[:, 0:W], op=ALU.subtract))

        def part2():
            sy.op("pe", [B_dta, B_cst], [B_PS[5]], lambda e: e.matmul(PS[5][:, 64:64 + W], lhsT=C("tri_p"), rhs=dta[:, 0:W], start=True, stop=True))
            sy.op("pe", [B_dta, B_cst], [B_PS[5]], lambda e: e.matmul(PS[5][:, 128:128 + W], lhsT=ones_f, rhs=dta[:, 0:W], start=True, stop=True))
            sy.op("act", [B_PS[5]], [B_cum], lambda e: e.copy(out=cum[:, 0:W], in_=PS[5][:, 64:64 + W]))
            sy.op("act", [B_PS[5]], [B_ncum], lambda e: e.mul(out=ncum[:, 0:W], in_=PS[5][:, 64:64 + W], mul=-1.0))
            sy.op("dve", [B_PS[5], B_cum], [B_tend], lambda e: e.tensor_tensor(out=tend[:, 0:W], in0=PS[5][:, 128:128 + W], in1=cum[:, 0:W], op=ALU.subtract))
            sy.op("act", [B_tend], [B_tend], lambda e: e.activation(out=tend[:, 0:W], in_=tend[:, 0:W], func=AF.Exp))
            sy.op("act", [B_PS[5]], [B_decB], lambda e: e.activation(out=decB[:, 0:W], in_=PS[5][:, 128:128 + W], func=AF.Exp))
            sy.op("dve", [B_dte, B_tend], [B_dtte], lambda e: e.tensor_tensor(out=dtte[:, 0:W], in0=dte[:, 0:W], in1=tend[:, 0:W], op=ALU.mult))
        deferred.append(part2)

    def ssd_prelude(c0, nt, vidx, sample):
        tri = C("tri_s") if sample else C("tri_p")
        clm = C("blk_s") if sample else ones_f
        run_deferred()
        if sample:
            TOFF[0] = 0
            ssd_dt_sample(c0, nt, vidx, tri, clm)
        ssd_transposes(c0, nt)

    def ssd_dt_sample(c0, nt, vidx, tri, clm):
        sample = True

        def mm(e):
            for kc in range(KC):
                ins = e.matmul(PS[5][0:nt, 0:32], lhsT=hT[:, kc, c0:c0 + nt], rhs=wdt[:, kc, :], start=(kc == 0), stop=(kc == KC - 1))
            return ins
        sy.op("pe", [B_hT, B_wdt], [B_PS[5]], mm)
        sy.op("dve", [B_PS[5], B_cst], [B_dtx], lambda e: e.tensor_tensor(out=dtx[0:nt, 0:32], in0=PS[5][0:nt, 0:32], in1=C("dt_bias")[0:nt, :], op=ALU.add))
        sy.op("dve", [B_dtx], [B_dtl], lambda e: e.scalar_tensor_tensor(out=dtl[0:nt, 0:32], in0=dtx[0:nt, 0:32], scalar=-1.0, in1=dtx[0:nt, 0:32], op0=ALU.mult, op1=ALU.max))
        sy.op("act", [B_dtl], [B_dtl], lambda e: e.activation(out=dtl[0:nt, 0:32], in_=dtl[0:nt, 0:32], func=AF.Exp, scale=-1.0))
        sy.op("act", [B_dtl], [B_dtl], lambda e: e.activation(out=dtl[0:nt, 0:32], in_=dtl[0:nt, 0:32], func=AF.Ln, bias=1.0, scale=1.0))
        sy.op("dve", [B_dtx, B_dtl], [B_dte], lambda e: e.scalar_tensor_tensor(
            out=dte[0:nt, 0:32], in0=dtx[0:nt, 0:32], scalar=0.0, in1=dtl[0:nt, 0:32], op0=ALU.max, op1=ALU.add))
        vo = CST["valid"][0] + vidx
        sy.op("dve", [B_dte, B_cst], [B_dte], lambda e: e.tensor_scalar_mul(out=dte[0:nt, 0:32], in0=dte[0:nt, 0:32], scalar1=cst[0:nt, vo:vo + 1]))
        sy.op("dve", [B_dte, B_aB], [B_dta], lambda e: e.tensor_tensor(out=dta[0:nt, 0:32], in0=dte[0:nt, 0:32], in1=aB[0:nt, :], op=ALU.mult))
        sy.op("dve", [B_dta], [B_dth], lambda e: e.tensor_copy(out=dth[0:nt, 0:32], in_=dta[0:nt, 0:32]))
        sy.op("dve", [B_dta, B_dth], [B_dtl2], lambda e: e.tensor_tensor(out=dtl2[0:nt, 0:32], in0=dta[0:nt, 0:32], in1=dth[0:nt, 0:32], op=ALU.subtract))
        sy.op("pe", [B_dta, B_cst], [B_PS[5]], lambda e: e.matmul(PS[5][0:nt, 32:64], lhsT=tri[0:nt, 0:nt], rhs=dta[0:nt, 0:32], start=True, stop=True))
        sy.op("pe", [B_dta, B_cst], [B_PS[5]], lambda e: e.matmul(PS[5][0:nt, 64:96], lhsT=clm[0:nt, 0:nt], rhs=dta[0:nt, 0:32], start=True, stop=True))
        if not sample:
            sy.op("pe", [B_dta, B_cst], [B_PS[5]], lambda e: e.matmul(PS[5][:, 96:128], lhsT=ones_f[0:nt, :], rhs=dta[0:nt, 0:32], start=True, stop=True))
        sy.op("act", [B_PS[5]], [B_cum], lambda e: e.copy(out=cum[0:nt, 0:32], in_=PS[5][0:nt, 32:64]))
        sy.op("act", [B_PS[5]], [B_ncum], lambda e: e.mul(out=ncum[0:nt, 0:32], in_=PS[5][0:nt, 32:64], mul=-1.0))
        sy.op("dve", [B_PS[5], B_cum], [B_tend], lambda e: e.tensor_tensor(out=tend[0:nt, 0:32], in0=PS[5][0:nt, 64:96], in1=cum[0:nt, 0:32], op=ALU.subtract))
        sy.op("act", [B_tend], [B_tend], lambda e: e.activation(out=tend[0:nt, 0:32], in_=tend[0:nt, 0:32], func=AF.Exp))
        if not sample:
            sy.op("act", [B_PS[5]], [B_decB], lambda e: e.activation(out=decB[:], in_=PS[5][:, 96:128], func=AF.Exp))
        sy.op("dve", [B_dte, B_tend], [B_dtte], lambda e: e.tensor_tensor(out=dtte[0:nt, 0:32], in0=dte[0:nt, 0:32], in1=tend[0:nt, 0:32], op=ALU.mult))

    def ssd_transposes(c0, nt):
        o = TOFF[0]
        for b4 in range(4):
            def tr(e, b4=b4):
                for jj in range(4):
                    j = 4 * b4 + jj
                    ins = e.transpose(out=PS[b4][0:nt, jj * 128:(jj + 1) * 128], in_=xsT[:, j, c0:c0 + nt], identity=ident_f)
                return ins
            sy.op("pe", [B_R1[0], B_cst], [B_PS[b4]], tr)
        for b4 in range(4):
            pv = PS[b4][0:nt, :].rearrange("p (h x) -> p h x", h=8)
            sy.op("dve", [B_PS[b4], B_dte], [B_xdt], lambda e, b4=b4, pv=pv: e.tensor_tensor(
                out=xdt[0:nt, b4 * 512:(b4 + 1) * 512].rearrange("p (h x) -> p h x", h=8), in0=pv,
                in1=dte[0:nt, o + b4 * 8:o + (b4 + 1) * 8].unsqueeze(2).to_broadcast([nt, 8, 64]), op=ALU.mult))
            sy.op("dve", [B_PS[b4], B_dtte], [B_xdtp], lambda e, b4=b4, pv=pv: e.tensor_tensor(
                out=xdtp[0:nt, b4 * 512:(b4 + 1) * 512].rearrange("p (h x) -> p h x", h=8), in0=pv,
                in1=dtte[0:nt, o + b4 * 8:o + (b4 + 1) * 8].unsqueeze(2).to_broadcast([nt, 8, 64]), op=ALU.mult))
        psb = PS[4].bitcast(BF16)

        def trb(e):
            for g in range(NG):
                ins = e.transpose(out=psb[0:nt, g * 128:(g + 1) * 128], in_=BT[:, g, c0:c0 + nt], identity=identb[:])
            return ins
        sy.op("pe", [B_R1[1], B_identb], [B_PS[4]], trb)
        sy.op("act", [B_PS[4]], [B_btok], lambda e: e.copy(out=btok[0:nt, :], in_=psb[0:nt, 0:NG * 128]))

    def ssd_group_common(g, c0, nt, sample):
        trib = tris_b if sample else trip_b
        mneg = mnegs_b if sample else mnegp_b
        Bcon = [B_trs, B_mns] if sample else [B_trp, B_mnp]
        p1, p2, pc = (5, 6, 4) if (g % 2 == 0 or sample) else (1, 2, 3)
        Rh_, BRh = rot(Rh, "Rh")
        Rl_, BRl = rot(Rl, "Rl")
        E_, BE = rot(Eg, "Eg")
        X_, BX = rot(Xg, "Xg")
        MT_, BMT = rot(MTg, "MTg")
        CE_, BCE = rot(CEg, "CEg")
        w4 = 4 * nt
        for R_, BR, src, Bsrc in ((Rh_, BRh, dth, B_dth), (Rl_, BRl, dtl2, B_dtl2)):
            sy.op("dve", [Bsrc] + Bcon, [BR], lambda e, R_=R_, src=src: e.tensor_tensor(
                out=R_[0:nt, 0:w4].rearrange("p (i q) -> p i q", i=4), in0=trib[0:nt, 0:nt].unsqueeze(1).to_broadcast([nt, 4, nt]),
                in1=src[0:nt, 4 * g:4 * g + 4].unsqueeze(2).to_broadcast([nt, 4, nt]), op=ALU.mult))

        def mm1(e):
            e.matmul(PS[p1][:, 0:w4], lhsT=onesb[0:nt, :], rhs=Rh_[0:nt, 0:w4], start=True, stop=False)
            return e.matmul(PS[p1][:, 0:w4], lhsT=onesb[0:nt, :], rhs=Rl_[0:nt, 0:w4], start=False, stop=True)
        sy.op("pe", [BRh, BRl, B_onesb], [B_PS[p1]], mm1)

        def mm2(e):
            e.matmul(PS[p2][0:nt, 0:w4], lhsT=onesb[0:nt, 0:nt], rhs=Rh_[0:nt, 0:w4], start=True, stop=False)
            e.matmul(PS[p2][0:nt, 0:w4], lhsT=onesb[0:nt, 0:nt], rhs=Rl_[0:nt, 0:w4], start=False, stop=False)
            return e.matmul(PS[p2][0:nt, 0:w4], lhsT=identb[0:nt, 0:nt], rhs=mneg[0:nt, 0:w4], start=False, stop=True)
        sy.op("pe", [BRh, BRl, B_onesb, B_identb] + Bcon, [B_PS[p2]], mm2)
        sy.op("pe", [B_R1[1], B_R1[2]], [B_PS[pc]], lambda e: e.matmul(
            PS[pc][0:nt, 0:nt], lhsT=BT[:, g, c0:c0 + nt], rhs=CT[:, g, c0:c0 + nt], start=True, stop=True))
        sy.op("act", [B_PS[p1]], [BX], lambda e: e.activation(out=X_[:, 0:w4], in_=PS[p1][:, 0:w4], func=AF.Exp))
        for i in range(4):
            sy.op("act", [B_PS[p2], B_ncum], [BE], lambda e, i=i: e.activation(
                out=E_[0:nt, i * nt:(i + 1) * nt], in_=PS[p2][0:nt, i * nt:(i + 1) * nt], func=AF.Exp,
                bias=ncum[0:nt, 4 * g + i:4 * g + i + 1], scale=1.0))
        sy.big = True
        sy.op("dve", [BE, B_PS[pc]], [BMT], lambda e: e.tensor_tensor(
            out=MT_[0:nt, 0:w4].rearrange("p (i q) -> p i q", i=4), in0=E_[0:nt, 0:w4].rearrange("p (i q) -> p i q", i=4),
            in1=PS[pc][0:nt, 0:nt].unsqueeze(1).to_broadcast([nt, 4, nt]), op=ALU.mult))
        sy.op("dve", [BX, B_R1[2]], [BCE], lambda e: e.tensor_tensor(
            out=CE_[:, 0:w4].rearrange("p (i q) -> p i q", i=4), in0=X_[:, 0:w4].rearrange("p (i q) -> p i q", i=4),
            in1=CT[:, g, c0:c0 + nt].unsqueeze(1).to_broadcast([128, 4, nt]), op=ALU.mult))
        sy.big = False
        return MT_, BMT, CE_, BCE

    def gate_group(g, c0, nt, psy_ap, Bpsy, first, last, sbank):
        v1_, Bv1 = rot(v1g, "v1g")
        v_, Bv = rot(vg, "vg")
        sq_, Bsq = rot(sqg, "sqg")
        for c in range(2):
            kc = 2 * g + c
            sy.op("dve", [B_R1[0], B_cst, Bpsy], [Bv1], lambda e, c=c, kc=kc: e.scalar_tensor_tensor(
                out=v1_[:, c * nt:(c + 1) * nt], in0=xsT[:, kc, c0:c0 + nt], scalar=C("dsk", kc, kc + 1),
                in1=psy_ap[:, c, 0:nt], op0=ALU.mult, op1=ALU.add))
        sy.op("dve", [Bv1, B_zs], [Bv], lambda e: e.tensor_tensor(
            out=v_[:, 0:2 * nt].rearrange("p (c t) -> p c t", c=2), in0=v1_[:, 0:2 * nt].rearrange("p (c t) -> p c t", c=2),
            in1=zs[:, 2 * g:2 * g + 2, c0:c0 + nt], op=ALU.mult))
        sy.op("act", [Bv], [Bsq], lambda e: e.activation(out=sq_[:, 0:2 * nt], in_=v_[:, 0:2 * nt], func=AF.Square))
        for c in range(2):
            kc = 2 * g + c
            sy.op("act", [Bv, B_cst], [B_gy], lambda e, c=c, kc=kc: e.activation(
                out=gyT[:, kc, c0:c0 + nt], in_=v_[:, c * nt:(c + 1) * nt], func=AF.Copy, scale=C("g_norm_b", kc, kc + 1)))

        def mm(e):
            e.matmul(PS[sbank][:, 0:nt], lhsT=onesb[:], rhs=sq_[:, 0:nt], start=first, stop=False)
            return e.matmul(PS[sbank][:, 0:nt], lhsT=onesb[:], rhs=sq_[:, nt:2 * nt], start=False, stop=last)
        sy.op("pe", [Bsq, B_onesb], [B_PS[sbank]], mm)

    def ssd_tile(c0, vidx, want_y, steps=None, toff=0):
        nt = 128
        w4 = 4 * nt
        TOFF[0] = TO = toff
        ssd_prelude(c0, nt, vidx, False)
        if want_y:
            def buf(lst, g):
                return lst[g % 2]

            def st_R(g):
                for lst, src, Bsrc in ((Rh, dth, B_dth), (Rl, dtl2, B_dtl2)):
                    R_, BR = buf(lst, g)
                    sy.op("dve", [Bsrc, B_trp], [BR], lambda e, R_=R_, src=src: e.tensor_tensor(
                        out=R_[0:nt, 0:w4].rearrange("p (i q) -> p i q", i=4), in0=trip_b[0:nt, 0:nt].unsqueeze(1).to_broadcast([nt, 4, nt]),
                        in1=src[0:nt, TO + 4 * g:TO + 4 * g + 4].unsqueeze(2).to_broadcast([nt, 4, nt]), op=ALU.mult))

            def st_P(g):
                p1, p2, pc = (5, 6, 4) if g % 2 == 0 else (1, 2, 3)
                (Rh_, BRh), (Rl_, BRl) = buf(Rh, g), buf(Rl, g)
                (E_, BE), (X_, BX) = buf(Eg, g), buf(Xg, g)

                def mm1(e):
                    e.matmul(PS[p1][:, 0:w4], lhsT=onesb[0:nt, :], rhs=Rh_[0:nt, 0:w4], start=True, stop=False)
                    return e.matmul(PS[p1][:, 0:w4], lhsT=onesb[0:nt, :], rhs=Rl_[0:nt, 0:w4], start=False, stop=True)
                sy.op("pe", [BRh, BRl, B_onesb], [B_PS[p1]], mm1)

                def mm2(e):
                    e.matmul(PS[p2][0:nt, 0:w4], lhsT=onesb[0:nt, 0:nt], rhs=Rh_[0:nt, 0:w4], start=True, stop=False)
                    e.matmul(PS[p2][0:nt, 0:w4], lhsT=onesb[0:nt, 0:nt], rhs=Rl_[0:nt, 0:w4], start=False, stop=False)
                    return e.matmul(PS[p2][0:nt, 0:w4], lhsT=identb[0:nt, 0:nt], rhs=mnegp_b[0:nt, 0:w4], start=False, stop=True)
                sy.op("pe", [BRh, BRl, B_onesb, B_identb, B_mnp], [B_PS[p2]], mm2)
                sy.op("pe", [B_R1[1], B_R1[2]], [B_PS[pc]], lambda e: e.matmul(
                    PS[pc][0:nt, 0:nt], lhsT=BT[:, g, c0:c0 + nt], rhs=CT[:, g, c0:c0 + nt], start=True, stop=True))
                sy.op("act", [B_PS[p1]], [BX], lambda e: e.activation(out=X_[:, 0:w4], in_=PS[p1][:, 0:w4], func=AF.Exp))
                for i in range(4):
                    sy.op("act", [B_PS[p2], B_ncum], [BE], lambda e, i=i: e.activation(
                        out=E_[0:nt, i * nt:(i + 1) * nt], in_=PS[p2][0:nt, i * nt:(i + 1) * nt], func=AF.Exp,
                        bias=ncum[0:nt, TO + 4 * g + i:TO + 4 * g + i + 1], scale=1.0))

            def st_M(g):
                pc = 4 if g % 2 == 0 else 3
                (E_, BE), (X_, BX) = buf(Eg, g), buf(Xg, g)
                (MT_, BMT), (CE_, BCE) = buf(MTg, g), buf(CEg, g)
                sy.big = True
                sy.op("dve", [BE, B_PS[pc]], [BMT], lambda e: e.tensor_tensor(
                    out=MT_[0:nt, 0:w4].rearrange("p (i q) -> p i q", i=4), in0=E_[0:nt, 0:w4].rearrange("p (i q) -> p i q", i=4),
                    in1=PS[pc][0:nt, 0:nt].unsqueeze(1).to_broadcast([nt, 4, nt]), op=ALU.mult))
                sy.op("dve", [BX, B_R1[2]], [BCE], lambda e: e.tensor_tensor(
                    out=CE_[:, 0:w4].rearrange("p (i q) -> p i q", i=4), in0=X_[:, 0:w4].rearrange("p (i q) -> p i q", i=4),
                    in1=CT[:, g, c0:c0 + nt].unsqueeze(1).to_broadcast([128, 4, nt]), op=ALU.mult))
                sy.big = False

            def psy_of(g):
                return PS[7][:, (g % 2) * 256:(g % 2) * 256 + 256].rearrange("p (c t) -> p c t", c=2)

            def st_Y(g):
                (MT_, BMT), (CE_, BCE) = buf(MTg, g), buf(CEg, g)
                psy = psy_of(g)

                def mmy(e):
                    for i in range(4):
                        hd = 4 * g + i
                        hf = hd % 2
                        o = psy[hf * 64:(hf + 1) * 64, i // 2, :]
                        e.matmul(o, lhsT=xdt[0:nt, hd * 64:(hd + 1) * 64], rhs=MT_[0:nt, i * nt:(i + 1) * nt], start=True, stop=False)
                        ins = e.matmul(o, lhsT=hbf[:, hd * 64:(hd + 1) * 64], rhs=CE_[:, i * nt:(i + 1) * nt], start=False, stop=True)
                    return ins
                sy.op("pe", [B_xdt, BMT, B_hbf, BCE], [B_PS7a], mmy)

            def st_V(g):
                psy = psy_of(g)
                (v1_, Bv1), (v_, Bv), (sq_, Bsq) = buf(v1g, g), buf(vg, g), buf(sqg, g)
                for c in range(2):
                    kc = 2 * g + c
                    sy.op("dve", [B_R1[0], B_cst, B_PS7a], [Bv1], lambda e, c=c, kc=kc: e.scalar_tensor_tensor(
                        out=v1_[:, c * nt:(c + 1) * nt], in0=xsT[:, kc, c0:c0 + nt], scalar=C("dsk", kc, kc + 1),
                        in1=psy[:, c, 0:nt], op0=ALU.mult, op1=ALU.add))
                sy.big = True
                sy.op("dve", [Bv1, B_zs], [Bv], lambda e: e.tensor_tensor(
                    out=v_[:, 0:2 * nt].rearrange("p (c t) -> p c t", c=2), in0=v1_[:, 0:2 * nt].rearrange("p (c t) -> p c t", c=2),
                    in1=zs[:, 2 * g:2 * g + 2, c0:c0 + nt], op=ALU.mult))
                sy.big = False
                sy.op("act", [Bv], [Bsq], lambda e: e.activation(out=sq_[:, 0:2 * nt], in_=v_[:, 0:2 * nt], func=AF.Square))
                for c in range(2):
                    kc = 2 * g + c
                    sy.op("act", [Bv, B_cst], [B_gy], lambda e, c=c, kc=kc: e.activation(
                        out=gyT[:, kc, c0:c0 + nt], in_=v_[:, c * nt:(c + 1) * nt], func=AF.Copy, scale=C("g_norm_b", kc, kc + 1)))

            def st_S(g):
                sq_, Bsq = buf(sqg, g)

                def mm(e):
                    e.matmul(PS[0][:, 0:nt], lhsT=onesb[:], rhs=sq_[:, 0:nt], start=(g == 0), stop=False)
                    return e.matmul(PS[0][:, 0:nt], lhsT=onesb[:], rhs=sq_[:, nt:2 * nt], start=False, stop=(g == NG - 1))
                sy.op("pe", [Bsq, B_onesb], [B_PS[0]], mm)

            st_R(0)
            st_R(1)
            st_P(0)
            st_M(0)
            for i in range(NG + 1):
                if i + 2 < NG:
                    st_R(i + 2)
                if i + 1 < NG:
                    st_P(i + 1)
                if i < NG:
                    st_Y(i)
                    st_V(i)
                if i + 1 < NG:
                    st_M(i + 1)
                if 1 <= i:
                    st_S(i - 1)
            sy.op("act", [B_PS[0]], [B_rsy], lambda e: e.activation(out=rsy[:, c0:c0 + nt], in_=PS[0][:, 0:nt], func=AF.Sqrt, bias=EPS, scale=1.0 / DB))
            sy.op("dve", [B_rsy], [B_rsy], lambda e: e.reciprocal(out=rsy[:, c0:c0 + nt], in_=rsy[:, c0:c0 + nt]))
        for b4 in range(4):
            def mms(e, b4=b4):
                for gg in range(2):
                    g = 2 * b4 + gg
                    ins = e.matmul(PS[b4][:, gg * 256:(gg + 1) * 256], lhsT=btok[0:nt, g * 128:(g + 1) * 128],
                                   rhs=xdtp[0:nt, g * 256:(g + 1) * 256], start=True, stop=True)
                return ins
            sy.op("pe", [B_btok, B_xdtp], [B_PS[b4]], mms)
        sy.op("dve", [B_hst, B_decB], [B_hst], lambda e: e.tensor_tensor(
            out=hst[:].rearrange("p (h x) -> p h x", h=H), in0=hst[:].rearrange("p (h x) -> p h x", h=H),
            in1=decB[:, toff:toff + 32].unsqueeze(2).to_broadcast([128, H, HP]), op=ALU.mult))
        for b4 in range(4):
            sy.op("dve", [B_hst, B_PS[b4]], [B_hst], lambda e, b4=b4: e.tensor_tensor(
                out=hst[:, b4 * 512:(b4 + 1) * 512], in0=hst[:, b4 * 512:(b4 + 1) * 512], in1=PS[b4][:, :], op=ALU.add))
        sy.op("act", [B_hst], [B_hbf], lambda e: e.copy(out=hbf[:], in_=hst[:]))

    def ssd_samples(c0, vidx):
        nt = NSTOK
        ssd_prelude(c0, nt, vidx, True)
        psy_all = [PS[0], PS[1]]
        for bk in range(2):
            sy.op("dve", [], [B_PS[bk]], lambda e, bk=bk: e.memset(PS[bk][:, :], 0.0))
        for g in range(NG):
            MT_, BMT, CE_, BCE = ssd_group_common(g, c0, nt, True)
            sy.op("act", [BCE], [B_CEall], lambda e: e.copy(out=CEall[:, g * 4 * nt:(g + 1) * 4 * nt], in_=CE_[:, 0:4 * nt]))

            def mmy(e):
                for i in range(4):
                    hd = 4 * g + i
                    pr = hd // 2
                    hf = hd % 2
                    o = PS[pr // 8][hf * 64:(hf + 1) * 64, (pr % 8) * 64:(pr % 8) * 64 + nt]
                    ins = e.matmul(o, lhsT=xdt[0:nt, hd * 64:(hd + 1) * 64], rhs=MT_[0:nt, i * nt:(i + 1) * nt], start=False, stop=False,
                                   skip_group_check=True)
                return ins
            sy.op("pe", [B_xdt, BMT], [B_PS[0], B_PS[1]], mmy)
        R_, BR = Eg[0]
        so0 = CST["sel"][0]
        sy.op("dve", [B_dta, B_cst], [BR], lambda e: e.tensor_tensor(
            out=R_[0:nt, 0:NSEQ * 32].rearrange("p (s h) -> p s h", s=NSEQ),
            in0=cst[0:nt, so0:so0 + NSEQ].unsqueeze(2).to_broadcast([nt, NSEQ, 32]),
            in1=dta[0:nt, 0:32].unsqueeze(1).to_broadcast([nt, NSEQ, 32]), op=ALU.mult))
        sy.op("pe", [BR, B_cst], [B_PS[4]], lambda e: e.matmul(PS[4][:, 0:NSEQ * 32], lhsT=ones_f[0:nt, :], rhs=R_[0:nt, 0:NSEQ * 32], start=True, stop=True))
        sy.op("act", [B_PS[4]], [B_decS], lambda e: e.activation(out=decS[:], in_=PS[4][:, 0:NSEQ * 32], func=AF.Exp))
        hbufs = [(h0s, B_h0s), (h0s2[:, :], B_h0s2)]
        sy.dma("sp", hbufs[0][0], ssm_d[0], [], [hbufs[0][1]])
        for s in range(NSEQ):
            hs, Bhs = hbufs[s % 2]
            if s + 1 < NSEQ:
                sy.dma("sp", hbufs[(s + 1) % 2][0], ssm_d[s + 1], [], [hbufs[(s + 1) % 2][1]])
            sy.op("act", [Bhs], [B_h0b], lambda e: e.copy(out=h0b[:], in_=hs[:]))

            def mmi(e):
                for hd in range(H):
                    pr = hd // 2
                    hf = hd % 2
                    o = PS[pr // 8][hf * 64:(hf + 1) * 64, (pr % 8) * 64 + 4 * s:(pr % 8) * 64 + 4 * s + 4]
                    ins = e.matmul(o, lhsT=h0b[:, hd * 64:(hd + 1) * 64], rhs=CEall[:, hd * nt + 4 * s:hd * nt + 4 * s + 4],
                                   start=False, stop=(s == NSEQ - 1), skip_group_check=True)
                return ins
            sy.op("pe", [B_h0b, B_CEall], [B_PS[0], B_PS[1]], mmi)
            so = CST["sel"][0] + s
            sy.op("dve", [B_btok, B_cst], [B_bmsk], lambda e: e.tensor_scalar_mul(out=bmsk[0:nt, :], in0=btok[0:nt, :], scalar1=cst[0:nt, so:so + 1]))
            for b4 in range(2):
                def mms(e, b4=b4):
                    for gg in range(2):
                        g = 2 * b4 + gg
                        ins = e.matmul(PS[2 + b4][:, gg * 256:(gg + 1) * 256], lhsT=bmsk[0:nt, g * 128:(g + 1) * 128],
                                       rhs=xdtp[0:nt, g * 256:(g + 1) * 256], start=True, stop=True)
                    return ins
                sy.op("pe", [B_bmsk, B_xdtp], [B_PS[2 + b4]], mms)
            for b4 in range(2):
                def mms2(e, b4=b4):
                    for gg in range(2):
                        g = 4 + 2 * b4 + gg
                        ins = e.matmul(PS[6 + b4][:, gg * 256:(gg + 1) * 256], lhsT=bmsk[0:nt, g * 128:(g + 1) * 128],
                                       rhs=xdtp[0:nt, g * 256:(g + 1) * 256], start=True, stop=True)
                    return ins
                sy.op("pe", [B_bmsk, B_xdtp], [B_PS[6 + b4]], mms2)
            sy.op("dve", [Bhs, B_decS], [Bhs], lambda e: e.tensor_tensor(
                out=hs[:].rearrange("p (h x) -> p h x", h=H), in0=hs[:].rearrange("p (h x) -> p h x", h=H),
                in1=decS[:, s * 32:(s + 1) * 32].unsqueeze(2).to_broadcast([128, H, HP]), op=ALU.mult))
            for b4, bank in enumerate((2, 3, 6, 7)):
                sy.op("dve", [Bhs, B_PS[bank]], [Bhs], lambda e, b4=b4, bank=bank: e.tensor_tensor(
                    out=hs[:, b4 * 512:(b4 + 1) * 512], in0=hs[:, b4 * 512:(b4 + 1) * 512], in1=PS[bank][:, :], op=ALU.add))
            sy.dma("sp", osssm_o[s], hs[:], [Bhs], [])
            cache_tick(2.0)
        for g in range(NG):
            psy = PS[g // 4][:, (g % 4) * 128:(g % 4) * 128 + 128].rearrange("p (c t) -> p c t", c=2)
            gate_group(g, c0, nt, psy, B_PS[g // 4], g == 0, g == NG - 1, 2)
        sy.op("act", [B_PS[2]], [B_rsy], lambda e: e.activation(out=rsy[:, c0:c0 + nt], in_=PS[2][:, 0:nt], func=AF.Sqrt, bias=EPS, scale=1.0 / DB))
        sy.op("dve", [B_rsy], [B_rsy], lambda e: e.reciprocal(out=rsy[:, c0:c0 + nt], in_=rsy[:, c0:c0 + nt]))

    def s6_merge(n):
        for j in range(KC):
            wa, Bwa = wload("w6a", w6a_d, j, 24 * 128)
            wb, Bwb = wload("w6b", w6b_d, j, 32 * 128)
            bs = 4 * (j % 2)
            proj_group(PS[bs], B_PS[bs], wa, Bwa, 0, NJA, 128, 0, uA, [B_uA], n)
            proj_group(PS[bs + 1], B_PS[bs + 1], wa, Bwa, NJA, KC, 128, 0, gyT, [B_gy], n)
            proj_group(PS[bs + 2], B_PS[bs + 2], wb, Bwb, 0, KC, 128, 0, hT, [B_hT], n)
            proj_group(PS[bs + 3], B_PS[bs + 3], wb, Bwb, KC, KC, 128, 0, hT, [B_hT], n)
            rd3 = [B_PS[bs + 3]]
            sa, Bsa = rot(tmpA, "tmpA")
            sb_, Bsb = rot(tmpB, "tmpB")
            t1, Bt1 = rot(tmpC, "tmpC")
            t2, Bt2 = rot(tmpD, "tmpD")
            sy.op("act", [B_PS[bs + 2], B_cst], [Bsa], lambda e: e.activation(out=sa[:, 0:n], in_=PS[bs + 2][:, 0:n], func=AF.Sigmoid, bias=C("b_gate", j, j + 1), scale=1.0))
            sy.op("act", rd3 + [B_cst], [Bsb], lambda e: e.activation(out=sb_[:, 0:n], in_=PS[bs + 3][:, 0:n], func=AF.Sigmoid, bias=C("b_gate", 16 + j, 17 + j), scale=1.0))
            sy.op("dve", [B_PS[bs], Bsa, B_cst], [Bt1], lambda e: e.scalar_tensor_tensor(
                out=t1[:, 0:n], in0=PS[bs][:, 0:n], scalar=C("b_a_out", j, j + 1), in1=sa[:, 0:n], op0=ALU.add, op1=ALU.mult))
            sy.big = True
            sy.op("dve", [B_PS[bs + 1], B_rsy], [Bt2], lambda e: e.tensor_tensor(out=t2[:, 0:n], in0=PS[bs + 1][:, 0:n], in1=rsy[:, 0:n], op=ALU.mult))
            sy.op("dve", [Bt2, Bsb], [Bt2], lambda e: e.tensor_tensor(out=t2[:, 0:n], in0=t2[:, 0:n], in1=sb_[:, 0:n], op=ALU.mult))
            sy.op("dve", [Bt1, Bt2], [B_min], lambda e: e.tensor_tensor(out=minT[:, j, 0:n], in0=t1[:, 0:n], in1=t2[:, 0:n], op=ALU.add))
            sy.big = False

    def s7_out_res(n):
        for j in range(KC):
            if j % 2 == 0:
                w, Bw = wload("wO", wO_d, j // 2, KC * 256)
            ps, Bps = PS[j % 2], B_PS[j % 2]
            proj_group(ps, Bps, w, Bw, 0, KC, 256, 128 * (j % 2), minT, [B_min], n)
            sq, Bsq = rot(sqb, "sqb")
            sy.op("act", [Bps], [B_R2], lambda e: e.copy(out=mT[:, j, 0:n], in_=ps[:, 0:n]))
            sy.op("act", [Bps], [Bsq], lambda e: e.activation(out=sq[:, 0:n], in_=ps[:, 0:n], func=AF.Square))
            sy.op("pe", [Bsq, B_onesb], [B_PS[6]], lambda e: e.matmul(PS[6][:, 0:n], lhsT=onesb[:], rhs=sq[:, 0:n], start=(j == 0), stop=(j == KC - 1)))
        rstd_from(PS[6][:, 0:n], B_PS[6], n, rs1, B_rs1, 1.0 / D)
        for j in range(KC):
            t1, Bt1 = rot(tmpC, "tmpC")
            sy.big = True
            sy.op("dve", [B_R2, B_rs1, B_cst], [Bt1], lambda e: e.scalar_tensor_tensor(
                out=t1[:, 0:n], in0=mT[:, j, 0:n], scalar=C("g_post1", j, j + 1), in1=rs1[:, 0:n], op0=ALU.mult, op1=ALU.mult))
            sy.op("dve", [Bt1, B_xT], [B_xT], lambda e: e.tensor_tensor(out=xT[:, j, 0:n], in0=xT[:, j, 0:n], in1=t1[:, 0:n], op=ALU.add))
            sy.big = False
        sy.op("act", [B_xT], [B_sqT], lambda e: e.activation(out=sqT[:, :, 0:n], in_=xT[:, :, 0:n], func=AF.Square))

        def mm(e):
            for kc in range(KC):
                ins = e.matmul(PS[7][:, 0:n], lhsT=onesb[:], rhs=sqT[:, kc, 0:n], start=(kc == 0), stop=(kc == KC - 1))
            return ins
        sy.op("pe", [B_sqT, B_onesb], [B_PS[7]], mm)
        rstd_from(PS[7][:, 0:n], B_PS[7], n, rs2, B_rs2, 1.0 / D)
        for kc in range(KC):
            sy.op("dve", [B_xT, B_rs2, B_cst], [B_hT], lambda e, kc=kc: e.scalar_tensor_tensor(
                out=hT[:, kc, 0:n], in0=xT[:, kc, 0:n], scalar=C("g_pre2", kc, kc + 1), in1=rs2[:, 0:n], op0=ALU.mult, op1=ALU.mult))

    def s8_ffn_up(np_, ns, need_f):
        n = np_ + ns
        W3 = CST["w_dw_f"][0]
        if ns:
            sy.dma("sp", scf_s, scf_d, [], [B_R2])
        for j in range(NF):
            w, Bw = wload("wU", wU_d, j, KC * 256)
            bs = 2 * (j % 2)
            proj_group(PS[bs], B_PS[bs], w, Bw, 0, KC, 256, 0, hT, [B_hT], n)
            proj_group(PS[bs + 1], B_PS[bs + 1], w, Bw, 0, KC, 256, 128, hT, [B_hT], n)
            accs = []
            for half, (exl, tl) in enumerate(((ext1, tmpA), (ext2, tmpB))):
                cc = j + NF * half
                ps, Bps = PS[bs + half], B_PS[bs + half]
                ex, Bex = rot(exl, "e" + str(half))
                ac, Bac = rot(tl, "t" + str(half))
                sy.op("act", [Bps], [Bex], lambda e, ex=ex, ps=ps: e.copy(out=ex[:, 2:2 + np_], in_=ps[:, 0:np_]))
                sy.op("dve", [B_hF], [Bex], lambda e, ex=ex, cc=cc: e.tensor_copy(out=ex[:, 0:2], in_=haloF[:, cc, :]))
                sy.op("dve", [Bex], [B_hF], lambda e, ex=ex, cc=cc: e.tensor_copy(out=haloF[:, cc, :], in_=ex[:, np_:np_ + 2]))
                if need_f:
                    sy.op("act", [Bps, B_cst], [Bac], lambda e, ac=ac, ps=ps, cc=cc: e.activation(
                        out=ac[:, 0:np_], in_=ps[:, 0:np_], func=AF.Identity, bias=C("b_dw_f", cc, cc + 1),
                        scale=cst[:, W3 + cc * CF + 2:W3 + cc * CF + 3]))
                    sy.big = True
                    for k in range(CF - 1):
                        wk = cst[:, W3 + cc * CF + k:W3 + cc * CF + k + 1]
                        sy.op("dve", [Bex, B_cst], [Bac], lambda e, ex=ex, ac=ac, wk=wk, k=k: e.scalar_tensor_tensor(
                            out=ac[:, 0:np_], in0=ex[:, k:k + np_], scalar=wk, in1=ac[:, 0:np_], op0=ALU.mult, op1=ALU.add))
                    sy.big = False
                if ns:
                    exs, Bexs = rot(tmpC if half == 0 else tmpD, "es" + str(half))
                    exv = exs[:, 0:NSEQ * 6].rearrange("p (s t) -> p s t", s=NSEQ)
                    acv = ac[:, np_:n].rearrange("p (s t) -> p s t", s=NSEQ)
                    sy.op("dve", [B_R2], [Bexs], lambda e, exv=exv, cc=cc: e.tensor_copy(out=exv[:, :, 0:2], in_=scf_s[:, cc, :, :]))
                    sy.op("act", [Bps], [Bexs], lambda e, exv=exv, ps=ps: e.copy(out=exv[:, :, 2:6], in_=ps[:, np_:n].rearrange("p (s t) -> p s t", s=NSEQ)))
                    sy.op("dve", [Bexs], [B_R2], lambda e, exv=exv, cc=cc: e.tensor_copy(out=ocf_s[:, cc, :, :], in_=exv[:, :, 4:6]))
                    for k in range(CF):
                        wk = cst[:, W3 + cc * CF + k:W3 + cc * CF + k + 1]
                        if k == 0:
                            sy.op("dve", [Bexs, B_cst], [Bac], lambda e, exv=exv, acv=acv, wk=wk, cc=cc: e.tensor_scalar(
                                out=acv, in0=exv[:, :, 0:4], scalar1=wk, scalar2=C("b_dw_f", cc, cc + 1), op0=ALU.mult, op1=ALU.add))
                        else:
                            sy.op("dve", [Bexs, B_cst], [Bac], lambda e, exv=exv, acv=acv, wk=wk, k=k: e.scalar_tensor_tensor(
                                out=acv, in0=exv[:, :, k:k + 4], scalar=wk, in1=acv, op0=ALU.mult, op1=ALU.add))
                accs.append((ac, Bac))
            if need_f:
                (gc, Bgc), (vc, Bvc) = accs
                sy.op("act", [Bgc], [Bgc], lambda e: e.activation(out=gc[:, 0:n], in_=gc[:, 0:n], func=AF.Gelu_apprx_tanh))
                sy.op("dve", [Bgc, Bvc], [B_R1[0], B_R1[1], B_R1[2]], lambda e: e.tensor_tensor(out=fT[:, j, 0:n], in0=gc[:, 0:n], in1=vc[:, 0:n], op=ALU.mult))
        if ns:
            sy.dma("sp", oscf_o, ocf_s, [B_R2], [])

    def s9_ffn_down(n, out_cols, skip_cols):
        fo = mT
        for i in range(KC):
            ps, Bps = PS[i % 2], B_PS[i % 2]
            w0, Bw0 = wload("wD", wD_d, 2 * i, 22 * 128)
            w1, Bw1 = wload("wD", wD_d, 2 * i + 1, 22 * 128)

            def mm(e):
                for k in range(NF):
                    w = w0 if k < 22 else w1
                    o = (k % 22) * 128
                    ins = e.matmul(ps[:, 0:n], lhsT=w[:, o:o + 128], rhs=fT[:, k, 0:n], start=(k == 0), stop=(k == NF - 1))
                return ins
            sy.op("pe", [Bw0, Bw1, B_R1[0], B_R1[1], B_R1[2]], [Bps], mm)
            sq, Bsq = rot(sqb, "sqb")
            sy.op("act", [Bps], [B_R2], lambda e: e.copy(out=fo[:, i, 0:n], in_=ps[:, 0:n]))
            sy.op("act", [Bps], [Bsq], lambda e: e.activation(out=sq[:, 0:n], in_=ps[:, 0:n], func=AF.Square))
            sy.op("pe", [Bsq, B_onesb], [B_PS[6]], lambda e: e.matmul(PS[6][:, 0:n], lhsT=onesb[:], rhs=sq[:, 0:n], start=(i == 0), stop=(i == KC - 1)))
        rstd_from(PS[6][:, 0:n], B_PS[6], n, rs1, B_rs1, 1.0 / D)
        sy.big = True
        for i in range(KC):
            sy.op("dve", [B_R2, B_rs1, B_cst], [B_R2], lambda e: e.scalar_tensor_tensor(
                out=fo[:, i, 0:n], in0=fo[:, i, 0:n], scalar=C("g_post2", i, i + 1), in1=rs1[:, 0:n], op0=ALU.mult, op1=ALU.mult))
            sy.op("dve", [B_R2, B_xT], [B_R2], lambda e: e.tensor_tensor(out=fo[:, i, 0:n], in0=fo[:, i, 0:n], in1=xT[:, i, 0:n], op=ALU.add))
        sy.big = False
        sy.dma("sp", yT_o[:, :, out_cols:out_cols + n - skip_cols], fo[:, :, skip_cols:n], [B_R2], [])

    for st in range(NPRE // TSP):
        n = TSP
        s1_load_norm(C_PRE + st * TSP, n)
        last = st == NPRE // TSP - 1
        ssd_dt_all([(t * 128, st * 2 + t) for t in range(n // 128)])
        s3_xbc(n, 0, 32 if st in (0, NPRE // TSP - 1) else 24)
        for t in range(n // 128):
            ssd_tile(t * 128, st * 2 + t, False, None, 32 * t)
        if last:
            halo_a_from_prefix(n)
    sts = [(C_LEAD, 128, NSTOK)] + [(C_MAIN + k * TSP, TSP, 0) for k in range(NMAIN // TSP)]
    for si, (c0, np_, ns) in enumerate(sts):
        n = np_ + ns
        cache_on[0] = (si == 0)
        if ns:
            sy.dma("sp", xT[:, :, np_:n], xT_d[:, :, C_SMP:C_SMP + ns], [], [B_xT])
            sy.dma("sp", xT[:, :, 0:np_], xT_d[:, :, c0:c0 + np_], [], [B_xT])
            sy.op("act", [B_xT], [B_sqT], lambda e: e.activation(out=sqT[:, :, 0:n], in_=xT[:, :, 0:n], func=AF.Square))

            def mm(e):
                for kc in range(KC):
                    ins = e.matmul(PS[7][:, 0:n], lhsT=onesb[:], rhs=sqT[:, kc, 0:n], start=(kc == 0), stop=(kc == KC - 1))
                return ins
            sy.op("pe", [B_sqT, B_onesb], [B_PS[7]], mm)
            rstd_from(PS[7][:, 0:n], B_PS[7], n, rs1, B_rs1, 1.0 / D)
            for kc in range(KC):
                sy.op("dve", [B_xT, B_rs1, B_cst], [B_hT], lambda e, kc=kc: e.scalar_tensor_tensor(
                    out=hT[:, kc, 0:n], in0=xT[:, kc, 0:n], scalar=C("g_pre1", kc, kc + 1), in1=rs1[:, 0:n],
                    op0=ALU.mult, op1=ALU.mult))
        else:
            s1_load_norm(c0, n)
        ssd_dt_all([(t * 128, 8 + (0 if si == 0 else 1 + (si - 1) * 2 + t)) for t in range(np_ // 128)])
        steps = s234(np_, ns)
        for t in range(np_ // 128):
            vidx = 8 + (0 if si == 0 else 1 + (si - 1) * 2 + t)
            ssd_tile(t * 128, vidx, True, steps, 32 * t)
        while steps:
            steps.pop(0)()
        if ns:
            fence([B_R2, B_h0s, B_h0b, B_CEall, B_bmsk])
            ssd_samples(np_, 17)
            fence([B_h0s, B_h0b, B_CEall, B_bmsk, B_R2])
        s6_merge(n)
        s7_out_res(n)
        if si == 0:
            s8_ffn_up(np_, ns, True)
            s9_ffn_down(n, NMAIN, np_)
        else:
            s8_ffn_up(np_, ns, True)
            s9_ffn_down(n, (si - 1) * TSP, 0)
    sy.dma("sp", oca_o, haloA[:], [B_hA], [])
    sy.dma("sp", ocb_o, haloB[:], [B_hB], [])
    sy.dma("sp", ocf_o, haloF[:], [B_hF], [])
    sy.dma("sp", ossm_o, hst[:], [B_hst], [])
    sy.finish("sp", [B_hA, B_hB, B_hF, B_hst, B_R2, B_h0s, B_h0s2, B_R1[0], B_ocb])
    return nc


def _pk(v):
    v = np.asarray(v, np.float32)
    return np.ascontiguousarray(v.reshape(-1, 128).T)


def _wtile(w, cols):
    ws = w[:, cols]
    K = ws.shape[0]
    return np.ascontiguousarray(ws.reshape(K // 128, 128, len(cols)).transpose(1, 0, 2).reshape(128, -1))


def _const_masks():
    k = np.arange(128)[:, None]
    q = np.arange(128)[None, :]
    m = {}
    m["ones"] = np.ones((128, 128), np.float32)
    m["ident"] = np.eye(128, dtype=np.float32)
    m["tri_p"] = (k <= q).astype(np.float32)
    same = (k // 4 == q // 4)
    ts = np.zeros((128, 64), np.float32)
    ts[:64] = ((k <= q) & same)[:64, :64]
    m["tri_s"] = ts
    bs = np.zeros((128, 64), np.float32)
    bs[:64] = same[:64, :64]
    m["blk_s"] = bs
    mp = np.where(q < k, -30000.0, 0.0).astype(np.float32)
    m["mneg_p"] = np.tile(mp, (1, 4))
    ms = np.zeros((128, 64), np.float32)
    ms[:64] = np.where(((k <= q) & same)[:64, :64], 0.0, -30000.0)
    m["mneg_s"] = np.tile(ms, (1, 4))
    sel = np.zeros((128, NSEQ), np.float32)
    sel[np.arange(64), np.arange(64) // 4] = 1.0
    m["sel"] = sel
    return m


_NC_CACHE = {}


def kernel(x_prompt, x_sample, state_conv_a, state_conv_b, state_ssm, state_conv_ffn, meta_tokens,
           g_pre1, g_post1, w_in, b_gate, w_dw_a, b_dw_a, g_ln_a, b_ln_a, w_a_out, b_a_out,
           w_dw_b, b_dw_b, dt_bias, a_log, d_skip, g_norm_b, w_b_out, w_o,
           g_pre2, g_post2, w_up, w_dw_f, b_dw_f, w_down):
    f = lambda a: np.asarray(a, np.float32)
    x_prompt, x_sample, meta_tokens = f(x_prompt), f(x_sample), f(meta_tokens)
    w_in0, w_up0, w_down0 = f(w_in)[0], f(w_up)[0], f(w_down)[0]
    w_a0, w_b0, w_o0 = f(w_a_out)[0], f(w_b_out)[0], f(w_o)[0]
    oA, oZ, oX, oDT, oG = 0, 2 * DA, 2 * DA + DB, 2 * DA + DB + DXBC, 2 * DA + DB + DXBC + H
    ar = np.arange
    wA = np.stack([_wtile(w_in0, np.concatenate([oA + j * 128 + ar(128), oA + DA + j * 128 + ar(128)])) for j in range(NJA)])
    wX = np.stack([_wtile(w_in0, oX + c * 256 + ar(256)) for c in range(16)])
    wDT = _wtile(w_in0, oDT + ar(32))
    wZ = np.stack([_wtile(w_in0, oZ + c * 256 + ar(256)) for c in range(8)])
    w6a = np.stack([np.concatenate([_wtile(w_a0, j * 128 + ar(128)), _wtile(w_b0, j * 128 + ar(128))], axis=1) for j in range(KC)])
    w6b = np.stack([np.concatenate([_wtile(w_in0, oG + j * 128 + ar(128)), _wtile(w_in0, oG + D + j * 128 + ar(128))], axis=1) for j in range(KC)])
    wO = np.stack([_wtile(w_o0, c * 256 + ar(256)) for c in range(8)])
    wU = np.stack([_wtile(w_up0, np.concatenate([j * 128 + ar(128), DFF + j * 128 + ar(128)])) for j in range(NF)])
    wD = np.stack([_wtile(w_down0[hh * 2816:(hh + 1) * 2816], i * 128 + ar(128)) for i in range(KC) for hh in range(2)])
    cst = np.zeros((128, NCST), np.float32)

    def put(name, arr):
        o, n = CST[name]
        arr = np.asarray(arr, np.float32)
        assert arr.shape == (128, n), (name, arr.shape, n)
        cst[:, o:o + n] = arr
    put("g_pre1", _pk(f(g_pre1)[0])); put("g_post1", _pk(f(g_post1)[0]))
    put("g_pre2", _pk(f(g_pre2)[0])); put("g_post2", _pk(f(g_post2)[0]))
    put("g_norm_b", _pk(f(g_norm_b)[0])); put("b_a_out", _pk(f(b_a_out)[0]))
    put("dsk", _pk(np.repeat(f(d_skip)[0], HP)))
    put("b_gate", _pk(f(b_gate)[0]))
    put("w_dw_a", f(w_dw_a)[0].T.reshape(NJA, 128, CA).transpose(1, 0, 2).reshape(128, -1))
    put("b_dw_a", _pk(f(b_dw_a)[0])); put("g_ln_a", _pk(f(g_ln_a)[0])); put("b_ln_a", _pk(f(b_ln_a)[0]))
    put("w_dw_b", f(w_dw_b)[0].T.reshape(32, 128, CB).transpose(1, 0, 2).reshape(128, -1))
    put("b_dw_b", _pk(f(b_dw_b)[0]))
    put("w_dw_f", f(w_dw_f)[0].T.reshape(88, 128, CF).transpose(1, 0, 2).reshape(128, -1))
    put("b_dw_f", _pk(f(b_dw_f)[0]))
    put("dt_bias", np.tile(f(dt_bias)[0][None, :], (128, 1)))
    put("a_log", np.tile(f(a_log)[0][None, :], (128, 1)))
    for k_, v_ in _const_masks().items():
        put(k_, v_)
    in_maps = []
    chunk0 = np.zeros((128, D), np.float32)
    chunk0[128 - NMETA:] = meta_tokens
    v0 = np.zeros(128, np.float32)
    v0[128 - NMETA:] = 1.0
    for c in range(8):
        b, half = c // 2, c % 2
        xs_tok = np.zeros((NCOL, D), np.float32)
        valid = np.zeros((128, 18), np.float32)
        if half == 1:
            xs_tok[0:128] = chunk0
            xs_tok[128:1024] = x_prompt[b, 0:896]
            xs_tok[C_LEAD:C_LEAD + 128] = x_prompt[b, 896:1024]
            valid[:, 0] = v0
            valid[:, 1:8] = 1.0
            valid[:, 8] = 1.0
        else:
            xs_tok[C_LEAD:C_LEAD + 128] = chunk0
            valid[:, 8] = v0
        xs_tok[C_MAIN:C_MAIN + NMAIN] = x_prompt[b, half * 1024:(half + 1) * 1024]
        valid[:, 9:17] = 1.0
        xs_tok[C_SMP:] = x_sample[c * NSEQ:(c + 1) * NSEQ].reshape(NSTOK, D)
        valid[:, 17] = 1.0
        cc = cst.copy()
        o, n = CST["valid"]
        cc[:, o:o + n] = valid
        xT = np.ascontiguousarray(xs_tok.T.reshape(KC, 128, NCOL).transpose(1, 0, 2))
        sl = slice(c * NSEQ, (c + 1) * NSEQ)
        sca = np.ascontiguousarray(f(state_conv_a)[0, sl].transpose(2, 0, 1).reshape(NJA, 128, NSEQ, CA - 1).transpose(1, 0, 2, 3))
        scb = np.ascontiguousarray(f(state_conv_b)[0, sl].transpose(2, 0, 1).reshape(32, 128, NSEQ, CB - 1).transpose(1, 0, 2, 3))
        scf = np.ascontiguousarray(f(state_conv_ffn)[0, sl].transpose(2, 0, 1).reshape(88, 128, NSEQ, CF - 1).transpose(1, 0, 2, 3))
        ssm = np.ascontiguousarray(f(state_ssm)[0, sl].reshape(NSEQ, DB, NS).transpose(0, 2, 1))
        in_maps.append({"xT": xT, "cst": cc, "wA": wA, "wX": wX, "wDT": wDT, "wZ": wZ, "w6a": w6a, "w6b": w6b,
                        "wO": wO, "wU": wU, "wD": wD, "sca": sca, "scb": scb, "scf": scf, "ssm": ssm})
    if "nc" not in _NC_CACHE:
        _NC_CACHE["nc"] = build_program()
    res = run_bass_kernel_spmd(_NC_CACHE["nc"], in_maps, core_ids=list(range(8)))
    R = res.results
    y_prompt = np.zeros((4, 2048, D), np.float32)
    y_sample = np.zeros((128, 4, D), np.float32)
    ca_p = np.zeros((1, 4, CA - 1, DA), np.float32)
    cb_p = np.zeros((1, 4, CB - 1, DXBC), np.float32)
    h_p = np.zeros((1, 4, H, HP, NS), np.float32)
    cf_p = np.zeros((1, 4, CF - 1, 2 * DFF), np.float32)
    ca_s = np.zeros((1, 128, CA - 1, DA), np.float32)
    cb_s = np.zeros((1, 128, CB - 1, DXBC), np.float32)
    h_s = np.zeros((1, 128, H, HP, NS), np.float32)
    cf_s = np.zeros((1, 128, CF - 1, 2 * DFF), np.float32)

    def unT(a):
        return a.transpose(2, 1, 0).reshape(a.shape[2], -1)
    for c in range(8):
        b, half = c // 2, c % 2
        r = R[c]
        yT = r["yT"]
        y_prompt[b, half * 1024:(half + 1) * 1024] = unT(yT[:, :, 0:NMAIN])
        y_sample[c * NSEQ:(c + 1) * NSEQ] = unT(yT[:, :, NMAIN:]).reshape(NSEQ, 4, D)
        if half == 1:
            ca_p[0, b] = unT(r["o_ca"])
            cb_p[0, b] = unT(r["o_cb"])
            cf_p[0, b] = unT(r["o_cf"])
            h_p[0, b] = r["o_ssm"].T.reshape(H, HP, NS)
        sl = slice(c * NSEQ, (c + 1) * NSEQ)
        ca_s[0, sl] = r["o_sca"].transpose(2, 3, 1, 0).reshape(NSEQ, CA - 1, DA)
        cb_s[0, sl] = r["o_scb"].transpose(2, 3, 1, 0).reshape(NSEQ, CB - 1, DXBC)
        cf_s[0, sl] = r["o_scf"].transpose(2, 3, 1, 0).reshape(NSEQ, CF - 1, 2 * DFF)
        h_s[0, sl] = r["o_sssm"].transpose(0, 2, 1).reshape(NSEQ, H, HP, NS)
    return (y_prompt, y_sample, ca_p, cb_p, h_p, cf_p, ca_s, cb_s, h_s, cf_s)
```

```python
import numpy as np
import concourse.bass as bass
import concourse.mybir as mybir
from concourse.bass_utils import run_bass_kernel_spmd

F32 = mybir.dt.float32
BF16 = mybir.dt.bfloat16
ALU = mybir.AluOpType
AF = mybir.ActivationFunctionType

D = 2048; KC = 16
DA = 1024; NJA = 8; CA = 31
DB = 2048; H = 32; HP = 64; NG = 8; NS = 128
DXBC = 4096; CB = 4
DFF = 5632; NF = 44; CF = 3
NMETA = 16
EPS = 1e-6
NSEQ = 16
NSTOK = 64
TSP = 256
TSMAX = 256
NPRE = 1024
NMAIN = 1024
C_PRE = 0
C_LEAD = NPRE
C_MAIN = NPRE + 128
C_SMP = C_MAIN + NMAIN
NCOL = C_SMP + NSTOK
WSLOT = 4096
NWSLOT = 4


class Buf:
    __slots__ = ("name", "w", "r", "dsem", "dval")

    def __init__(self, name):
        self.name = name
        self.w = None
        self.r = {}
        self.dsem = None
        self.dval = 0


class Sync:
    def __init__(self, nc, same_engine_waits=True):
        self.nc = nc
        self.eng = {"pe": nc.tensor, "act": nc.scalar, "dve": nc.vector, "pool": nc.gpsimd, "sp": nc.sync}
        self.sem = {k: nc.alloc_semaphore("s_" + k) for k in ("pe", "act", "dve", "pool")}
        self.tick = {k: 0 for k in self.sem}
        self.seen = {k: {} for k in self.eng}
        self.same = same_engine_waits
        self.nins = 0
        self.big = False

    def _waits(self, e, reads, writes, extra=()):
        need = {}
        deps = list(extra)
        for b in reads:
            if b.w is not None:
                deps.append(b.w)
        for b in writes:
            if b.w is not None:
                deps.append(b.w)
            deps.extend(d for d in b.r.values() if not (d[0] == "eng" and d[1] == e))
        for d in deps:
            if d[0] == "eng":
                key, sem, val = d[1], self.sem[d[1]], d[2]
                if key == e and (e == "pe" or not self.same or d[3]):
                    continue
            else:
                key, sem, val = d[3], d[1], d[2]
            if self.seen[e].get(key, 0) >= val:
                continue
            if key not in need or need[key][1] < val:
                need[key] = (sem, val)
        for key, (sem, val) in need.items():
            self.eng[e].wait_ge(sem, val)
            self.seen[e][key] = val

    def op(self, e, reads, writes, fn):
        self._waits(e, reads, writes)
        ins = fn(self.eng[e])
        self.nins += 1
        self.tick[e] += 1
        T = self.tick[e]
        ins.then_inc(self.sem[e], 1)
        for b in writes:
            b.w = ("eng", e, T, self.big)
            b.r = {}
        for b in reads:
            if b not in writes:
                b.r[e] = ("eng", e, T, False)
        return ins

    def dma(self, q, out, in_, reads, writes, sembuf=None, extra=()):
        self._waits(q, reads, writes, extra)
        sb = sembuf or (writes[0] if writes else reads[0])
        if sb.dsem is None:
            sb.dsem = self.nc.alloc_semaphore("d_" + sb.name)
        ins = self.eng[q].dma_start(out=out, in_=in_)
        sb.dval += 16
        ins.then_inc(sb.dsem, 16)
        dep = ("dma", sb.dsem, sb.dval, "d_" + sb.name)
        for b in writes:
            b.w = dep
            b.r = {}
        for b in reads:
            if b not in writes:
                b.r["d_" + sb.name] = dep
        return ins

    def finish(self, e, bufs):
        self._waits(e, [], bufs)


def _cst_layout():
    off = {}
    cur = 0

    def add(name, n):
        nonlocal cur
        off[name] = (cur, n)
        cur += n

    for nm in ("g_pre1", "g_post1", "g_pre2", "g_post2", "g_norm_b", "b_a_out", "dsk"):
        add(nm, 16)
    add("b_gate", 32)
    add("w_dw_a", NJA * CA)
    for nm in ("b_dw_a", "g_ln_a", "b_ln_a"):
        add(nm, NJA)
    add("w_dw_b", 32 * CB)
    add("b_dw_b", 32)
    add("w_dw_f", 88 * CF)
    add("b_dw_f", 88)
    add("dt_bias", 32)
    add("a_log", 32)
    add("valid", 18)
    add("ones", 128)
    add("ident", 128)
    add("tri_p", 128)
    add("tri_s", 64)
    add("blk_s", 64)
    add("mneg_p", 512)
    add("mneg_s", 256)
    add("sel", NSEQ)
    return off, cur


CST, NCST = _cst_layout()


def build_program():
    nc = bass.Bass("TRN2", target_bir_lowering=False)
    sy = Sync(nc)

    def din(name, shape, dt=F32):
        return nc.dram_tensor(name, list(shape), dt, kind="ExternalInput").ap()

    def dout(name, shape):
        return nc.dram_tensor(name, list(shape), F32, kind="ExternalOutput").ap()

    xT_d = din("xT", [128, KC, NCOL])
    cst_d = din("cst", [128, NCST])
    wA_d = din("wA", [NJA, 128, KC * 256])
    wX_d = din("wX", [16, 128, KC * 256])
    wDT_d = din("wDT", [128, KC * 32])
    wZ_d = din("wZ", [8, 128, KC * 256])
    w6a_d = din("w6a", [16, 128, 24 * 128])
    w6b_d = din("w6b", [16, 128, 32 * 128])
    wO_d = din("wO", [8, 128, KC * 256])
    wU_d = din("wU", [NF, 128, KC * 256])
    wD_d = din("wD", [32, 128, 22 * 128])
    sca_d = din("sca", [128, NJA, NSEQ, CA - 1])
    scb_d = din("scb", [128, 32, NSEQ, CB - 1])
    scf_d = din("scf", [128, 88, NSEQ, CF - 1])
    ssm_d = din("ssm", [NSEQ, 128, DB])

    yT_o = dout("yT", [128, KC, NMAIN + NSTOK])
    oca_o = dout("o_ca", [128, NJA, CA - 1])
    ocb_o = dout("o_cb", [128, 32, CB - 1])
    ocf_o = dout("o_cf", [128, 88, CF - 1])
    ossm_o = dout("o_ssm", [128, DB])
    osca_o = dout("o_sca", [128, NJA, NSEQ, CA - 1])
    oscb_o = dout("o_scb", [128, 32, NSEQ, CB - 1])
    oscf_o = dout("o_scf", [128, 88, NSEQ, CF - 1])
    osssm_o = dout("o_sssm", [NSEQ, 128, DB])

    def sb(name, shape, dt=F32):
        return nc.alloc_sbuf_tensor(name, list(shape), dt)

    TS = TSMAX
    cst = sb("cst_s", [128, NCST]); B_cst = Buf("cst")
    xT = sb("xTs", [128, KC, TS]); B_xT = Buf("xT")
    hT = sb("hT", [128, KC, TS], BF16); B_hT = Buf("hT")
    uA = sb("uA", [128, NJA, TS], BF16); B_uA = Buf("uA")
    R1 = sb("R1", [128, 12288], BF16); B_R1 = [Buf("R1_xs"), Buf("R1_B"), Buf("R1_C")]
    R2 = sb("R2", [128, 5632]); B_R2 = Buf("R2")
    ZG = sb("ZG", [128, 2 * KC * TS], BF16)
    zs = ZG[:, 0:KC * TS].rearrange("p (j t) -> p j t", j=KC); B_zs = Buf("zs")
    gyT = ZG[:, KC * TS:2 * KC * TS].rearrange("p (j t) -> p j t", j=KC); B_gy = Buf("gyT")
    ZGf = ZG[:, :].bitcast(F32).rearrange("p (j t) -> p j t", j=KC)
    minT = zs; B_min = B_zs
    wsl = [sb(f"w{i}", [128, WSLOT], BF16) for i in range(NWSLOT)]
    B_w = [Buf(f"w{i}") for i in range(NWSLOT)]
    wdt = sb("wdt", [128, KC, 32], BF16); B_wdt = Buf("wdt")
    haloA = sb("haloA", [128, NJA, CA - 1]); B_hA = Buf("haloA")
    haloB = sb("haloB", [128, 32, CB - 1]); B_hB = Buf("haloB")
    haloF = sb("haloF", [128, 88, CF - 1]); B_hF = Buf("haloF")
    xsT = R1[:, 0:8192].bitcast(F32).rearrange("p (j t) -> p j t", j=KC)
    BT = R1[:, 8192:10240].rearrange("p (j t) -> p j t", j=NG)
    CT = R1[:, 10240:12288].rearrange("p (j t) -> p j t", j=NG)
    fT = R1[:, 0:NF * TS].rearrange("p (j t) -> p j t", j=NF)
    uexts = R1[:, 0:2 * NJA * NSEQ * 34].bitcast(F32).rearrange("p (j s t) -> p j s t", j=NJA, s=NSEQ)
    uext = R2[:, 0:NJA * (CA - 1 + TS)].rearrange("p (j t) -> p j t", j=NJA)
    cvA = R2[:, NJA * (CA - 1 + TS):NJA * (CA - 1 + TS) + NJA * TS].rearrange("p (j t) -> p j t", j=NJA)
    mT = R2[:, 0:KC * TS].rearrange("p (j t) -> p j t", j=KC)
    scb_s = R2[:, 0:1536].rearrange("p (j s t) -> p j s t", j=32, s=NSEQ); B_scb = B_R2
    ocb_s = R2[:, 1536:3072].rearrange("p (j s t) -> p j s t", j=32, s=NSEQ); B_ocb = B_R2
    h0b = R2[:, 0:1024].bitcast(BF16); B_h0b = Buf("h0b")
    CEall = R2[:, 1024:2048].bitcast(BF16); B_CEall = Buf("CEall")
    bmsk = R2[:, 2048:2560].bitcast(BF16); B_bmsk = Buf("bmsk")
    h0s = R2[:, 3072:5120]; B_h0s = Buf("h0s")
    scf_s = R2[:, 0:2816].rearrange("p (j s t) -> p j s t", j=88, s=NSEQ)
    ocf_s = R2[:, 2816:5632].rearrange("p (j s t) -> p j s t", j=88, s=NSEQ)

    def wt(name, n, dt=F32):
        return sb(name, [128, n], dt), Buf(name)

    rs1, B_rs1 = wt("rs1", TS)
    rs2, B_rs2 = wt("rs2", TS)
    rsy, B_rsy = wt("rsy", TS)
    t_mean, B_mean = wt("t_mean", TS)
    t_nmr, B_nmr = wt("t_nmr", TS)
    tmpA = [wt(f"tmpA{i}", TS) for i in range(2)]
    tmpB = [wt(f"tmpB{i}", TS) for i in range(2)]
    tmpC = [wt(f"tmpC{i}", TS) for i in range(2)]
    tmpD = [wt(f"tmpD{i}", TS) for i in range(2)]
    ext1 = [wt(f"ext1_{i}", TS + 4) for i in range(2)]
    ext2 = [wt(f"ext2_{i}", TS + 4) for i in range(2)]
    sqb = [wt(f"sqb{i}", TS, BF16) for i in range(2)]
    sqT = minT
    B_sqT = B_min
    dtx, B_dtx = wt("dtx", 64)
    dtl, B_dtl = wt("dtl", 64)
    dte, B_dte = wt("dte", 64)
    dta, B_dta = wt("dta", 64)
    cum, B_cum = wt("cum", 64)
    ncum, B_ncum = wt("ncum", 64)
    tend, B_tend = wt("tend", 64)
    dtte, B_dtte = wt("dtte", 64)
    decB, B_decB = wt("decB", 64)
    xdt, B_xdt = wt("xdt", DB, BF16)
    h0s2, B_h0s2 = wt("h0s2", DB)
    xdtp, B_xdtp = wt("xdtp", DB, BF16)
    btok, B_btok = wt("btok", NG * 128, BF16)
    hst, B_hst = wt("hst", DB)
    hbf, B_hbf = wt("hbf", DB, BF16)
    Rh = [wt(f"Rh{i}", 512, BF16) for i in range(2)]
    Rl = [wt(f"Rl{i}", 512, BF16) for i in range(2)]
    dth, B_dth = wt("dth", 64, BF16)
    dtl2, B_dtl2 = wt("dtl2", 64, BF16)
    mnegp_b, B_mnp = wt("mnegp_b", 512, BF16)
    mnegs_b, B_mns = wt("mnegs_b", 256, BF16)
    trip_b, B_trp = wt("trip_b", 128, BF16)
    tris_b, B_trs = wt("tris_b", 64, BF16)
    Eg = [wt(f"Eg{i}", 512) for i in range(2)]
    Xg = [wt(f"Xg{i}", 512) for i in range(2)]
    decS, B_decS = Xg[1]
    MTg = [wt(f"MTg{i}", 512, BF16) for i in range(2)]
    CEg = [wt(f"CEg{i}", 512, BF16) for i in range(2)]
    v1g = [wt("v1g0", 256)] * 2
    vg = [wt(f"vg{i}", 256) for i in range(2)]
    sqg = [wt(f"sqg{i}", 256, BF16) for i in range(2)]
    onesb, B_onesb = wt("onesb", 128, BF16)
    identb, B_identb = wt("identb", 128, BF16)

    PS = [nc.alloc_psum_tensor(f"ps{i}", [128, 512], F32) for i in range(8)]
    B_PS = [Buf(f"ps{i}") for i in range(8)]
    B_PS7a = B_PS7b = B_PS[7]

    def C(name, a=0, b=None):
        o, n = CST[name]
        if b is None:
            b = n
        return cst[:, o + a:o + b]

    sy.dma("sp", cst[:], cst_d, [], [B_cst])
    sy.dma("pool", wdt[:].rearrange("p k c -> p (k c)"), wDT_d, [], [B_wdt])
    sy.op("dve", [B_cst], [B_onesb], lambda e: e.tensor_copy(out=onesb[:], in_=C("ones")))
    sy.op("dve", [B_cst], [B_identb], lambda e: e.tensor_copy(out=identb[:], in_=C("ident")))
    sy.op("dve", [B_cst], [B_mnp], lambda e: e.tensor_copy(out=mnegp_b[:], in_=C("mneg_p")))
    sy.op("dve", [B_cst], [B_mns], lambda e: e.tensor_copy(out=mnegs_b[:], in_=C("mneg_s")))
    sy.op("dve", [B_cst], [B_trp], lambda e: e.tensor_copy(out=trip_b[:], in_=C("tri_p")))
    sy.op("dve", [B_cst], [B_trs], lambda e: e.tensor_copy(out=tris_b[:], in_=C("tri_s")))
    aB, B_aB = wt("aB", 32)
    sy.op("act", [B_cst], [B_aB], lambda e: e.activation(out=aB[:], in_=C("a_log"), func=AF.Exp))
    sy.op("dve", [B_aB], [B_aB], lambda e: e.tensor_scalar_mul(out=aB[:], in0=aB[:], scalar1=-1.0))
    for t_, b_ in ((haloA, B_hA), (haloB, B_hB), (haloF, B_hF), (hst, B_hst)):
        sy.op("pool", [], [b_], lambda e, t_=t_: e.memset(t_[:], 0.0))
    sy.op("pool", [], [B_hbf], lambda e: e.memset(hbf[:], 0.0))

    ones_f = C("ones")
    ident_f = C("ident")

    wctr = [0]

    wscr = {}
    wdone = {}
    B_ws = [Buf(f"wst{i}") for i in range(NWSLOT)]

    def wload(name, full, idx, n):
        i = wctr[0] % NWSLOT
        wctr[0] += 1
        if name not in wscr:
            wscr[name] = nc.dram_tensor("scr_" + name, [full.shape[0], 128, n], BF16, kind="Internal").ap()
        key = (name, idx)
        if key not in wdone and name in ("wU", "wD") and cache_on[0]:
            sy.dma("pool", wsl[i][:, 0:n], full[idx], [], [B_w[i]])
        elif key not in wdone:
            sy.dma("pool", wsl[i][:, 0:n], full[idx], [], [B_w[i]])
            sy.dma("sp", wscr[name][idx], wsl[i][:, 0:n], [B_w[i]], [], sembuf=B_ws[i])
            wdone[key] = ("dma", B_ws[i].dsem, B_ws[i].dval, "d_" + B_ws[i].name)
        else:
            sy.dma("pool", wsl[i][:, 0:n], wscr[name][idx], [], [B_w[i]], extra=[wdone[key]])
        return wsl[i], B_w[i]

    cache_list = ([x for j in range(KC) for x in (("w6a", w6a_d, j, 24 * 128), ("w6b", w6b_d, j, 32 * 128))]
                  + [("wO", wO_d, i, KC * 256) for i in range(8)])
    cache_it = iter(cache_list)
    cache_on = [False]
    cache_acc = [0.0]

    def cache_tick(amount):
        if not cache_on[0]:
            return
        cache_acc[0] += amount
        while cache_acc[0] >= 1.0:
            cache_acc[0] -= 1.0
            item = next(cache_it, None)
            if item is not None and (item[0], item[2]) not in wdone:
                wload(*item)

    rr = {}

    def rot(lst, key):
        i = rr.get(key, 0)
        rr[key] = i + 1
        return lst[i % len(lst)]

    def rstd_from(ps_ap, Bps, n, out_t, B_out, scale):
        sy.op("act", [Bps], [B_out], lambda e: e.activation(out=out_t[:, 0:n], in_=ps_ap, func=AF.Sqrt,
                                                            bias=EPS, scale=scale))
        sy.op("dve", [B_out], [B_out], lambda e: e.reciprocal(out=out_t[:, 0:n], in_=out_t[:, 0:n]))

    def s1_prefetch(c0, n):
        sy.dma("sp", ZGf[:, :, 0:n], xT_d[:, :, c0:c0 + n], [], [B_zs, B_gy])
        sy.op("act", [B_zs, B_gy], [B_hT], lambda e: e.activation(out=hT[:, :, 0:n], in_=ZGf[:, :, 0:n], func=AF.Square))

        def mm(e):
            for kc in range(KC):
                ins = e.matmul(PS[7][:, 0:n], lhsT=onesb[:], rhs=hT[:, kc, 0:n], start=(kc == 0), stop=(kc == KC - 1))
            return ins
        sy.op("pe", [B_hT, B_onesb], [B_PS[7]], mm)
        rstd_from(PS[7][:, 0:n], B_PS[7], n, rs2, B_rs2, 1.0 / D)
        for kc in range(KC):
            sy.op("dve", [B_zs, B_gy, B_rs2, B_cst], [B_hT], lambda e, kc=kc: e.scalar_tensor_tensor(
                out=hT[:, kc, 0:n], in0=ZGf[:, kc, 0:n], scalar=C("g_pre1", kc, kc + 1), in1=rs2[:, 0:n],
                op0=ALU.mult, op1=ALU.mult))

    def s1_finish(n):
        sy.op("act", [B_zs, B_gy], [B_xT], lambda e: e.copy(out=xT[:, :, 0:n], in_=ZGf[:, :, 0:n]))

    def s1_load_norm(c0, n):
        sy.dma("sp", xT[:, :, 0:n], xT_d[:, :, c0:c0 + n], [], [B_xT])
        sy.op("act", [B_xT], [B_sqT], lambda e: e.activation(out=sqT[:, :, 0:n], in_=xT[:, :, 0:n], func=AF.Square))

        def mm(e):
            for kc in range(KC):
                ins = e.matmul(PS[7][:, 0:n], lhsT=onesb[:], rhs=sqT[:, kc, 0:n], start=(kc == 0), stop=(kc == KC - 1))
            return ins
        sy.op("pe", [B_sqT, B_onesb], [B_PS[7]], mm)
        rstd_from(PS[7][:, 0:n], B_PS[7], n, rs1, B_rs1, 1.0 / D)
        for kc in range(KC):
            sy.op("dve", [B_xT, B_rs1, B_cst], [B_hT], lambda e, kc=kc: e.scalar_tensor_tensor(
                out=hT[:, kc, 0:n], in0=xT[:, kc, 0:n], scalar=C("g_pre1", kc, kc + 1), in1=rs1[:, 0:n],
                op0=ALU.mult, op1=ALU.mult))

    def proj_group(ps, Bps, w, Bw, k0, nk, ncolw, c0w, rhs, Brhs, n):
        def mm(e):
            for k in range(nk):
                o = (k0 + k) * ncolw + c0w
                ins = e.matmul(ps[:, 0:n], lhsT=w[:, o:o + 128], rhs=rhs[:, k, 0:n], start=(k == 0), stop=(k == nk - 1))
            return ins
        sy.op("pe", [Bw] + Brhs, [Bps], mm)

    B_ux = [Buf(f"uext{j}") for j in range(NJA)]
    B_cv = [Buf(f"cvA{j}") for j in range(NJA)]
    fdummy = sb("fdummy", [128, 8]); B_fd = Buf("fdummy")

    def fence(bufs):
        sy.op("dve", [], list(bufs) + [B_fd], lambda e: e.memset(fdummy[:, 0:1], 0.0))

    def s2_begin(np_, ns):
        fence([B_R2] + B_ux + B_cv)
        if ns:
            for jj in range(NJA):
                for hh in range(2):
                    sy.dma("sp", uexts[:, jj, 8 * hh:8 * hh + 8, 0:CA - 1], sca_d[:, jj, 8 * hh:8 * hh + 8, :], [], [B_R1[0], B_R1[1]])

    def s2_chunk(j, np_, ns, defer_taps=False):
        n = np_ + ns
        W31 = CST["w_dw_a"][0]
        w, Bw = wload("wA", wA_d, j, KC * 256)
        pa, Bpa = PS[2 * (j % 2)], B_PS[2 * (j % 2)]
        pb, Bpb = PS[2 * (j % 2) + 1], B_PS[2 * (j % 2) + 1]
        proj_group(pa, Bpa, w, Bw, 0, KC, 256, 0, hT, [B_hT], n)
        proj_group(pb, Bpb, w, Bw, 0, KC, 256, 128, hT, [B_hT], n)
        sg, Bsg = rot(tmpA, "tmpA")
        sy.op("act", [Bpb], [Bsg], lambda e: e.activation(out=sg[:, 0:n], in_=pb[:, 0:n], func=AF.Sigmoid))
        sy.op("dve", [B_hA], [B_ux[j]], lambda e: e.tensor_copy(out=uext[:, j, 0:CA - 1], in_=haloA[:, j, :]))
        sy.op("dve", [Bpa, Bsg], [B_ux[j]], lambda e: e.tensor_tensor(
            out=uext[:, j, CA - 1:CA - 1 + np_], in0=pa[:, 0:np_], in1=sg[:, 0:np_], op=ALU.mult))
        sy.op("dve", [B_ux[j]], [B_hA], lambda e: e.tensor_copy(out=haloA[:, j, :], in_=uext[:, j, np_:np_ + CA - 1]))
        if ns:
            sy.op("dve", [Bpa, Bsg], [B_R1[0], B_R1[1]], lambda e: e.tensor_tensor(
                out=uexts[:, j, :, CA - 1:CA + 3], in0=pa[:, np_:n].rearrange("p (s t) -> p s t", s=NSEQ),
                in1=sg[:, np_:n].rearrange("p (s t) -> p s t", s=NSEQ), op=ALU.mult))
        run_deferred()
        if not defer_taps:
            s2_taps(j, np_, 0, CA)
        if ns:
            s2_sample_taps(j, np_, ns)
        cache_tick(1.0)

    def s2_taps(j, np_, k0, k1):
        W31 = CST["w_dw_a"][0]
        sy.big = True
        for k in range(k0, k1):
            wk = cst[:, W31 + j * CA + k:W31 + j * CA + k + 1]
            if k == 0:
                sy.op("dve", [B_ux[j], B_cst], [B_cv[j]], lambda e: e.tensor_scalar(
                    out=cvA[:, j, 0:np_], in0=uext[:, j, k:k + np_], scalar1=wk, scalar2=C("b_dw_a", j, j + 1),
                    op0=ALU.mult, op1=ALU.add))
            else:
                sy.op("dve", [B_ux[j], B_cst], [B_cv[j]], lambda e: e.scalar_tensor_tensor(
                    out=cvA[:, j, 0:np_], in0=uext[:, j, k:k + np_], scalar=wk, in1=cvA[:, j, 0:np_],
                    op0=ALU.mult, op1=ALU.add))
        sy.big = False

    def s2_sample_taps(j, np_, ns):
        n = np_ + ns
        W31 = CST["w_dw_a"][0]
        if ns:
            cs_ = cvA[:, j, np_:n].rearrange("p (s t) -> p s t", s=NSEQ)
            for k in range(CA):
                wk = cst[:, W31 + j * CA + k:W31 + j * CA + k + 1]
                if k == 0:
                    sy.op("dve", [B_R1[0], B_R1[1], B_cst], [B_cv[j]], lambda e: e.tensor_scalar(
                        out=cs_, in0=uexts[:, j, :, 0:4], scalar1=wk, scalar2=C("b_dw_a", j, j + 1),
                        op0=ALU.mult, op1=ALU.add))
                else:
                    sy.op("dve", [B_R1[0], B_R1[1], B_cst], [B_cv[j]], lambda e: e.scalar_tensor_tensor(
                        out=cs_, in0=uexts[:, j, :, k:k + 4], scalar=wk, in1=cs_, op0=ALU.mult, op1=ALU.add))

    def s2_tail_steps(np_, ns, bsum, bsq):
        n = np_ + ns
        steps = []

        def st_stats(j):
            def f():
                cp, Bcp = rot(sqg, "cpb")
                sq, Bsq = rot(sqb, "sqb")
                sy.op("act", [B_cv[j]], [Bcp], lambda e: e.copy(out=cp[:, 0:n], in_=cvA[:, j, 0:n]))
                sy.op("act", [B_cv[j]], [Bsq], lambda e: e.activation(out=sq[:, 0:n], in_=cvA[:, j, 0:n], func=AF.Square))
                sy.op("pe", [Bcp, B_onesb], [B_PS[bsum]], lambda e: e.matmul(
                    PS[bsum][:, 0:n], lhsT=onesb[:], rhs=cp[:, 0:n], start=(j == 0), stop=(j == NJA - 1)))
                sy.op("pe", [Bsq, B_onesb], [B_PS[bsq]], lambda e: e.matmul(
                    PS[bsq][:, 0:n], lhsT=onesb[:], rhs=sq[:, 0:n], start=(j == 0), stop=(j == NJA - 1)))
            return f

        def st_chain():
            sy.op("act", [B_PS[bsum]], [B_mean], lambda e: e.mul(out=t_mean[:, 0:n], in_=PS[bsum][:, 0:n], mul=1.0 / DA))
            m2, Bm2 = tmpD[1]
            sy.op("dve", [B_mean], [Bm2], lambda e: e.tensor_tensor(out=m2[:, 0:n], in0=t_mean[:, 0:n], in1=t_mean[:, 0:n], op=ALU.mult))
            sy.op("dve", [B_PS[bsq], Bm2], [Bm2], lambda e: e.scalar_tensor_tensor(
                out=m2[:, 0:n], in0=PS[bsq][:, 0:n], scalar=1.0 / DA, in1=m2[:, 0:n], op0=ALU.mult, op1=ALU.subtract))
            sy.op("act", [Bm2], [B_rs2], lambda e: e.activation(out=rs2[:, 0:n], in_=m2[:, 0:n], func=AF.Sqrt, bias=EPS, scale=1.0))
            sy.op("dve", [B_rs2], [B_rs2], lambda e: e.reciprocal(out=rs2[:, 0:n], in_=rs2[:, 0:n]))
            sy.op("dve", [B_mean, B_rs2], [B_nmr], lambda e: e.scalar_tensor_tensor(
                out=t_nmr[:, 0:n], in0=t_mean[:, 0:n], scalar=-1.0, in1=rs2[:, 0:n], op0=ALU.mult, op1=ALU.mult))

        def st_norm(j):
            def f():
                t1, Bt1 = rot(tmpC, "tmpC")
                sy.big = True
                sy.op("dve", [B_cv[j], B_rs2], [Bt1], lambda e: e.tensor_tensor(out=t1[:, 0:n], in0=cvA[:, j, 0:n], in1=rs2[:, 0:n], op=ALU.mult))
                sy.op("dve", [Bt1, B_nmr], [Bt1], lambda e: e.tensor_tensor(out=t1[:, 0:n], in0=t1[:, 0:n], in1=t_nmr[:, 0:n], op=ALU.add))
                sy.big = False
                sy.op("act", [Bt1, B_cst], [B_uA], lambda e: e.activation(
                    out=uA[:, j, 0:n], in_=t1[:, 0:n], func=AF.Silu, bias=C("b_ln_a", j, j + 1), scale=C("g_ln_a", j, j + 1)))
            return f
        if ns:
            def st_osca():
                for jj in range(NJA):
                    for hh in range(2):
                        sy.dma("sp", osca_o[:, jj, 8 * hh:8 * hh + 8, :], uexts[:, jj, 8 * hh:8 * hh + 8, 4:CA + 3],
                               [B_R1[0], B_R1[1]], [], sembuf=B_R1[0])
            steps.append(st_osca)
        steps += [st_stats(j) for j in range(NJA)]
        steps.append(st_chain)
        steps += [st_norm(j) for j in range(NJA)]
        steps.append(lambda: fence(B_ux + B_cv + [B_R2]))
        return steps

    def s2_tail(np_, ns):
        for f in s2_tail_steps(np_, ns, 6, 7):
            f()

    def halo_a_from_prefix(n_last):
        c0 = n_last - 32
        for j in range(NJA):
            w, Bw = wload("wA", wA_d, j, KC * 256)
            pa, Bpa = PS[2 * (j % 2)], B_PS[2 * (j % 2)]
            pb, Bpb = PS[2 * (j % 2) + 1], B_PS[2 * (j % 2) + 1]

            def mm(e, c0w, ps):
                for k in range(KC):
                    o = k * 256 + c0w
                    ins = e.matmul(ps[:, 0:32], lhsT=w[:, o:o + 128], rhs=hT[:, k, c0:c0 + 32], start=(k == 0), stop=(k == KC - 1))
                return ins
            sy.op("pe", [Bw, B_hT], [Bpa], lambda e: mm(e, 0, pa))
            sy.op("pe", [Bw, B_hT], [Bpb], lambda e: mm(e, 128, pb))
            sg, Bsg = rot(tmpA, "tmpA")
            sy.op("act", [Bpb], [Bsg], lambda e: e.activation(out=sg[:, 0:32], in_=pb[:, 0:32], func=AF.Sigmoid))
            sy.op("dve", [Bpa, Bsg], [B_hA], lambda e: e.tensor_tensor(
                out=haloA[:, j, :], in0=pa[:, 2:32], in1=sg[:, 2:32], op=ALU.mult))

    wcur = {}

    def s3_begin(ns):
        if ns:
            sy.dma("sp", scb_s, scb_d, [], [B_scb])

    def s3_chunk(cc, np_, ns, pbase):
        n = np_ + ns
        W4 = CST["w_dw_b"][0]
        if cc % 2 == 0:
            wcur["x"] = wload("wX", wX_d, cc // 2, KC * 256)
        w, Bw = wcur["x"]
        ps, Bps = PS[pbase + cc % 2], B_PS[pbase + cc % 2]
        proj_group(ps, Bps, w, Bw, 0, KC, 256, 128 * (cc % 2), hT, [B_hT], n)
        run_deferred()
        ex, Bex = rot(ext1, "ext1")
        ac, Bac = rot(tmpB, "tmpBx")
        sy.op("act", [Bps], [Bex], lambda e: e.copy(out=ex[:, 3:3 + np_], in_=ps[:, 0:np_]))
        sy.op("act", [Bps, B_cst], [Bac], lambda e: e.activation(
            out=ac[:, 0:np_], in_=ps[:, 0:np_], func=AF.Identity, bias=C("b_dw_b", cc, cc + 1),
            scale=cst[:, W4 + cc * CB + 3:W4 + cc * CB + 4]))
        sy.op("dve", [B_hB], [Bex], lambda e: e.tensor_copy(out=ex[:, 0:3], in_=haloB[:, cc, :]))
        sy.op("dve", [Bex], [B_hB], lambda e: e.tensor_copy(out=haloB[:, cc, :], in_=ex[:, np_:np_ + 3]))
        sy.big = True
        for k in range(CB - 1):
            wk = cst[:, W4 + cc * CB + k:W4 + cc * CB + k + 1]
            sy.op("dve", [Bex, B_cst], [Bac], lambda e: e.scalar_tensor_tensor(
                out=ac[:, 0:np_], in0=ex[:, k:k + np_], scalar=wk, in1=ac[:, 0:np_], op0=ALU.mult, op1=ALU.add))
        sy.big = False
        if ns:
            exs, Bexs = rot(ext2, "ext2")
            exv = exs[:, 0:NSEQ * 7].rearrange("p (s t) -> p s t", s=NSEQ)
            acv = ac[:, np_:n].rearrange("p (s t) -> p s t", s=NSEQ)
            sy.op("dve", [B_scb], [Bexs], lambda e: e.tensor_copy(out=exv[:, :, 0:3], in_=scb_s[:, cc, :, :]))
            sy.op("act", [Bps], [Bexs], lambda e: e.copy(out=exv[:, :, 3:7], in_=ps[:, np_:n].rearrange("p (s t) -> p s t", s=NSEQ)))
            sy.op("dve", [Bexs], [B_ocb], lambda e: e.tensor_copy(out=ocb_s[:, cc, :, :], in_=exv[:, :, 4:7]))
            for k in range(CB):
                wk = cst[:, W4 + cc * CB + k:W4 + cc * CB + k + 1]
                if k == 0:
                    sy.op("dve", [Bexs, B_cst], [Bac], lambda e: e.tensor_scalar(
                        out=acv, in0=exv[:, :, 0:4], scalar1=wk, scalar2=C("b_dw_b", cc, cc + 1), op0=ALU.mult, op1=ALU.add))
                else:
                    sy.op("dve", [Bexs, B_cst], [Bac], lambda e: e.scalar_tensor_tensor(
                        out=acv, in0=exv[:, :, k:k + 4], scalar=wk, in1=acv, op0=ALU.mult, op1=ALU.add))
        if cc < 16:
            dst, Bd = xsT[:, cc, 0:n], B_R1[0]
        elif cc < 24:
            dst, Bd = BT[:, cc - 16, 0:n], B_R1[1]
        else:
            dst, Bd = CT[:, cc - 24, 0:n], B_R1[2]
        sy.op("act", [Bac], [Bd], lambda e: e.activation(out=dst, in_=ac[:, 0:n], func=AF.Silu))
        cache_tick(0.5)

    def s3_end(ns):
        if ns:
            sy.dma("sp", oscb_o, ocb_s, [B_ocb], [])

    def s3_xbc(np_, ns, ncc):
        s3_begin(ns)
        for cc in range(ncc):
            s3_chunk(cc, np_, ns, 0)
        s3_end(ns)

    def s4_chunk(cc, n, pbase):
        if cc % 2 == 0:
            wcur["z"] = wload("wZ", wZ_d, cc // 2, KC * 256)
        w, Bw = wcur["z"]
        ps, Bps = PS[pbase + cc % 2], B_PS[pbase + cc % 2]
        proj_group(ps, Bps, w, Bw, 0, KC, 256, 128 * (cc % 2), hT, [B_hT], n)
        sy.op("act", [Bps], [B_zs], lambda e: e.activation(out=zs[:, cc, 0:n], in_=ps[:, 0:n], func=AF.Silu))
        cache_tick(0.5)

    def s234(np_, ns):
        n = np_ + ns
        if ns:
            s2_begin(np_, ns)
            for j in range(NJA):
                s2_chunk(j, np_, ns)
            s2_tail(np_, ns)
            s3_xbc(np_, ns, 32)
            for cc in range(KC):
                s4_chunk(cc, n, 0)
            return []
        s2_begin(np_, ns)
        s3_begin(ns)
        for j in range(NJA):
            s2_chunk(j, np_, ns, defer_taps=True)
            s2_taps(j, np_, 0, 8)
            s3_chunk(4 * j, np_, ns, 4)
            s2_taps(j, np_, 8, 16)
            s3_chunk(4 * j + 1, np_, ns, 4)
            s4_chunk(2 * j, n, 6)
            s2_taps(j, np_, 16, 24)
            s3_chunk(4 * j + 2, np_, ns, 4)
            s2_taps(j, np_, 24, CA)
            s3_chunk(4 * j + 3, np_, ns, 4)
            s4_chunk(2 * j + 1, n, 6)
        s3_end(ns)
        steps = s2_tail_steps(np_, ns, 6, 7)
        while steps:
            steps.pop(0)()
        return steps

    deferred = []
    TOFF = [0]

    def run_deferred():
        while deferred:
            deferred.pop(0)()

    def ssd_dt_all(tiles):
        nt = 128
        T = len(tiles)
        W = 32 * T

        def v3(ap):
            return ap.rearrange("p (t h) -> p t h", t=T)

        def mm(e):
            for t, (c0, _) in enumerate(tiles):
                for kc in range(KC):
                    ins = e.matmul(PS[5][0:nt, 32 * t:32 * t + 32], lhsT=hT[:, kc, c0:c0 + nt], rhs=wdt[:, kc, :], start=(kc == 0), stop=(kc == KC - 1))
            return ins
        sy.op("pe", [B_hT, B_wdt], [B_PS[5]], mm)
        sy.op("dve", [B_PS[5], B_cst], [B_dtx], lambda e: e.tensor_tensor(
            out=v3(dtx[:, 0:W]), in0=v3(PS[5][:, 0:W]), in1=C("dt_bias").unsqueeze(1).to_broadcast([128, T, 32]), op=ALU.add))
        sy.op("dve", [B_dtx], [B_dtl], lambda e: e.scalar_tensor_tensor(out=dtl[:, 0:W], in0=dtx[:, 0:W], scalar=-1.0, in1=dtx[:, 0:W], op0=ALU.mult, op1=ALU.max))
        sy.op("act", [B_dtl], [B_dtl], lambda e: e.activation(out=dtl[:, 0:W], in_=dtl[:, 0:W], func=AF.Exp, scale=-1.0))
        sy.op("act", [B_dtl], [B_dtl], lambda e: e.activation(out=dtl[:, 0:W], in_=dtl[:, 0:W], func=AF.Ln, bias=1.0, scale=1.0))
        sy.op("dve", [B_dtx, B_dtl], [B_dte], lambda e: e.scalar_tensor_tensor(
            out=dte[:, 0:W], in0=dtx[:, 0:W], scalar=0.0, in1=dtl[:, 0:W], op0=ALU.max, op1=ALU.add))
        for t, (_, vidx) in enumerate(tiles):
            vo = CST["valid"][0] + vidx
            sy.op("dve", [B_dte, B_cst], [B_dte], lambda e, t=t, vo=vo: e.tensor_scalar_mul(
                out=dte[:, 32 * t:32 * t + 32], in0=dte[:, 32 * t:32 * t + 32], scalar1=cst[:, vo:vo + 1]))
        sy.op("dve", [B_dte, B_aB], [B_dta], lambda e: e.tensor_tensor(
            out=v3(dta[:, 0:W]), in0=v3(dte[:, 0:W]), in1=aB[:, :].unsqueeze(1).to_broadcast([128, T, 32]), op=ALU.mult))
        sy.op("dve", [B_dta], [B_dth], lambda e: e.tensor_copy(out=dth[:, 0:W], in_=dta[:, 0:W]))
        sy.op("dve", [B_dta, B_dth], [B_dtl2], lambda e: e.tensor_tensor(out=dtl2[:, 0:W], in0=dta[:, 0:W], in1=dth[:, 0:W], op=ALU.subtract))

        def part2():
            sy.op("pe", [B_dta, B_cst], [B_PS[5]], lambda e: e.matmul(PS[5][:, 64:64 + W], lhsT=C("tri_p"), rhs=dta[:, 0:W], start=True, stop=True))
            sy.op("pe", [B_dta, B_cst], [B_PS[5]], lambda e: e.matmul(PS[5][:, 128:128 + W], lhsT=ones_f, rhs=dta[:, 0:W], start=True, stop=True))
            sy.op("act", [B_PS[5]], [B_cum], lambda e: e.copy(out=cum[:, 0:W], in_=PS[5][:, 64:64 + W]))
            sy.op("act", [B_PS[5]], [B_ncum], lambda e: e.mul(out=ncum[:, 0:W], in_=PS[5][:, 64:64 + W], mul=-1.0))
            sy.op("dve", [B_PS[5], B_cum], [B_tend], lambda e: e.tensor_tensor(out=tend[:, 0:W], in0=PS[5][:, 128:128 + W], in1=cum[:, 0:W], op=ALU.subtract))
            sy.op("act", [B_tend], [B_tend], lambda e: e.activation(out=tend[:, 0:W], in_=tend[:, 0:W], func=AF.Exp))
            sy.op("act", [B_PS[5]], [B_decB], lambda e: e.activation(out=decB[:, 0:W], in_=PS[5][:, 128:128 + W], func=AF.Exp))
            sy.op("dve", [B_dte, B_tend], [B_dtte], lambda e: e.tensor_tensor(out=dtte[:, 0:W], in0=dte[:, 0:W], in1=tend[:, 0:W], op=ALU.mult))
        deferred.append(part2)

    def ssd_prelude(c0, nt, vidx, sample):
        tri = C("tri_s") if sample else C("tri_p")
        clm = C("blk_s") if sample else ones_f
        run_deferred()
        if sample:
            TOFF[0] = 0
            ssd_dt_sample(c0, nt, vidx, tri, clm)
        ssd_transposes(c0, nt)

    def ssd_dt_sample(c0, nt, vidx, tri, clm):
        sample = True

        def mm(e):
            for kc in range(KC):
                ins = e.matmul(PS[5][0:nt, 0:32], lhsT=hT[:, kc, c0:c0 + nt], rhs=wdt[:, kc, :], start=(kc == 0), stop=(kc == KC - 1))
            return ins
        sy.op("pe", [B_hT, B_wdt], [B_PS[5]], mm)
        sy.op("dve", [B_PS[5], B_cst], [B_dtx], lambda e: e.tensor_tensor(out=dtx[0:nt, 0:32], in0=PS[5][0:nt, 0:32], in1=C("dt_bias")[0:nt, :], op=ALU.add))
        sy.op("dve", [B_dtx], [B_dtl], lambda e: e.scalar_tensor_tensor(out=dtl[0:nt, 0:32], in0=dtx[0:nt, 0:32], scalar=-1.0, in1=dtx[0:nt, 0:32], op0=ALU.mult, op1=ALU.max))
        sy.op("act", [B_dtl], [B_dtl], lambda e: e.activation(out=dtl[0:nt, 0:32], in_=dtl[0:nt, 0:32], func=AF.Exp, scale=-1.0))
        sy.op("act", [B_dtl], [B_dtl], lambda e: e.activation(out=dtl[0:nt, 0:32], in_=dtl[0:nt, 0:32], func=AF.Ln, bias=1.0, scale=1.0))
        sy.op("dve", [B_dtx, B_dtl], [B_dte], lambda e: e.scalar_tensor_tensor(
            out=dte[0:nt, 0:32], in0=dtx[0:nt, 0:32], scalar=0.0, in1=dtl[0:nt, 0:32], op0=ALU.max, op1=ALU.add))
        vo = CST["valid"][0] + vidx
        sy.op("dve", [B_dte, B_cst], [B_dte], lambda e: e.tensor_scalar_mul(out=dte[0:nt, 0:32], in0=dte[0:nt, 0:32], scalar1=cst[0:nt, vo:vo + 1]))
        sy.op("dve", [B_dte, B_aB], [B_dta], lambda e: e.tensor_tensor(out=dta[0:nt, 0:32], in0=dte[0:nt, 0:32], in1=aB[0:nt, :], op=ALU.mult))
        sy.op("dve", [B_dta], [B_dth], lambda e: e.tensor_copy(out=dth[0:nt, 0:32], in_=dta[0:nt, 0:32]))
        sy.op("dve", [B_dta, B_dth], [B_dtl2], lambda e: e.tensor_tensor(out=dtl2[0:nt, 0:32], in0=dta[0:nt, 0:32], in1=dth[0:nt, 0:32], op=ALU.subtract))
        sy.op("pe", [B_dta, B_cst], [B_PS[5]], lambda e: e.matmul(PS[5][0:nt, 32:64], lhsT=tri[0:nt, 0:nt], rhs=dta[0:nt, 0:32], start=True, stop=True))
        sy.op("pe", [B_dta, B_cst], [B_PS[5]], lambda e: e.matmul(PS[5][0:nt, 64:96], lhsT=clm[0:nt, 0:nt], rhs=dta[0:nt, 0:32], start=True, stop=True))
        if not sample:
            sy.op("pe", [B_dta, B_cst], [B_PS[5]], lambda e: e.matmul(PS[5][:, 96:128], lhsT=ones_f[0:nt, :], rhs=dta[0:nt, 0:32], start=True, stop=True))
        sy.op("act", [B_PS[5]], [B_cum], lambda e: e.copy(out=cum[0:nt, 0:32], in_=PS[5][0:nt, 32:64]))
        sy.op("act", [B_PS[5]], [B_ncum], lambda e: e.mul(out=ncum[0:nt, 0:32], in_=PS[5][0:nt, 32:64], mul=-1.0))
        sy.op("dve", [B_PS[5], B_cum], [B_tend], lambda e: e.tensor_tensor(out=tend[0:nt, 0:32], in0=PS[5][0:nt, 64:96], in1=cum[0:nt, 0:32], op=ALU.subtract))
        sy.op("act", [B_tend], [B_tend], lambda e: e.activation(out=tend[0:nt, 0:32], in_=tend[0:nt, 0:32], func=AF.Exp))
        if not sample:
            sy.op("act", [B_PS[5]], [B_decB], lambda e: e.activation(out=decB[:], in_=PS[5][:, 96:128], func=AF.Exp))
        sy.op("dve", [B_dte, B_tend], [B_dtte], lambda e: e.tensor_tensor(out=dtte[0:nt, 0:32], in0=dte[0:nt, 0:32], in1=tend[0:nt, 0:32], op=ALU.mult))

    def ssd_transposes(c0, nt):
        o = TOFF[0]
        for b4 in range(4):
            def tr(e, b4=b4):
                for jj in range(4):
                    j = 4 * b4 + jj
                    ins = e.transpose(out=PS[b4][0:nt, jj * 128:(jj + 1) * 128], in_=xsT[:, j, c0:c0 + nt], identity=ident_f)
                return ins
            sy.op("pe", [B_R1[0], B_cst], [B_PS[b4]], tr)
        for b4 in range(4):
            pv = PS[b4][0:nt, :].rearrange("p (h x) -> p h x", h=8)
            sy.op("dve", [B_PS[b4], B_dte], [B_xdt], lambda e, b4=b4, pv=pv: e.tensor_tensor(
                out=xdt[0:nt, b4 * 512:(b4 + 1) * 512].rearrange("p (h x) -> p h x", h=8), in0=pv,
                in1=dte[0:nt, o + b4 * 8:o + (b4 + 1) * 8].unsqueeze(2).to_broadcast([nt, 8, 64]), op=ALU.mult))
            sy.op("dve", [B_PS[b4], B_dtte], [B_xdtp], lambda e, b4=b4, pv=pv: e.tensor_tensor(
                out=xdtp[0:nt, b4 * 512:(b4 + 1) * 512].rearrange("p (h x) -> p h x", h=8), in0=pv,
                in1=dtte[0:nt, o + b4 * 8:o + (b4 + 1) * 8].unsqueeze(2).to_broadcast([nt, 8, 64]), op=ALU.mult))
        psb = PS[4].bitcast(BF16)

        def trb(e):
            for g in range(NG):
                ins = e.transpose(out=psb[0:nt, g * 128:(g + 1) * 128], in_=BT[:, g, c0:c0 + nt], identity=identb[:])
            return ins
        sy.op("pe", [B_R1[1], B_identb], [B_PS[4]], trb)
        sy.op("act", [B_PS[4]], [B_btok], lambda e: e.copy(out=btok[0:nt, :], in_=psb[0:nt, 0:NG * 128]))

    def ssd_group_common(g, c0, nt, sample):
        trib = tris_b if sample else trip_b
        mneg = mnegs_b if sample else mnegp_b
        Bcon = [B_trs, B_mns] if sample else [B_trp, B_mnp]
        p1, p2, pc = (5, 6, 4) if (g % 2 == 0 or sample) else (1, 2, 3)
        Rh_, BRh = rot(Rh, "Rh")
        Rl_, BRl = rot(Rl, "Rl")
        E_, BE = rot(Eg, "Eg")
        X_, BX = rot(Xg, "Xg")
        MT_, BMT = rot(MTg, "MTg")
        CE_, BCE = rot(CEg, "CEg")
        w4 = 4 * nt
        for R_, BR, src, Bsrc in ((Rh_, BRh, dth, B_dth), (Rl_, BRl, dtl2, B_dtl2)):
            sy.op("dve", [Bsrc] + Bcon, [BR], lambda e, R_=R_, src=src: e.tensor_tensor(
                out=R_[0:nt, 0:w4].rearrange("p (i q) -> p i q", i=4), in0=trib[0:nt, 0:nt].unsqueeze(1).to_broadcast([nt, 4, nt]),
                in1=src[0:nt, 4 * g:4 * g + 4].unsqueeze(2).to_broadcast([nt, 4, nt]), op=ALU.mult))

        def mm1(e):
            e.matmul(PS[p1][:, 0:w4], lhsT=onesb[0:nt, :], rhs=Rh_[0:nt, 0:w4], start=True, stop=False)
            return e.matmul(PS[p1][:, 0:w4], lhsT=onesb[0:nt, :], rhs=Rl_[0:nt, 0:w4], start=False, stop=True)
        sy.op("pe", [BRh, BRl, B_onesb], [B_PS[p1]], mm1)

        def mm2(e):
            e.matmul(PS[p2][0:nt, 0:w4], lhsT=onesb[0:nt, 0:nt], rhs=Rh_[0:nt, 0:w4], start=True, stop=False)
            e.matmul(PS[p2][0:nt, 0:w4], lhsT=onesb[0:nt, 0:nt], rhs=Rl_[0:nt, 0:w4], start=False, stop=False)
            return e.matmul(PS[p2][0:nt, 0:w4], lhsT=identb[0:nt, 0:nt], rhs=mneg[0:nt, 0:w4], start=False, stop=True)
        sy.op("pe", [BRh, BRl, B_onesb, B_identb] + Bcon, [B_PS[p2]], mm2)
        sy.op("pe", [B_R1[1], B_R1[2]], [B_PS[pc]], lambda e: e.matmul(
            PS[pc][0:nt, 0:nt], lhsT=BT[:, g, c0:c0 + nt], rhs=CT[:, g, c0:c0 + nt], start=True, stop=True))
        sy.op("act", [B_PS[p1]], [BX], lambda e: e.activation(out=X_[:, 0:w4], in_=PS[p1][:, 0:w4], func=AF.Exp))
        for i in range(4):
            sy.op("act", [B_PS[p2], B_ncum], [BE], lambda e, i=i: e.activation(
                out=E_[0:nt, i * nt:(i + 1) * nt], in_=PS[p2][0:nt, i * nt:(i + 1) * nt], func=AF.Exp,
                bias=ncum[0:nt, 4 * g + i:4 * g + i + 1], scale=1.0))
        sy.big = True
        sy.op("dve", [BE, B_PS[pc]], [BMT], lambda e: e.tensor_tensor(
            out=MT_[0:nt, 0:w4].rearrange("p (i q) -> p i q", i=4), in0=E_[0:nt, 0:w4].rearrange("p (i q) -> p i q", i=4),
            in1=PS[pc][0:nt, 0:nt].unsqueeze(1).to_broadcast([nt, 4, nt]), op=ALU.mult))
        sy.op("dve", [BX, B_R1[2]], [BCE], lambda e: e.tensor_tensor(
            out=CE_[:, 0:w4].rearrange("p (i q) -> p i q", i=4), in0=X_[:, 0:w4].rearrange("p (i q) -> p i q", i=4),
            in1=CT[:, g, c0:c0 + nt].unsqueeze(1).to_broadcast([128, 4, nt]), op=ALU.mult))
        sy.big = False
        return MT_, BMT, CE_, BCE

    def gate_group(g, c0, nt, psy_ap, Bpsy, first, last, sbank):
        v1_, Bv1 = rot(v1g, "v1g")
        v_, Bv = rot(vg, "vg")
        sq_, Bsq = rot(sqg, "sqg")
        for c in range(2):
            kc = 2 * g + c
            sy.op("dve", [B_R1[0], B_cst, Bpsy], [Bv1], lambda e, c=c, kc=kc: e.scalar_tensor_tensor(
                out=v1_[:, c * nt:(c + 1) * nt], in0=xsT[:, kc, c0:c0 + nt], scalar=C("dsk", kc, kc + 1),
                in1=psy_ap[:, c, 0:nt], op0=ALU.mult, op1=ALU.add))
        sy.op("dve", [Bv1, B_zs], [Bv], lambda e: e.tensor_tensor(
            out=v_[:, 0:2 * nt].rearrange("p (c t) -> p c t", c=2), in0=v1_[:, 0:2 * nt].rearrange("p (c t) -> p c t", c=2),
            in1=zs[:, 2 * g:2 * g + 2, c0:c0 + nt], op=ALU.mult))
        sy.op("act", [Bv], [Bsq], lambda e: e.activation(out=sq_[:, 0:2 * nt], in_=v_[:, 0:2 * nt], func=AF.Square))
        for c in range(2):
            kc = 2 * g + c
            sy.op("act", [Bv, B_cst], [B_gy], lambda e, c=c, kc=kc: e.activation(
                out=gyT[:, kc, c0:c0 + nt], in_=v_[:, c * nt:(c + 1) * nt], func=AF.Copy, scale=C("g_norm_b", kc, kc + 1)))

        def mm(e):
            e.matmul(PS[sbank][:, 0:nt], lhsT=onesb[:], rhs=sq_[:, 0:nt], start=first, stop=False)
            return e.matmul(PS[sbank][:, 0:nt], lhsT=onesb[:], rhs=sq_[:, nt:2 * nt], start=False, stop=last)
        sy.op("pe", [Bsq, B_onesb], [B_PS[sbank]], mm)

    def ssd_tile(c0, vidx, want_y, steps=None, toff=0):
        nt = 128
        w4 = 4 * nt
        TOFF[0] = TO = toff
        ssd_prelude(c0, nt, vidx, False)
        if want_y:
            def buf(lst, g):
                return lst[g % 2]

            def st_R(g):
                for lst, src, Bsrc in ((Rh, dth, B_dth), (Rl, dtl2, B_dtl2)):
                    R_, BR = buf(lst, g)
                    sy.op("dve", [Bsrc, B_trp], [BR], lambda e, R_=R_, src=src: e.tensor_tensor(
                        out=R_[0:nt, 0:w4].rearrange("p (i q) -> p i q", i=4), in0=trip_b[0:nt, 0:nt].unsqueeze(1).to_broadcast([nt, 4, nt]),
                        in1=src[0:nt, TO + 4 * g:TO + 4 * g + 4].unsqueeze(2).to_broadcast([nt, 4, nt]), op=ALU.mult))

            def st_P(g):
                p1, p2, pc = (5, 6, 4) if g % 2 == 0 else (1, 2, 3)
                (Rh_, BRh), (Rl_, BRl) = buf(Rh, g), buf(Rl, g)
                (E_, BE), (X_, BX) = buf(Eg, g), buf(Xg, g)

                def mm1(e):
                    e.matmul(PS[p1][:, 0:w4], lhsT=onesb[0:nt, :], rhs=Rh_[0:nt, 0:w4], start=True, stop=False)
                    return e.matmul(PS[p1][:, 0:w4], lhsT=onesb[0:nt, :], rhs=Rl_[0:nt, 0:w4], start=False, stop=True)
                sy.op("pe", [BRh, BRl, B_onesb], [B_PS[p1]], mm1)

                def mm2(e):
                    e.matmul(PS[p2][0:nt, 0:w4], lhsT=onesb[0:nt, 0:nt], rhs=Rh_[0:nt, 0:w4], start=True, stop=False)
                    e.matmul(PS[p2][0:nt, 0:w4], lhsT=onesb[0:nt, 0:nt], rhs=Rl_[0:nt, 0:w4], start=False, stop=False)
                    return e.matmul(PS[p2][0:nt, 0:w4], lhsT=identb[0:nt, 0:nt], rhs=mnegp_b[0:nt, 0:w4], start=False, stop=True)
                sy.op("pe", [BRh, BRl, B_onesb, B_identb, B_mnp], [B_PS[p2]], mm2)
                sy.op("pe", [B_R1[1], B_R1[2]], [B_PS[pc]], lambda e: e.matmul(
                    PS[pc][0:nt, 0:nt], lhsT=BT[:, g, c0:c0 + nt], rhs=CT[:, g, c0:c0 + nt], start=True, stop=True))
                sy.op("act", [B_PS[p1]], [BX], lambda e: e.activation(out=X_[:, 0:w4], in_=PS[p1][:, 0:w4], func=AF.Exp))
                for i in range(4):
                    sy.op("act", [B_PS[p2], B_ncum], [BE], lambda e, i=i: e.activation(
                        out=E_[0:nt, i * nt:(i + 1) * nt], in_=PS[p2][0:nt, i * nt:(i + 1) * nt], func=AF.Exp,
                        bias=ncum[0:nt, TO + 4 * g + i:TO + 4 * g + i + 1], scale=1.0))

            def st_M(g):
                pc = 4 if g % 2 == 0 else 3
                (E_, BE), (X_, BX) = buf(Eg, g), buf(Xg, g)
                (MT_, BMT), (CE_, BCE) = buf(MTg, g), buf(CEg, g)
                sy.big = True
                sy.op("dve", [BE, B_PS[pc]], [BMT], lambda e: e.tensor_tensor(
                    out=MT_[0:nt, 0:w4].rearrange("p (i q) -> p i q", i=4), in0=E_[0:nt, 0:w4].rearrange("p (i q) -> p i q", i=4),
                    in1=PS[pc][0:nt, 0:nt].unsqueeze(1).to_broadcast([nt, 4, nt]), op=ALU.mult))
                sy.op("dve", [BX, B_R1[2]], [BCE], lambda e: e.tensor_tensor(
                    out=CE_[:, 0:w4].rearrange("p (i q) -> p i q", i=4), in0=X_[:, 0:w4].rearrange("p (i q) -> p i q", i=4),
                    in1=CT[:, g, c0:c0 + nt].unsqueeze(1).to_broadcast([128, 4, nt]), op=ALU.mult))
                sy.big = False

            def psy_of(g):
                return PS[7][:, (g % 2) * 256:(g % 2) * 256 + 256].rearrange("p (c t) -> p c t", c=2)

            def st_Y(g):
                (MT_, BMT), (CE_, BCE) = buf(MTg, g), buf(CEg, g)
                psy = psy_of(g)

                def mmy(e):
                    for i in range(4):
                        hd = 4 * g + i
                        hf = hd % 2
                        o = psy[hf * 64:(hf + 1) * 64, i // 2, :]
                        e.matmul(o, lhsT=xdt[0:nt, hd * 64:(hd + 1) * 64], rhs=MT_[0:nt, i * nt:(i + 1) * nt], start=True, stop=False)
                        ins = e.matmul(o, lhsT=hbf[:, hd * 64:(hd + 1) * 64], rhs=CE_[:, i * nt:(i + 1) * nt], start=False, stop=True)
                    return ins
                sy.op("pe", [B_xdt, BMT, B_hbf, BCE], [B_PS7a], mmy)

            def st_V(g):
                psy = psy_of(g)
                (v1_, Bv1), (v_, Bv), (sq_, Bsq) = buf(v1g, g), buf(vg, g), buf(sqg, g)
                for c in range(2):
                    kc = 2 * g + c
                    sy.op("dve", [B_R1[0], B_cst, B_PS7a], [Bv1], lambda e, c=c, kc=kc: e.scalar_tensor_tensor(
                        out=v1_[:, c * nt:(c + 1) * nt], in0=xsT[:, kc, c0:c0 + nt], scalar=C("dsk", kc, kc + 1),
                        in1=psy[:, c, 0:nt], op0=ALU.mult, op1=ALU.add))
                sy.big = True
                sy.op("dve", [Bv1, B_zs], [Bv], lambda e: e.tensor_tensor(
                    out=v_[:, 0:2 * nt].rearrange("p (c t) -> p c t", c=2), in0=v1_[:, 0:2 * nt].rearrange("p (c t) -> p c t", c=2),
                    in1=zs[:, 2 * g:2 * g + 2, c0:c0 + nt], op=ALU.mult))
                sy.big = False
                sy.op("act", [Bv], [Bsq], lambda e: e.activation(out=sq_[:, 0:2 * nt], in_=v_[:, 0:2 * nt], func=AF.Square))
                for c in range(2):
                    kc = 2 * g + c
                    sy.op("act", [Bv, B_cst], [B_gy], lambda e, c=c, kc=kc: e.activation(
                        out=gyT[:, kc, c0:c0 + nt], in_=v_[:, c * nt:(c + 1) * nt], func=AF.Copy, scale=C("g_norm_b", kc, kc + 1)))

            def st_S(g):
                sq_, Bsq = buf(sqg, g)

                def mm(e):
                    e.matmul(PS[0][:, 0:nt], lhsT=onesb[:], rhs=sq_[:, 0:nt], start=(g == 0), stop=False)
                    return e.matmul(PS[0][:, 0:nt], lhsT=onesb[:], rhs=sq_[:, nt:2 * nt], start=False, stop=(g == NG - 1))
                sy.op("pe", [Bsq, B_onesb], [B_PS[0]], mm)

            st_R(0)
            st_R(1)
            st_P(0)
            st_M(0)
            for i in range(NG + 1):
                if i + 2 < NG:
                    st_R(i + 2)
                if i + 1 < NG:
                    st_P(i + 1)
                if i < NG:
                    st_Y(i)
                    st_V(i)
                if i + 1 < NG:
                    st_M(i + 1)
                if 1 <= i:
                    st_S(i - 1)
            sy.op("act", [B_PS[0]], [B_rsy], lambda e: e.activation(out=rsy[:, c0:c0 + nt], in_=PS[0][:, 0:nt], func=AF.Sqrt, bias=EPS, scale=1.0 / DB))
            sy.op("dve", [B_rsy], [B_rsy], lambda e: e.reciprocal(out=rsy[:, c0:c0 + nt], in_=rsy[:, c0:c0 + nt]))
        for b4 in range(4):
            def mms(e, b4=b4):
                for gg in range(2):
                    g = 2 * b4 + gg
                    ins = e.matmul(PS[b4][:, gg * 256:(gg + 1) * 256], lhsT=btok[0:nt, g * 128:(g + 1) * 128],
                                   rhs=xdtp[0:nt, g * 256:(g + 1) * 256], start=True, stop=True)
                return ins
            sy.op("pe", [B_btok, B_xdtp], [B_PS[b4]], mms)
        sy.op("dve", [B_hst, B_decB], [B_hst], lambda e: e.tensor_tensor(
            out=hst[:].rearrange("p (h x) -> p h x", h=H), in0=hst[:].rearrange("p (h x) -> p h x", h=H),
            in1=decB[:, toff:toff + 32].unsqueeze(2).to_broadcast([128, H, HP]), op=ALU.mult))
        for b4 in range(4):
            sy.op("dve", [B_hst, B_PS[b4]], [B_hst], lambda e, b4=b4: e.tensor_tensor(
                out=hst[:, b4 * 512:(b4 + 1) * 512], in0=hst[:, b4 * 512:(b4 + 1) * 512], in1=PS[b4][:, :], op=ALU.add))
        sy.op("act", [B_hst], [B_hbf], lambda e: e.copy(out=hbf[:], in_=hst[:]))

    def ssd_samples(c0, vidx):
        nt = NSTOK
        ssd_prelude(c0, nt, vidx, True)
        psy_all = [PS[0], PS[1]]
        for bk in range(2):
            sy.op("dve", [], [B_PS[bk]], lambda e, bk=bk: e.memset(PS[bk][:, :], 0.0))
        for g in range(NG):
            MT_, BMT, CE_, BCE = ssd_group_common(g, c0, nt, True)
            sy.op("act", [BCE], [B_CEall], lambda e: e.copy(out=CEall[:, g * 4 * nt:(g + 1) * 4 * nt], in_=CE_[:, 0:4 * nt]))

            def mmy(e):
                for i in range(4):
                    hd = 4 * g + i
                    pr = hd // 2
                    hf = hd % 2
                    o = PS[pr // 8][hf * 64:(hf + 1) * 64, (pr % 8) * 64:(pr % 8) * 64 + nt]
                    ins = e.matmul(o, lhsT=xdt[0:nt, hd * 64:(hd + 1) * 64], rhs=MT_[0:nt, i * nt:(i + 1) * nt], start=False, stop=False,
                                   skip_group_check=True)
                return ins
            sy.op("pe", [B_xdt, BMT], [B_PS[0], B_PS[1]], mmy)
        R_, BR = Eg[0]
        so0 = CST["sel"][0]
        sy.op("dve", [B_dta, B_cst], [BR], lambda e: e.tensor_tensor(
            out=R_[0:nt, 0:NSEQ * 32].rearrange("p (s h) -> p s h", s=NSEQ),
            in0=cst[0:nt, so0:so0 + NSEQ].unsqueeze(2).to_broadcast([nt, NSEQ, 32]),
            in1=dta[0:nt, 0:32].unsqueeze(1).to_broadcast([nt, NSEQ, 32]), op=ALU.mult))
        sy.op("pe", [BR, B_cst], [B_PS[4]], lambda e: e.matmul(PS[4][:, 0:NSEQ * 32], lhsT=ones_f[0:nt, :], rhs=R_[0:nt, 0:NSEQ * 32], start=True, stop=True))
        sy.op("act", [B_PS[4]], [B_decS], lambda e: e.activation(out=decS[:], in_=PS[4][:, 0:NSEQ * 32], func=AF.Exp))
        hbufs = [(h0s, B_h0s), (h0s2[:, :], B_h0s2)]
        sy.dma("sp", hbufs[0][0], ssm_d[0], [], [hbufs[0][1]])
        for s in range(NSEQ):
            hs, Bhs = hbufs[s % 2]
            if s + 1 < NSEQ:
                sy.dma("sp", hbufs[(s + 1) % 2][0], ssm_d[s + 1], [], [hbufs[(s + 1) % 2][1]])
            sy.op("act", [Bhs], [B_h0b], lambda e: e.copy(out=h0b[:], in_=hs[:]))

            def mmi(e):
                for hd in range(H):
                    pr = hd // 2
                    hf = hd % 2
                    o = PS[pr // 8][hf * 64:(hf + 1) * 64, (pr % 8) * 64 + 4 * s:(pr % 8) * 64 + 4 * s + 4]
                    ins = e.matmul(o, lhsT=h0b[:, hd * 64:(hd + 1) * 64], rhs=CEall[:, hd * nt + 4 * s:hd * nt + 4 * s + 4],
                                   start=False, stop=(s == NSEQ - 1), skip_group_check=True)
                return ins
            sy.op("pe", [B_h0b, B_CEall], [B_PS[0], B_PS[1]], mmi)
            so = CST["sel"][0] + s
            sy.op("dve", [B_btok, B_cst], [B_bmsk], lambda e: e.tensor_scalar_mul(out=bmsk[0:nt, :], in0=btok[0:nt, :], scalar1=cst[0:nt, so:so + 1]))
            for b4 in range(2):
                def mms(e, b4=b4):
                    for gg in range(2):
                        g = 2 * b4 + gg
                        ins = e.matmul(PS[2 + b4][:, gg * 256:(gg + 1) * 256], lhsT=bmsk[0:nt, g * 128:(g + 1) * 128],
                                       rhs=xdtp[0:nt, g * 256:(g + 1) * 256], start=True, stop=True)
                    return ins
                sy.op("pe", [B_bmsk, B_xdtp], [B_PS[2 + b4]], mms)
            for b4 in range(2):
                def mms2(e, b4=b4):
                    for gg in range(2):
                        g = 4 + 2 * b4 + gg
                        ins = e.matmul(PS[6 + b4][:, gg * 256:(gg + 1) * 256], lhsT=bmsk[0:nt, g * 128:(g + 1) * 128],
                                       rhs=xdtp[0:nt, g * 256:(g + 1) * 256], start=True, stop=True)
                    return ins
                sy.op("pe", [B_bmsk, B_xdtp], [B_PS[6 + b4]], mms2)
            sy.op("dve", [Bhs, B_decS], [Bhs], lambda e: e.tensor_tensor(
                out=hs[:].rearrange("p (h x) -> p h x", h=H), in0=hs[:].rearrange("p (h x) -> p h x", h=H),
                in1=decS[:, s * 32:(s + 1) * 32].unsqueeze(2).to_broadcast([128, H, HP]), op=ALU.mult))
            for b4, bank in enumerate((2, 3, 6, 7)):
                sy.op("dve", [Bhs, B_PS[bank]], [Bhs], lambda e, b4=b4, bank=bank: e.tensor_tensor(
                    out=hs[:, b4 * 512:(b4 + 1) * 512], in0=hs[:, b4 * 512:(b4 + 1) * 512], in1=PS[bank][:, :], op=ALU.add))
            sy.dma("sp", osssm_o[s], hs[:], [Bhs], [])
            cache_tick(2.0)
        for g in range(NG):
            psy = PS[g // 4][:, (g % 4) * 128:(g % 4) * 128 + 128].rearrange("p (c t) -> p c t", c=2)
            gate_group(g, c0, nt, psy, B_PS[g // 4], g == 0, g == NG - 1, 2)
        sy.op("act", [B_PS[2]], [B_rsy], lambda e: e.activation(out=rsy[:, c0:c0 + nt], in_=PS[2][:, 0:nt], func=AF.Sqrt, bias=EPS, scale=1.0 / DB))
        sy.op("dve", [B_rsy], [B_rsy], lambda e: e.reciprocal(out=rsy[:, c0:c0 + nt], in_=rsy[:, c0:c0 + nt]))

    def s6_merge(n):
        for j in range(KC):
            wa, Bwa = wload("w6a", w6a_d, j, 24 * 128)
            wb, Bwb = wload("w6b", w6b_d, j, 32 * 128)
            bs = 4 * (j % 2)
            proj_group(PS[bs], B_PS[bs], wa, Bwa, 0, NJA, 128, 0, uA, [B_uA], n)
            proj_group(PS[bs + 1], B_PS[bs + 1], wa, Bwa, NJA, KC, 128, 0, gyT, [B_gy], n)
            proj_group(PS[bs + 2], B_PS[bs + 2], wb, Bwb, 0, KC, 128, 0, hT, [B_hT], n)
            proj_group(PS[bs + 3], B_PS[bs + 3], wb, Bwb, KC, KC, 128, 0, hT, [B_hT], n)
            rd3 = [B_PS[bs + 3]]
            sa, Bsa = rot(tmpA, "tmpA")
            sb_, Bsb = rot(tmpB, "tmpB")
            t1, Bt1 = rot(tmpC, "tmpC")
            t2, Bt2 = rot(tmpD, "tmpD")
            sy.op("act", [B_PS[bs + 2], B_cst], [Bsa], lambda e: e.activation(out=sa[:, 0:n], in_=PS[bs + 2][:, 0:n], func=AF.Sigmoid, bias=C("b_gate", j, j + 1), scale=1.0))
            sy.op("act", rd3 + [B_cst], [Bsb], lambda e: e.activation(out=sb_[:, 0:n], in_=PS[bs + 3][:, 0:n], func=AF.Sigmoid, bias=C("b_gate", 16 + j, 17 + j), scale=1.0))
            sy.op("dve", [B_PS[bs], Bsa, B_cst], [Bt1], lambda e: e.scalar_tensor_tensor(
                out=t1[:, 0:n], in0=PS[bs][:, 0:n], scalar=C("b_a_out", j, j + 1), in1=sa[:, 0:n], op0=ALU.add, op1=ALU.mult))
            sy.big = True
            sy.op("dve", [B_PS[bs + 1], B_rsy], [Bt2], lambda e: e.tensor_tensor(out=t2[:, 0:n], in0=PS[bs + 1][:, 0:n], in1=rsy[:, 0:n], op=ALU.mult))
            sy.op("dve", [Bt2, Bsb], [Bt2], lambda e: e.tensor_tensor(out=t2[:, 0:n], in0=t2[:, 0:n], in1=sb_[:, 0:n], op=ALU.mult))
            sy.op("dve", [Bt1, Bt2], [B_min], lambda e: e.tensor_tensor(out=minT[:, j, 0:n], in0=t1[:, 0:n], in1=t2[:, 0:n], op=ALU.add))
            sy.big = False

    def s7_out_res(n):
        for j in range(KC):
            if j % 2 == 0:
                w, Bw = wload("wO", wO_d, j // 2, KC * 256)
            ps, Bps = PS[j % 2], B_PS[j % 2]
            proj_group(ps, Bps, w, Bw, 0, KC, 256, 128 * (j % 2), minT, [B_min], n)
            sq, Bsq = rot(sqb, "sqb")
            sy.op("act", [Bps], [B_R2], lambda e: e.copy(out=mT[:, j, 0:n], in_=ps[:, 0:n]))
            sy.op("act", [Bps], [Bsq], lambda e: e.activation(out=sq[:, 0:n], in_=ps[:, 0:n], func=AF.Square))
            sy.op("pe", [Bsq, B_onesb], [B_PS[6]], lambda e: e.matmul(PS[6][:, 0:n], lhsT=onesb[:], rhs=sq[:, 0:n], start=(j == 0), stop=(j == KC - 1)))
        rstd_from(PS[6][:, 0:n], B_PS[6], n, rs1, B_rs1, 1.0 / D)
        for j in range(KC):
            t1, Bt1 = rot(tmpC, "tmpC")
            sy.big = True
            sy.op("dve", [B_R2, B_rs1, B_cst], [Bt1], lambda e: e.scalar_tensor_tensor(
                out=t1[:, 0:n], in0=mT[:, j, 0:n], scalar=C("g_post1", j, j + 1), in1=rs1[:, 0:n], op0=ALU.mult, op1=ALU.mult))
            sy.op("dve", [Bt1, B_xT], [B_xT], lambda e: e.tensor_tensor(out=xT[:, j, 0:n], in0=xT[:, j, 0:n], in1=t1[:, 0:n], op=ALU.add))
            sy.big = False
        sy.op("act", [B_xT], [B_sqT], lambda e: e.activation(out=sqT[:, :, 0:n], in_=xT[:, :, 0:n], func=AF.Square))

        def mm(e):
            for kc in range(KC):
                ins = e.matmul(PS[7][:, 0:n], lhsT=onesb[:], rhs=sqT[:, kc, 0:n], start=(kc == 0), stop=(kc == KC - 1))
            return ins
        sy.op("pe", [B_sqT, B_onesb], [B_PS[7]], mm)
        rstd_from(PS[7][:, 0:n], B_PS[7], n, rs2, B_rs2, 1.0 / D)
        for kc in range(KC):
            sy.op("dve", [B_xT, B_rs2, B_cst], [B_hT], lambda e, kc=kc: e.scalar_tensor_tensor(
                out=hT[:, kc, 0:n], in0=xT[:, kc, 0:n], scalar=C("g_pre2", kc, kc + 1), in1=rs2[:, 0:n], op0=ALU.mult, op1=ALU.mult))

    def s8_ffn_up(np_, ns, need_f):
        n = np_ + ns
        W3 = CST["w_dw_f"][0]
        if ns:
            sy.dma("sp", scf_s, scf_d, [], [B_R2])
        for j in range(NF):
            w, Bw = wload("wU", wU_d, j, KC * 256)
            bs = 2 * (j % 2)
            proj_group(PS[bs], B_PS[bs], w, Bw, 0, KC, 256, 0, hT, [B_hT], n)
            proj_group(PS[bs + 1], B_PS[bs + 1], w, Bw, 0, KC, 256, 128, hT, [B_hT], n)
            accs = []
            for half, (exl, tl) in enumerate(((ext1, tmpA), (ext2, tmpB))):
                cc = j + NF * half
                ps, Bps = PS[bs + half], B_PS[bs + half]
                ex, Bex = rot(exl, "e" + str(half))
                ac, Bac = rot(tl, "t" + str(half))
                sy.op("act", [Bps], [Bex], lambda e, ex=ex, ps=ps: e.copy(out=ex[:, 2:2 + np_], in_=ps[:, 0:np_]))
                sy.op("dve", [B_hF], [Bex], lambda e, ex=ex, cc=cc: e.tensor_copy(out=ex[:, 0:2], in_=haloF[:, cc, :]))
                sy.op("dve", [Bex], [B_hF], lambda e, ex=ex, cc=cc: e.tensor_copy(out=haloF[:, cc, :], in_=ex[:, np_:np_ + 2]))
                if need_f:
                    sy.op("act", [Bps, B_cst], [Bac], lambda e, ac=ac, ps=ps, cc=cc: e.activation(
                        out=ac[:, 0:np_], in_=ps[:, 0:np_], func=AF.Identity, bias=C("b_dw_f", cc, cc + 1),
                        scale=cst[:, W3 + cc * CF + 2:W3 + cc * CF + 3]))
                    sy.big = True
                    for k in range(CF - 1):
                        wk = cst[:, W3 + cc * CF + k:W3 + cc * CF + k + 1]
                        sy.op("dve", [Bex, B_cst], [Bac], lambda e, ex=ex, ac=ac, wk=wk, k=k: e.scalar_tensor_tensor(
                            out=ac[:, 0:np_], in0=ex[:, k:k + np_], scalar=wk, in1=ac[:, 0:np_], op0=ALU.mult, op1=ALU.add))
                    sy.big = False
                if ns:
                    exs, Bexs = rot(tmpC if half == 0 else tmpD, "es" + str(half))
                    exv = exs[:, 0:NSEQ * 6].rearrange("p (s t) -> p s t", s=NSEQ)
                    acv = ac[:, np_:n].rearrange("p (s t) -> p s t", s=NSEQ)
                    sy.op("dve", [B_R2], [Bexs], lambda e, exv=exv, cc=cc: e.tensor_copy(out=exv[:, :, 0:2], in_=scf_s[:, cc, :, :]))
                    sy.op("act", [Bps], [Bexs], lambda e, exv=exv, ps=ps: e.copy(out=exv[:, :, 2:6], in_=ps[:, np_:n].rearrange("p (s t) -> p s t", s=NSEQ)))
                    sy.op("dve", [Bexs], [B_R2], lambda e, exv=exv, cc=cc: e.tensor_copy(out=ocf_s[:, cc, :, :], in_=exv[:, :, 4:6]))
                    for k in range(CF):
                        wk = cst[:, W3 + cc * CF + k:W3 + cc * CF + k + 1]
                        if k == 0:
                            sy.op("dve", [Bexs, B_cst], [Bac], lambda e, exv=exv, acv=acv, wk=wk, cc=cc: e.tensor_scalar(
                                out=acv, in0=exv[:, :, 0:4], scalar1=wk, scalar2=C("b_dw_f", cc, cc + 1), op0=ALU.mult, op1=ALU.add))
                        else:
                            sy.op("dve", [Bexs, B_cst], [Bac], lambda e, exv=exv, acv=acv, wk=wk, k=k: e.scalar_tensor_tensor(
                                out=acv, in0=exv[:, :, k:k + 4], scalar=wk, in1=acv, op0=ALU.mult, op1=ALU.add))
                accs.append((ac, Bac))
            if need_f:
                (gc, Bgc), (vc, Bvc) = accs
                sy.op("act", [Bgc], [Bgc], lambda e: e.activation(out=gc[:, 0:n], in_=gc[:, 0:n], func=AF.Gelu_apprx_tanh))
                sy.op("dve", [Bgc, Bvc], [B_R1[0], B_R1[1], B_R1[2]], lambda e: e.tensor_tensor(out=fT[:, j, 0:n], in0=gc[:, 0:n], in1=vc[:, 0:n], op=ALU.mult))
        if ns:
            sy.dma("sp", oscf_o, ocf_s, [B_R2], [])

    def s9_ffn_down(n, out_cols, skip_cols, hook=None):
        fo = mT
        for i in range(KC):
            if i == 2 and hook is not None:
                hook()
            ps, Bps = PS[i % 2], B_PS[i % 2]
            w0, Bw0 = wload("wD", wD_d, 2 * i, 22 * 128)
            w1, Bw1 = wload("wD", wD_d, 2 * i + 1, 22 * 128)

            def mm_half(e, hh, w):
                for k in range(22 * hh, 22 * hh + 22):
                    o = (k % 22) * 128
                    ins = e.matmul(ps[:, 0:n], lhsT=w[:, o:o + 128], rhs=fT[:, k, 0:n], start=(k == 0), stop=(k == NF - 1))
                return ins
            sy.op("pe", [Bw0, B_R1[0], B_R1[1], B_R1[2]], [Bps], lambda e: mm_half(e, 0, w0))
            sy.op("pe", [Bw1, B_R1[0], B_R1[1], B_R1[2]], [Bps], lambda e: mm_half(e, 1, w1))
            sq, Bsq = rot(sqb, "sqb")
            sy.op("act", [Bps], [B_R2], lambda e: e.copy(out=fo[:, i, 0:n], in_=ps[:, 0:n]))
            sy.op("act", [Bps], [Bsq], lambda e: e.activation(out=sq[:, 0:n], in_=ps[:, 0:n], func=AF.Square))
            sy.op("pe", [Bsq, B_onesb], [B_PS[6]], lambda e: e.matmul(PS[6][:, 0:n], lhsT=onesb[:], rhs=sq[:, 0:n], start=(i == 0), stop=(i == KC - 1)))
        rstd_from(PS[6][:, 0:n], B_PS[6], n, rs1, B_rs1, 1.0 / D)
        sy.big = True
        for i in range(KC):
            sy.op("dve", [B_R2, B_rs1, B_cst], [B_R2], lambda e: e.scalar_tensor_tensor(
                out=fo[:, i, 0:n], in0=fo[:, i, 0:n], scalar=C("g_post2", i, i + 1), in1=rs1[:, 0:n], op0=ALU.mult, op1=ALU.mult))
            sy.op("dve", [B_R2, B_xT], [B_R2], lambda e: e.tensor_tensor(out=fo[:, i, 0:n], in0=fo[:, i, 0:n], in1=xT[:, i, 0:n], op=ALU.add))
        sy.big = False
        sy.dma("sp", yT_o[:, :, out_cols:out_cols + n - skip_cols], fo[:, :, skip_cols:n], [B_R2], [])

    for st in range(NPRE // TSP):
        n = TSP
        s1_load_norm(C_PRE + st * TSP, n)
        last = st == NPRE // TSP - 1
        ssd_dt_all([(t * 128, st * 2 + t) for t in range(n // 128)])
        s3_xbc(n, 0, 32 if st in (0, NPRE // TSP - 1) else 24)
        for t in range(n // 128):
            ssd_tile(t * 128, st * 2 + t, False, None, 32 * t)
        if last:
            halo_a_from_prefix(n)
    sts = [(C_LEAD, 128, NSTOK)] + [(C_MAIN + k * TSP, TSP, 0) for k in range(NMAIN // TSP)]
    prefetched = [False]
    for si, (c0, np_, ns) in enumerate(sts):
        n = np_ + ns
        cache_on[0] = (si == 0)
        if ns:
            sy.dma("sp", xT[:, :, np_:n], xT_d[:, :, C_SMP:C_SMP + ns], [], [B_xT])
            sy.dma("sp", xT[:, :, 0:np_], xT_d[:, :, c0:c0 + np_], [], [B_xT])
            sy.op("act", [B_xT], [B_sqT], lambda e: e.activation(out=sqT[:, :, 0:n], in_=xT[:, :, 0:n], func=AF.Square))

            def mm(e):
                for kc in range(KC):
                    ins = e.matmul(PS[7][:, 0:n], lhsT=onesb[:], rhs=sqT[:, kc, 0:n], start=(kc == 0), stop=(kc == KC - 1))
                return ins
            sy.op("pe", [B_sqT, B_onesb], [B_PS[7]], mm)
            rstd_from(PS[7][:, 0:n], B_PS[7], n, rs1, B_rs1, 1.0 / D)
            for kc in range(KC):
                sy.op("dve", [B_xT, B_rs1, B_cst], [B_hT], lambda e, kc=kc: e.scalar_tensor_tensor(
                    out=hT[:, kc, 0:n], in0=xT[:, kc, 0:n], scalar=C("g_pre1", kc, kc + 1), in1=rs1[:, 0:n],
                    op0=ALU.mult, op1=ALU.mult))
        elif prefetched[0]:
            s1_finish(n)
        else:
            s1_load_norm(c0, n)
        prefetched[0] = False
        nxt_hook = None
        if si + 1 < len(sts):
            nc0, nnp, _ = sts[si + 1]

            def nxt_hook(nc0=nc0, nnp=nnp):
                s1_prefetch(nc0, nnp)
                prefetched[0] = True
        ssd_dt_all([(t * 128, 8 + (0 if si == 0 else 1 + (si - 1) * 2 + t)) for t in range(np_ // 128)])
        steps = s234(np_, ns)
        for t in range(np_ // 128):
            vidx = 8 + (0 if si == 0 else 1 + (si - 1) * 2 + t)
            ssd_tile(t * 128, vidx, True, steps, 32 * t)
        while steps:
            steps.pop(0)()
        if ns:
            fence([B_R2, B_h0s, B_h0b, B_CEall, B_bmsk])
            ssd_samples(np_, 17)
            fence([B_h0s, B_h0b, B_CEall, B_bmsk, B_R2])
        s6_merge(n)
        s7_out_res(n)
        if si == 0:
            s8_ffn_up(np_, ns, True)
            s9_ffn_down(n, NMAIN, np_, nxt_hook)
        else:
            s8_ffn_up(np_, ns, True)
            s9_ffn_down(n, (si - 1) * TSP, 0, nxt_hook)
    sy.dma("sp", oca_o, haloA[:], [B_hA], [])
    sy.dma("sp", ocb_o, haloB[:], [B_hB], [])
    sy.dma("sp", ocf_o, haloF[:], [B_hF], [])
    sy.dma("sp", ossm_o, hst[:], [B_hst], [])
    sy.finish("sp", [B_hA, B_hB, B_hF, B_hst, B_R2, B_h0s, B_h0s2, B_R1[0], B_ocb])
    return nc


def _pk(v):
    v = np.asarray(v, np.float32)
    return np.ascontiguousarray(v.reshape(-1, 128).T)


def _wtile(w, cols):
    ws = w[:, cols]
    K = ws.shape[0]
    return np.ascontiguousarray(ws.reshape(K // 128, 128, len(cols)).transpose(1, 0, 2).reshape(128, -1))


def _const_masks():
    k = np.arange(128)[:, None]
    q = np.arange(128)[None, :]
    m = {}
    m["ones"] = np.ones((128, 128), np.float32)
    m["ident"] = np.eye(128, dtype=np.float32)
    m["tri_p"] = (k <= q).astype(np.float32)
    same = (k // 4 == q // 4)
    ts = np.zeros((128, 64), np.float32)
    ts[:64] = ((k <= q) & same)[:64, :64]
    m["tri_s"] = ts
    bs = np.zeros((128, 64), np.float32)
    bs[:64] = same[:64, :64]
    m["blk_s"] = bs
    mp = np.where(q < k, -30000.0, 0.0).astype(np.float32)
    m["mneg_p"] = np.tile(mp, (1, 4))
    ms = np.zeros((128, 64), np.float32)
    ms[:64] = np.where(((k <= q) & same)[:64, :64], 0.0, -30000.0)
    m["mneg_s"] = np.tile(ms, (1, 4))
    sel = np.zeros((128, NSEQ), np.float32)
    sel[np.arange(64), np.arange(64) // 4] = 1.0
    m["sel"] = sel
    return m


_NC_CACHE = {}


def kernel(x_prompt, x_sample, state_conv_a, state_conv_b, state_ssm, state_conv_ffn, meta_tokens,
           g_pre1, g_post1, w_in, b_gate, w_dw_a, b_dw_a, g_ln_a, b_ln_a, w_a_out, b_a_out,
           w_dw_b, b_dw_b, dt_bias, a_log, d_skip, g_norm_b, w_b_out, w_o,
           g_pre2, g_post2, w_up, w_dw_f, b_dw_f, w_down):
    f = lambda a: np.asarray(a, np.float32)
    x_prompt, x_sample, meta_tokens = f(x_prompt), f(x_sample), f(meta_tokens)
    w_in0, w_up0, w_down0 = f(w_in)[0], f(w_up)[0], f(w_down)[0]
    w_a0, w_b0, w_o0 = f(w_a_out)[0], f(w_b_out)[0], f(w_o)[0]
    oA, oZ, oX, oDT, oG = 0, 2 * DA, 2 * DA + DB, 2 * DA + DB + DXBC, 2 * DA + DB + DXBC + H
    ar = np.arange
    wA = np.stack([_wtile(w_in0, np.concatenate([oA + j * 128 + ar(128), oA + DA + j * 128 + ar(128)])) for j in range(NJA)])
    wX = np.stack([_wtile(w_in0, oX + c * 256 + ar(256)) for c in range(16)])
    wDT = _wtile(w_in0, oDT + ar(32))
    wZ = np.stack([_wtile(w_in0, oZ + c * 256 + ar(256)) for c in range(8)])
    w6a = np.stack([np.concatenate([_wtile(w_a0, j * 128 + ar(128)), _wtile(w_b0, j * 128 + ar(128))], axis=1) for j in range(KC)])
    w6b = np.stack([np.concatenate([_wtile(w_in0, oG + j * 128 + ar(128)), _wtile(w_in0, oG + D + j * 128 + ar(128))], axis=1) for j in range(KC)])
    wO = np.stack([_wtile(w_o0, c * 256 + ar(256)) for c in range(8)])
    wU = np.stack([_wtile(w_up0, np.concatenate([j * 128 + ar(128), DFF + j * 128 + ar(128)])) for j in range(NF)])
    wD = np.stack([_wtile(w_down0[hh * 2816:(hh + 1) * 2816], i * 128 + ar(128)) for i in range(KC) for hh in range(2)])
    cst = np.zeros((128, NCST), np.float32)

    def put(name, arr):
        o, n = CST[name]
        arr = np.asarray(arr, np.float32)
        assert arr.shape == (128, n), (name, arr.shape, n)
        cst[:, o:o + n] = arr
    put("g_pre1", _pk(f(g_pre1)[0])); put("g_post1", _pk(f(g_post1)[0]))
    put("g_pre2", _pk(f(g_pre2)[0])); put("g_post2", _pk(f(g_post2)[0]))
    put("g_norm_b", _pk(f(g_norm_b)[0])); put("b_a_out", _pk(f(b_a_out)[0]))
    put("dsk", _pk(np.repeat(f(d_skip)[0], HP)))
    put("b_gate", _pk(f(b_gate)[0]))
    put("w_dw_a", f(w_dw_a)[0].T.reshape(NJA, 128, CA).transpose(1, 0, 2).reshape(128, -1))
    put("b_dw_a", _pk(f(b_dw_a)[0])); put("g_ln_a", _pk(f(g_ln_a)[0])); put("b_ln_a", _pk(f(b_ln_a)[0]))
    put("w_dw_b", f(w_dw_b)[0].T.reshape(32, 128, CB).transpose(1, 0, 2).reshape(128, -1))
    put("b_dw_b", _pk(f(b_dw_b)[0]))
    put("w_dw_f", f(w_dw_f)[0].T.reshape(88, 128, CF).transpose(1, 0, 2).reshape(128, -1))
    put("b_dw_f", _pk(f(b_dw_f)[0]))
    put("dt_bias", np.tile(f(dt_bias)[0][None, :], (128, 1)))
    put("a_log", np.tile(f(a_log)[0][None, :], (128, 1)))
    for k_, v_ in _const_masks().items():
        put(k_, v_)
    in_maps = []
    chunk0 = np.zeros((128, D), np.float32)
    chunk0[128 - NMETA:] = meta_tokens
    v0 = np.zeros(128, np.float32)
    v0[128 - NMETA:] = 1.0
    for c in range(8):
        b, half = c // 2, c % 2
        xs_tok = np.zeros((NCOL, D), np.float32)
        valid = np.zeros((128, 18), np.float32)
        if half == 1:
            xs_tok[0:128] = chunk0
            xs_tok[128:1024] = x_prompt[b, 0:896]
            xs_tok[C_LEAD:C_LEAD + 128] = x_prompt[b, 896:1024]
            valid[:, 0] = v0
            valid[:, 1:8] = 1.0
            valid[:, 8] = 1.0
        else:
            xs_tok[C_LEAD:C_LEAD + 128] = chunk0
            valid[:, 8] = v0
        xs_tok[C_MAIN:C_MAIN + NMAIN] = x_prompt[b, half * 1024:(half + 1) * 1024]
        valid[:, 9:17] = 1.0
        xs_tok[C_SMP:] = x_sample[c * NSEQ:(c + 1) * NSEQ].reshape(NSTOK, D)
        valid[:, 17] = 1.0
        cc = cst.copy()
        o, n = CST["valid"]
        cc[:, o:o + n] = valid
        xT = np.ascontiguousarray(xs_tok.T.reshape(KC, 128, NCOL).transpose(1, 0, 2))
        sl = slice(c * NSEQ, (c + 1) * NSEQ)
        sca = np.ascontiguousarray(f(state_conv_a)[0, sl].transpose(2, 0, 1).reshape(NJA, 128, NSEQ, CA - 1).transpose(1, 0, 2, 3))
        scb = np.ascontiguousarray(f(state_conv_b)[0, sl].transpose(2, 0, 1).reshape(32, 128, NSEQ, CB - 1).transpose(1, 0, 2, 3))
        scf = np.ascontiguousarray(f(state_conv_ffn)[0, sl].transpose(2, 0, 1).reshape(88, 128, NSEQ, CF - 1).transpose(1, 0, 2, 3))
        ssm = np.ascontiguousarray(f(state_ssm)[0, sl].reshape(NSEQ, DB, NS).transpose(0, 2, 1))
        in_maps.append({"xT": xT, "cst": cc, "wA": wA, "wX": wX, "wDT": wDT, "wZ": wZ, "w6a": w6a, "w6b": w6b,
                        "wO": wO, "wU": wU, "wD": wD, "sca": sca, "scb": scb, "scf": scf, "ssm": ssm})
    if "nc" not in _NC_CACHE:
        _NC_CACHE["nc"] = build_program()
    res = run_bass_kernel_spmd(_NC_CACHE["nc"], in_maps, core_ids=list(range(8)))
    R = res.results
    y_prompt = np.zeros((4, 2048, D), np.float32)
    y_sample = np.zeros((128, 4, D), np.float32)
    ca_p = np.zeros((1, 4, CA - 1, DA), np.float32)
    cb_p = np.zeros((1, 4, CB - 1, DXBC), np.float32)
    h_p = np.zeros((1, 4, H, HP, NS), np.float32)
    cf_p = np.zeros((1, 4, CF - 1, 2 * DFF), np.float32)
    ca_s = np.zeros((1, 128, CA - 1, DA), np.float32)
    cb_s = np.zeros((1, 128, CB - 1, DXBC), np.float32)
    h_s = np.zeros((1, 128, H, HP, NS), np.float32)
    cf_s = np.zeros((1, 128, CF - 1, 2 * DFF), np.float32)

    def unT(a):
        return a.transpose(2, 1, 0).reshape(a.shape[2], -1)
    for c in range(8):
        b, half = c // 2, c % 2
        r = R[c]
        yT = r["yT"]
        y_prompt[b, half * 1024:(half + 1) * 1024] = unT(yT[:, :, 0:NMAIN])
        y_sample[c * NSEQ:(c + 1) * NSEQ] = unT(yT[:, :, NMAIN:]).reshape(NSEQ, 4, D)
        if half == 1:
            ca_p[0, b] = unT(r["o_ca"])
            cb_p[0, b] = unT(r["o_cb"])
            cf_p[0, b] = unT(r["o_cf"])
            h_p[0, b] = r["o_ssm"].T.reshape(H, HP, NS)
        sl = slice(c * NSEQ, (c + 1) * NSEQ)
        ca_s[0, sl] = r["o_sca"].transpose(2, 3, 1, 0).reshape(NSEQ, CA - 1, DA)
        cb_s[0, sl] = r["o_scb"].transpose(2, 3, 1, 0).reshape(NSEQ, CB - 1, DXBC)
        cf_s[0, sl] = r["o_scf"].transpose(2, 3, 1, 0).reshape(NSEQ, CF - 1, 2 * DFF)
        h_s[0, sl] = r["o_sssm"].transpose(0, 2, 1).reshape(NSEQ, H, HP, NS)
    return (y_prompt, y_sample, ca_p, cb_p, h_p, cf_p, ca_s, cb_s, h_s, cf_s)
```
